# Optimizing a Trainium2 kernel written in Bass

```python
import math
import jax
import jax.numpy as jnp
from jax import lax
import numpy as np


D_MODEL = 2048
BATCH = 2
SEQ = 8192
DEPTH = 4

GRID_W = 64
CTX_LEN = 256
RMS_EPS = 1e-6
N_MOD = 6
HEAD_DIM = 128
A_WIDTH = D_MODEL // 2
A_Q_HEADS = A_WIDTH // HEAD_DIM
A_KV_HEADS = max(1, A_Q_HEADS // 4)
A_GROUP = A_Q_HEADS // A_KV_HEADS
WINDOW = 128
BLOCK = 128
ROPE_BASE = 10000.0
NEG_INF = -1e30
B_WIDTH = D_MODEL - A_WIDTH
HY_EMB = 33
HY_BANDS = (HY_EMB - 1) // 2
HY_HIDDEN = 64
HY_FAST_DECAY = 0.3
HY_SLOW_DECAY = 1.5
HY_TARGET = 1e-2
S5_GROUP = 16
S5_GROUPS = D_MODEL // S5_GROUP
S5_STATE = 64
S5_DT_MIN = 1e-3
S5_DT_MAX = 1e-1
D_FF = ((8 * D_MODEL // 3 + 127) // 128) * 128
N_EVEN = (DEPTH + 1) // 2
N_ODD = DEPTH // 2
Q_COLS = A_Q_HEADS * HEAD_DIM
KV_COLS = A_KV_HEADS * HEAD_DIM
HY_COLS = 3 * B_WIDTH
IN_COLS = Q_COLS + 2 * KV_COLS + HY_COLS

kernel_name = 'hybrid_swa_hyena_s5_prefix_dit'


def rmsnorm(x, g):
    xf = x.astype(jnp.float32)
    y = xf * lax.rsqrt(jnp.mean(xf * xf, axis=-1, keepdims=True) + RMS_EPS)
    return (y * g.astype(jnp.float32)).astype(x.dtype)


def modulate(x, g, shift, scale):
    return rmsnorm(x, g) * (1.0 + scale) + shift


def dwconv3(u, w, b):
    up = jnp.pad(u, ((0, 0), (1, 1), (0, 0)))
    return up[:, :-2] * w[0] + up[:, 1:-1] * w[1] + up[:, 2:] * w[2] + b


def axial_rope_tables(n):
    rows = n // GRID_W
    row = jnp.broadcast_to(jnp.arange(rows, dtype=jnp.float32)[:, None], (rows, GRID_W)).reshape(-1)
    col = jnp.broadcast_to(jnp.arange(GRID_W, dtype=jnp.float32)[None, :], (rows, GRID_W)).reshape(-1)
    half = HEAD_DIM // 2
    inv_freq = ROPE_BASE ** (-jnp.arange(0, half, 2, dtype=jnp.float32) / half)
    ang = jnp.concatenate([row[:, None] * inv_freq, col[:, None] * inv_freq], axis=-1)
    return jnp.cos(ang), jnp.sin(ang)


def apply_axial_rope(x, cos, sin):
    qd = HEAD_DIM // 4
    c = cos[None, :, None, :]
    s = sin[None, :, None, :]
    cr, cc = c[..., :qd], c[..., qd:]
    sr, sc = s[..., :qd], s[..., qd:]
    x1, x2, x3, x4 = jnp.split(x.astype(jnp.float32), 4, axis=-1)
    out = jnp.concatenate([x1 * cr - x2 * sr, x2 * cr + x1 * sr, x3 * cc - x4 * sc, x4 * cc + x3 * sc], axis=-1)
    return out.astype(x.dtype)


def window_attention(q, k, v, kc, vc, sink):
    bsz, n = q.shape[0], q.shape[1]
    nc = kc.shape[1]
    nb = n // BLOCK
    scale = HEAD_DIM ** -0.5
    qb = q.reshape(bsz, nb, BLOCK, A_KV_HEADS, A_GROUP, HEAD_DIM)
    pad = ((0, 0), (BLOCK, BLOCK), (0, 0), (0, 0))
    kp, vp = jnp.pad(k, pad), jnp.pad(v, pad)

    def band(t):
        return jnp.concatenate([t[:, i * BLOCK:i * BLOCK + n].reshape(bsz, nb, BLOCK, A_KV_HEADS, HEAD_DIM) for i in range(3)], axis=2)

    kb, vb = band(kp), band(vp)
    qi = jnp.arange(BLOCK)[:, None]
    kj = jnp.arange(3 * BLOCK)[None, :]
    kpos = (jnp.arange(nb) * BLOCK - BLOCK)[:, None, None] + kj[None]
    valid = (jnp.abs(kj - BLOCK - qi) <= WINDOW)[None] & (kpos >= 0) & (kpos < n)
    s_band = jnp.einsum('bnqhgd,bnkhd->bnhgqk', qb, kb).astype(jnp.float32) * scale
    s_band = jnp.where(valid[None, :, None, None], s_band, NEG_INF)
    s_ctx = jnp.einsum('bnqhgd,bchd->bnhgqc', qb, kc).astype(jnp.float32) * scale
    s_sink = jnp.broadcast_to(sink.astype(jnp.float32).reshape(A_KV_HEADS, A_GROUP)[None, None, :, :, None, None], s_band.shape[:-1] + (1,))
    p = jax.nn.softmax(jnp.concatenate([s_band, s_ctx, s_sink], axis=-1), axis=-1).astype(v.dtype)
    nk = 3 * BLOCK
    o = jnp.einsum('bnhgqk,bnkhd->bnqhgd', p[..., :nk], vb) + jnp.einsum('bnhgqc,bchd->bnqhgd', p[..., nk:nk + nc], vc)
    return o.reshape(bsz, n, Q_COLS)


def context_attention(qc, kc, vc, sink):
    bsz, nc = qc.shape[0], qc.shape[1]
    scale = HEAD_DIM ** -0.5
    qg = qc.reshape(bsz, nc, A_KV_HEADS, A_GROUP, HEAD_DIM)
    s = jnp.einsum('bqhgd,bkhd->bhgqk', qg, kc).astype(jnp.float32) * scale
    s_sink = jnp.broadcast_to(sink.astype(jnp.float32).reshape(A_KV_HEADS, A_GROUP)[None, :, :, None, None], s.shape[:-1] + (1,))
    p = jax.nn.softmax(jnp.concatenate([s, s_sink], axis=-1), axis=-1).astype(vc.dtype)
    o = jnp.einsum('bhgqk,bkhd->bqhgd', p[..., :nc], vc)
    return o.reshape(bsz, nc, Q_COLS)


def hyena_filter(n, w1, b1, w2, b2, w3, freq):
    f32 = jnp.float32
    t = jnp.linspace(0.0, 1.0, n, dtype=f32)[:, None]
    w = (2.0 * math.pi / n) * jnp.arange(n, dtype=f32)
    bands = jnp.linspace(1e-4, HY_BANDS - 1, HY_BANDS, dtype=f32)
    ang = w[:, None] * bands[None, :]
    z = jnp.concatenate([t, jnp.cos(ang), -jnp.sin(ang)], axis=-1)
    fr = freq.astype(f32)
    hid = jnp.sin(fr * (z @ w1.astype(f32) + b1.astype(f32)))
    hid = jnp.sin(fr * (hid @ w2.astype(f32) + b2.astype(f32)))
    filt = (hid @ w3.astype(f32)).reshape(n, 2, B_WIDTH)
    max_decay = math.log(HY_TARGET) / HY_FAST_DECAY
    min_decay = math.log(HY_TARGET) / HY_SLOW_DECAY
    deltas = jnp.linspace(min_decay, max_decay, B_WIDTH, dtype=f32)
    decay = jnp.exp(-t * jnp.abs(deltas)[None, :])
    filt = filt * decay[:, None, :]
    circ = jnp.concatenate([filt[:, 0], jnp.zeros((1, B_WIDTH), f32), filt[:0:-1, 1]], axis=0)
    return circ / jnp.sum(jnp.abs(circ), axis=0, keepdims=True)


def long_conv(u, circ):
    n = u.shape[1]
    cf = jnp.fft.rfft(circ, n=2 * n, axis=0)
    uf = jnp.fft.rfft(u.astype(jnp.float32), n=2 * n, axis=1)
    return jnp.fft.irfft(uf * cf[None], n=2 * n, axis=1)[:, :n].astype(u.dtype)


def hyena(z, conv_w, conv_b, circ, bias):
    p = dwconv3(z, conv_w, conv_b)
    x0, x1, v = jnp.split(p, 3, axis=-1)
    v = v * x1
    v = long_conv(v, circ) + v * bias
    return v * x0


def conv_ffn(h, w_up, conv_w, conv_b, w_down):
    u = dwconv3(h @ w_up, conv_w, conv_b)
    gate, val = jnp.split(u, 2, axis=-1)
    return (jax.nn.silu(gate) * val) @ w_down


def attn_hyena_mixer(h, hc, w_in, w_out, sink, hy_conv_w, hy_conv_b, f_w1, f_b1, f_w2, f_b2, f_w3, f_freq, hy_bias, ctx_out):
    bsz, n = h.shape[0], h.shape[1]
    nc = hc.shape[1]
    cuts = [Q_COLS, Q_COLS + KV_COLS, Q_COLS + 2 * KV_COLS]
    q, k, v, zb = jnp.split(h @ w_in, cuts, axis=-1)
    cos, sin = axial_rope_tables(n)
    q = apply_axial_rope(q.reshape(bsz, n, A_Q_HEADS, HEAD_DIM), cos, sin)
    k = apply_axial_rope(k.reshape(bsz, n, A_KV_HEADS, HEAD_DIM), cos, sin)
    v = v.reshape(bsz, n, A_KV_HEADS, HEAD_DIM)
    if ctx_out:
        qc, kc, vc, zbc = jnp.split(hc @ w_in, cuts, axis=-1)
    else:
        kc, vc = jnp.split(hc @ w_in[:, Q_COLS:Q_COLS + 2 * KV_COLS], 2, axis=-1)
    kc = kc.reshape(bsz, nc, A_KV_HEADS, HEAD_DIM)
    vc = vc.reshape(bsz, nc, A_KV_HEADS, HEAD_DIM)
    y_a = window_attention(q, k, v, kc, vc, sink)
    y_b = hyena(zb, hy_conv_w, hy_conv_b, hyena_filter(n, f_w1, f_b1, f_w2, f_b2, f_w3, f_freq), hy_bias)
    y = jnp.concatenate([y_a, y_b], axis=-1) @ w_out
    if not ctx_out:
        return y, None
    yc_a = context_attention(qc.reshape(bsz, nc, A_Q_HEADS, HEAD_DIM), kc, vc, sink)
    yc_b = hyena(zbc, hy_conv_w, hy_conv_b, hyena_filter(nc, f_w1, f_b1, f_w2, f_b2, f_w3, f_freq), hy_bias)
    yc = jnp.concatenate([yc_a, yc_b], axis=-1) @ w_out
    return y, yc


def _lin_combine(e1, e2):
    a1, b1 = e1
    a2, b2 = e2
    return a1 * a2, a2 * b1 + b2


def s5_scan(u, lam_bar, b_bar, s0, reverse):
    bu = jnp.einsum('bngj,gpj->bngp', u.astype(jnp.complex64), b_bar)
    if s0 is not None:
        idx = -1 if reverse else 0
        bu = bu.at[:, idx].add(lam_bar * s0)
    a = jnp.broadcast_to(lam_bar, bu.shape)
    _, s = lax.associative_scan(_lin_combine, (a, bu), reverse=reverse, axis=1)
    return s


def s5_mixer(h, hc, lam_re, lam_im, log_step, b_re, b_im, c_re, c_im, d, w_glu, b_glu, ctx_out):
    f32 = jnp.float32
    lam = lax.complex(lam_re.astype(f32), lam_im.astype(f32))
    lam_bar = jnp.exp(lam * jnp.exp(log_step.astype(f32))[..., None])
    b_bar = ((lam_bar - 1.0) / lam)[..., None] * lax.complex(b_re.astype(f32), b_im.astype(f32))
    c_mat = lax.complex(c_re.astype(f32), c_im.astype(f32))
    d_g = d.astype(f32).reshape(S5_GROUPS, S5_GROUP)

    def groups(t):
        return t.astype(f32).reshape(t.shape[0], t.shape[1], S5_GROUPS, S5_GROUP)

    def readout(s, dirn):
        return jnp.real(jnp.einsum('bngp,gjp->bngj', s, c_mat[dirn]))

    def glu(t, ref):
        g = jax.nn.gelu(t.reshape(ref.shape).astype(ref.dtype))
        a, gt = jnp.split(g @ w_glu + b_glu, 2, axis=-1)
        return a * jax.nn.sigmoid(gt)

    u, uc = groups(h), groups(hc)
    y = d_g * u
    yc = d_g * uc if ctx_out else None
    for dirn, rev in ((0, False), (1, True)):
        s_ctx = s5_scan(uc, lam_bar[dirn], b_bar[dirn], None, rev)
        s_end = s_ctx[:, 0] if rev else s_ctx[:, -1]
        y = y + readout(s5_scan(u, lam_bar[dirn], b_bar[dirn], s_end, rev), dirn)
        if ctx_out:
            yc = yc + readout(s_ctx, dirn)
    return glu(y, h), (glu(yc, hc) if ctx_out else None)


def setup_inputs(seed: int = 0) -> dict:
    key = jax.random.key(seed)
    ks = iter(jax.random.split(key, 40))
    f32 = jnp.float32

    def nrm(shape, std):
        return std * jax.random.normal(next(ks), shape, f32)

    D = D_MODEL
    lam_im_base = math.pi * jnp.arange(S5_STATE, dtype=f32)
    return {
        'x': nrm((BATCH, SEQ, D), 1.0),
        'c': nrm((BATCH, D), 1.0),
        'ctx': nrm((BATCH, CTX_LEN, D), 1.0),
        'c_ctx': nrm((D,), 1.0),
        'w_mod': nrm((DEPTH, D, N_MOD * D), 0.5 * D ** -0.5),
        'b_mod': nrm((DEPTH, N_MOD * D), 0.02),
        'norm_g': 1.0 + nrm((DEPTH, 4, D), 0.05),
        'ffn_w_up': nrm((DEPTH, D, 2 * D_FF), D ** -0.5),
        'ffn_conv_w': nrm((DEPTH, 3, 2 * D_FF), 3 ** -0.5),
        'ffn_conv_b': nrm((DEPTH, 2 * D_FF), 0.02),
        'ffn_w_down': nrm((DEPTH, D_FF, D), D_FF ** -0.5),
        'ab_w_in': nrm((N_EVEN, D, IN_COLS), D ** -0.5),
        'ab_w_out': nrm((N_EVEN, Q_COLS + B_WIDTH, D), (Q_COLS + B_WIDTH) ** -0.5),
        'attn_sink': nrm((N_EVEN, A_Q_HEADS), 0.5),
        'hy_conv_w': nrm((N_EVEN, 3, HY_COLS), 3 ** -0.5),
        'hy_conv_b': nrm((N_EVEN, HY_COLS), 0.02),
        'hy_f_w1': nrm((N_EVEN, HY_EMB, HY_HIDDEN), 1.0),
        'hy_f_b1': nrm((N_EVEN, HY_HIDDEN), 0.5),
        'hy_f_w2': nrm((N_EVEN, HY_HIDDEN, HY_HIDDEN), HY_HIDDEN ** -0.5),
        'hy_f_b2': nrm((N_EVEN, HY_HIDDEN), 0.5),
        'hy_f_w3': nrm((N_EVEN, HY_HIDDEN, 2 * B_WIDTH), HY_HIDDEN ** -0.5),
        'hy_f_freq': 1.0 + nrm((N_EVEN, HY_HIDDEN), 0.1),
        'hy_bias': nrm((N_EVEN, B_WIDTH), 1.0),
        's5_lam_re': -0.5 + nrm((N_ODD, 2, S5_GROUPS, S5_STATE), 0.01),
        's5_lam_im': lam_im_base + nrm((N_ODD, 2, S5_GROUPS, S5_STATE), 0.01),
        's5_log_step': jax.random.uniform(next(ks), (N_ODD, 2, S5_GROUPS), f32, math.log(S5_DT_MIN), math.log(S5_DT_MAX)),
        's5_b_re': nrm((N_ODD, 2, S5_GROUPS, S5_STATE, S5_GROUP), (2 * S5_GROUP) ** -0.5),
        's5_b_im': nrm((N_ODD, 2, S5_GROUPS, S5_STATE, S5_GROUP), (2 * S5_GROUP) ** -0.5),
        's5_c_re': nrm((N_ODD, 2, S5_GROUPS, S5_GROUP, S5_STATE), 0.5 ** 0.5),
        's5_c_im': nrm((N_ODD, 2, S5_GROUPS, S5_GROUP, S5_STATE), 0.5 ** 0.5),
        's5_d': nrm((N_ODD, D), 1.0),
        's5_w_glu': nrm((N_ODD, D, 2 * D), D ** -0.5),
        's5_b_glu': nrm((N_ODD, 2 * D), 0.02),
    }


def reference(x, c, ctx, c_ctx, w_mod, b_mod, norm_g, ffn_w_up, ffn_conv_w, ffn_conv_b, ffn_w_down, ab_w_in, ab_w_out, attn_sink, hy_conv_w, hy_conv_b, hy_f_w1, hy_f_b1, hy_f_w2, hy_f_b2, hy_f_w3, hy_f_freq, hy_bias, s5_lam_re, s5_lam_im, s5_log_step, s5_b_re, s5_b_im, s5_c_re, s5_c_im, s5_d, s5_w_glu, s5_b_glu):
    xc = ctx
    s_lat = jax.nn.silu(c)
    s_ctx = jax.nn.silu(c_ctx)
    for layer in range(DEPTH):
        last = layer == DEPTH - 1
        m = jnp.split((s_lat @ w_mod[layer] + b_mod[layer])[:, None, :], N_MOD, axis=-1)
        mc = jnp.split(s_ctx @ w_mod[layer] + b_mod[layer], N_MOD, axis=-1)
        hm = modulate(x, norm_g[layer, 0], m[0], m[1])
        hmc = modulate(xc, norm_g[layer, 0], mc[0], mc[1])
        j = layer // 2
        if layer % 2 == 0:
            y, yc = attn_hyena_mixer(hm, hmc, ab_w_in[j], ab_w_out[j], attn_sink[j], hy_conv_w[j], hy_conv_b[j], hy_f_w1[j], hy_f_b1[j], hy_f_w2[j], hy_f_b2[j], hy_f_w3[j], hy_f_freq[j], hy_bias[j], not last)
        else:
            y, yc = s5_mixer(hm, hmc, s5_lam_re[j], s5_lam_im[j], s5_log_step[j], s5_b_re[j], s5_b_im[j], s5_c_re[j], s5_c_im[j], s5_d[j], s5_w_glu[j], s5_b_glu[j], not last)
        x = x + m[2] * rmsnorm(y, norm_g[layer, 1])
        f = conv_ffn(modulate(x, norm_g[layer, 2], m[3], m[4]), ffn_w_up[layer], ffn_conv_w[layer], ffn_conv_b[layer], ffn_w_down[layer])
        x = x + m[5] * rmsnorm(f, norm_g[layer, 3])
        if not last:
            xc = xc + mc[2] * rmsnorm(yc, norm_g[layer, 1])
            fc = conv_ffn(modulate(xc, norm_g[layer, 2], mc[3], mc[4]), ffn_w_up[layer], ffn_conv_w[layer], ffn_conv_b[layer], ffn_w_down[layer])
            xc = xc + mc[5] * rmsnorm(fc, norm_g[layer, 3])
    return x
```

```python
import math
import numpy as np
from contextlib import ExitStack
import concourse.bass as bass
import concourse.mybir as mybir
from concourse.bass_utils import run_bass_kernel_spmd

F32 = mybir.dt.float32
I32 = mybir.dt.int32
AF = mybir.ActivationFunctionType
ALU = mybir.AluOpType
AX = mybir.AxisListType

DMA_R = 14
ENGS = ("sync", "scalar", "vector", "gpsimd", "tensor")

D = 2048
KC = 16
SEQ = 8192
NCTX = 256
NB = 2
NCORE = 8
TO = 256
WW = TO + 2
NTL = 8
NT = 9
TCORE = 2112
DFF = 5504
GRID_W = 64
EPS = 1e-6
TWO_PI = 2.0 * math.pi


class Prog:
    def __init__(self):
        self.nc = bass.Bass("TRN2", target_bir_lowering=False)
        self.es = ExitStack()
        self.ops = []
        self.outkeys = []
        self.n = 0
        self.rots = {}

    def din(self, name, shape, dt=F32):
        return self.nc.dram_tensor(name, list(shape), dt, kind="ExternalInput").ap()

    def dout(self, name, shape, dt=F32):
        return self.nc.dram_tensor(name, list(shape), dt, kind="ExternalOutput").ap()

    def dscr(self, name, shape, dt=F32):
        return self.nc.dram_tensor(name, list(shape), dt, kind="Internal")

    def sb(self, name, shape, dt=F32):
        return self.es.enter_context(self.nc.sbuf_tensor(name, list(shape), dt))

    def ps(self, name, shape, dt=F32):
        return self.es.enter_context(self.nc.psum_tensor(name, list(shape), dt))

    def rot(self, name, n, shape, dt=F32, psum=False):
        if name not in self.rots:
            mk = self.ps if psum else self.sb
            self.rots[name] = [[mk("%s_%d" % (name, i), shape, dt) for i in range(n)], 0]
        lst = self.rots[name]
        i = lst[1] % len(lst[0])
        lst[1] += 1
        return lst[0][i], (name, i)

    def op(self, eng, fn, r=(), w=(), dma=False):
        self.ops.append((eng, fn, tuple(r), tuple(w), dma))

    def dma(self, out, in_, r=(), w=(), q="sync", **kw):
        self.op(q, lambda e: e.dma_start(out=out, in_=in_, **kw), r, w, dma=True)

    def store(self, out, in_, r=(), q="sync", **kw):
        self.n += 1
        k = ("__out", self.n)
        self.outkeys.append(k)
        self.dma(out, in_, r=r, w=(k,), q=q, **kw)

    def mm(self, out, lhsT, rhs, start, stop, r=(), w=()):
        self.op("tensor", lambda e: e.matmul(out, lhsT, rhs, start=start, stop=stop), r, w)

    def tr(self, out, in_, ident, r=(), w=()):
        self.op("tensor", lambda e: e.transpose(out, in_, ident), r, w)

    def act(self, out, in_, func, r=(), w=(), **kw):
        self.op("scalar", lambda e: e.activation(out=out, in_=in_, func=func, **kw), r, w)

    def v(self, name, r=(), w=(), eng="vector", **kw):
        self.op(eng, lambda e: getattr(e, name)(**kw), r, w)

    def build(self):
        nc = self.nc
        ops = self.ops
        ops.append(("sync", None, tuple(self.outkeys), (), False))
        N = len(ops)
        last_w = {}
        rd_c = {}
        rd_d = {}
        deps = [None] * N
        for i, (eng, fn, r, w, dma) in enumerate(ops):
            d = set()
            for k in r:
                j = last_w.get(k)
                if j is not None:
                    d.add(j)
            for k in w:
                j = last_w.get(k)
                if j is not None:
                    d.add(j)
                for j in rd_c.get(k, {}).values():
                    d.add(j)
                for j in rd_d.get(k, ()):
                    d.add(j)
            for k in r:
                if dma:
                    rd_d.setdefault(k, []).append(i)
                else:
                    rd_c.setdefault(k, {})[eng] = i
            for k in w:
                last_w[k] = i
                rd_c[k] = {}
                rd_d[k] = []
            d.discard(i)
            deps[i] = d
        waited_c = {e: {} for e in ENGS}
        waited_d = {e: set() for e in ENGS}
        final = [None] * N
        signaled = set()
        for i, (eng, fn, r, w, dma) in enumerate(ops):
            best = {}
            dl = []
            for j in deps[i]:
                je, _, _, _, jd = ops[j]
                if jd:
                    if j not in waited_d[eng]:
                        dl.append(j)
                        waited_d[eng].add(j)
                else:
                    if je == eng and eng == "tensor" and not dma:
                        continue
                    if j > best.get(je, -1):
                        best[je] = j
            cl = []
            for je, j in best.items():
                if waited_c[eng].get(je, -1) >= j:
                    continue
                waited_c[eng][je] = j
                cl.append(j)
                signaled.add(j)
            final[i] = (cl, dl)
        sem = {e: self.es.enter_context(nc.semaphore("s_" + e)) for e in ENGS}
        dq = {}
        for q in ("sync", "scalar", "gpsimd"):
            dq[q] = [self.es.enter_context(nc.semaphore("d_%s_%d" % (q, t))) for t in range(DMA_R)]
        cnt = {e: 0 for e in ENGS}
        dcnt = {q: 0 for q in dq}
        sig = [None] * N
        pre = [None] * N
        for i, (eng, fn, r, w, dma) in enumerate(ops):
            if dma:
                n = dcnt[eng]
                dcnt[eng] += 1
                s = dq[eng][n % DMA_R]
                sig[i] = (s, 16 * (n // DMA_R + 1))
                if n >= DMA_R:
                    pre[i] = (s, 16 * (n // DMA_R))
            elif i in signaled:
                cnt[eng] += 1
                sig[i] = (sem[eng], cnt[eng])
        per = {e: [] for e in ENGS}
        for i, o in enumerate(ops):
            per[o[0]].append(i)
        self.stats = {e: len(per[e]) for e in per}

        def emit(e, name):
            for i in per[name]:
                eng, fn, r, w, dma = ops[i]
                cl, dl = final[i]
                if pre[i] is not None:
                    e.wait_ge(pre[i][0], pre[i][1])
                for j in cl + dl:
                    e.wait_ge(sig[j][0], sig[j][1])
                if fn is None:
                    continue
                ins = fn(e)
                if dma:
                    ins.then_inc(sig[i][0], 16)
                elif sig[i] is not None:
                    ins.then_inc(sig[i][0], 1)

        with nc.Block() as block:
            @block.sync
            def _(e):
                emit(e, "sync")

            @block.scalar
            def _(e):
                emit(e, "scalar")

            @block.vector
            def _(e):
                emit(e, "vector")

            @block.gpsimd
            def _(e):
                emit(e, "gpsimd")

            @block.tensor
            def _(e):
                emit(e, "tensor")
        self.es.close()
        return nc


_N_LAUNCH = [0]


def run_prog(P, in_maps):
    nc = P.build()
    res = run_bass_kernel_spmd(nc, in_maps, core_ids=list(range(NCORE)))
    _N_LAUNCH[0] += 1
    return res.results


class TS:
    def __init__(self, P, wb_elems):
        self.P = P
        self.ones = P.sb("ones", [128, 128])
        P.v("memset", ap=self.ones[:], constant=1.0, w=["ones"])
        self.eps = P.sb("eps", [128, 1])
        P.v("memset", ap=self.eps[:], constant=EPS, w=["eps"])
        self.sq = P.sb("sq", [128, KC, WW])
        self.ps_stat = P.ps("ps_stat", [128, 512])
        self.rt = P.sb("rt", [128, WW])
        self.rstd = P.sb("rstd", [128, WW])
        self.wbuf = [P.sb("wbuf%d" % i, [128, wb_elems]) for i in range(2)]
        self.pp = [P.ps("pp%d" % i, [128, 512]) for i in range(4)]
        self.wcnt = 0
        self.pcnt = 0

    def rstd_of(self, src, skeys, c0, ncol):
        P = self.P
        P.act(self.sq[:, :, :ncol], src[:, :, c0:c0 + ncol], AF.Square, r=skeys, w=["sq"])
        for kc in range(KC):
            P.mm(self.ps_stat[:, :ncol], self.ones[:], self.sq[:, kc, :ncol], kc == 0, kc == KC - 1,
                 r=["ones", "sq"], w=["ps_stat"])
        P.act(self.rt[:, :ncol], self.ps_stat[:, :ncol], AF.Sqrt, bias=self.eps[:, 0:1], scale=1.0 / D,
              r=["ps_stat", "eps"], w=["rt"])
        P.v("reciprocal", out=self.rstd[:, :ncol], in_=self.rt[:, :ncol], r=["rt"], w=["rstd"])

    def gemm(self, w_ap, kch, c0, nchunks, sw, rhs, rkeys, ncol, evac):
        P = self.P
        per = sw // 128
        for s0 in range(0, nchunks, per):
            nper = min(per, nchunks - s0)
            b = self.wcnt % 2
            self.wcnt += 1
            wb = self.wbuf[b][:, 0:kch * nper * 128].rearrange("p (kc n) -> p kc n", kc=kch)
            src = w_ap[:, c0 + 128 * s0:c0 + 128 * (s0 + nper)].rearrange("(kc p) n -> p kc n", p=128)
            P.dma(wb, src, w=[("wb", b)])
            for o in range(nper):
                oc = s0 + o
                pi = self.pcnt % 4
                self.pcnt += 1
                pp = self.pp[pi]
                for kc in range(kch):
                    P.mm(pp[:, :ncol], wb[:, kc, o * 128:(o + 1) * 128], rhs(kc), kc == 0, kc == kch - 1,
                         r=[("wb", b)] + rkeys(kc), w=[("pp", pi)])
                evac(oc, pp, ("pp", pi))


def tile_info(ti):
    if ti < NTL:
        return 0, TO, TO * ti
    return 1, 64, 2048


def build_L0():
    P = Prog()
    sT = P.din("sT", [128, KC, 3])
    w = P.din("w", [4, D, 1536])
    bias = P.din("bias", [3, 4, 1536])
    o = P.dout("o", [3, 4, 1536])
    s_sb = P.sb("s_sb", [128, KC, 3])
    b_sb = P.sb("b_sb", [3, 4, 1536])
    o_sb = P.sb("o_sb", [3, 4, 1536])
    wb = [P.sb("wb%d" % i, [128, KC, 512]) for i in range(2)]
    pp = [P.ps("pp%d" % i, [128, 512]) for i in range(2)]
    P.dma(s_sb[:], sT[:], w=["s"])
    P.dma(b_sb[:], bias[:], w=["b"])
    P.act(s_sb[:], s_sb[:], AF.Silu, r=["s"], w=["s"])
    it = 0
    for l in range(4):
        for n0 in range(0, 1536, 512):
            b = it % 2
            it += 1
            P.dma(wb[b][:], w[l, :, n0:n0 + 512].rearrange("(kc p) n -> p kc n", p=128), w=[("wb", b)])
            for kc in range(KC):
                P.mm(pp[b][:3, :], s_sb[:, kc, :], wb[b][:, kc, :], kc == 0, kc == KC - 1,
                     r=["s", ("wb", b)], w=[("pp", b)])
            P.v("tensor_tensor", out=o_sb[:, l, n0:n0 + 512], in0=pp[b][:3, :], in1=b_sb[:, l, n0:n0 + 512],
                op=ALU.add, r=[("pp", b), "b"], w=["o"])
    P.store(o[:], o_sb[:], r=["o"])
    return P


NA_EVEN = 46


def norm_mod(P, T, src, skey, dst, dkey, Wt, gs_col, sh_col):
    T.rstd_of(src, [(skey, kc) for kc in range(KC)], 0, Wt)
    for kc in range(KC):
        P.v("scalar_tensor_tensor", out=dst[:, kc, :Wt], in0=src[:, kc, :Wt], scalar=gs_col(kc),
            in1=T.rstd[:, :Wt], op0=ALU.mult, op1=ALU.mult,
            r=[(skey, kc), "rstd", "gs"], w=[(dkey, kc)])
        P.act(dst[:, kc, :Wt], dst[:, kc, :Wt], AF.Identity, bias=sh_col(kc), scale=1.0,
              r=[(dkey, kc), "vec"], w=[(dkey, kc)])


def conv3(P, pp, pkey, Wt, nt, fl_sb, ti, cw_sb, ci):
    zs, zk = P.rot("zs", 2, [128, WW])
    acc, ak = P.rot("acc", 3, [128, TO])
    P.act(zs[:, :Wt], pp[:, :Wt], AF.Copy, r=[pkey], w=[zk])
    P.v("tensor_scalar", out=zs[:, 0:1], in0=zs[:, 0:1], scalar1=fl_sb[:, ti, 0:1], scalar2=None,
        op0=ALU.mult, r=[zk, "fl"], w=[zk])
    P.v("tensor_scalar", out=zs[:, nt + 1:nt + 2], in0=zs[:, nt + 1:nt + 2], scalar1=fl_sb[:, ti, 1:2],
        scalar2=None, op0=ALU.mult, r=[zk, "fl"], w=[zk])
    P.v("tensor_scalar", out=acc[:, :nt], in0=zs[:, 1:nt + 1], scalar1=cw_sb[:, ci, 1:2],
        scalar2=cw_sb[:, ci, 3:4], op0=ALU.mult, op1=ALU.add, r=[zk, "cw"], w=[ak])
    P.v("scalar_tensor_tensor", out=acc[:, :nt], in0=zs[:, 0:nt], scalar=cw_sb[:, ci, 0:1],
        in1=acc[:, :nt], op0=ALU.mult, op1=ALU.add, r=[zk, "cw", ak], w=[ak])
    P.v("scalar_tensor_tensor", out=acc[:, :nt], in0=zs[:, 2:nt + 2], scalar=cw_sb[:, ci, 2:3],
        in1=acc[:, :nt], op0=ALU.mult, op1=ALU.add, r=[zk, "cw", ak], w=[ak])
    return acc, ak


def build_A(even):
    P = Prog()
    T = TS(P, KC * 256)
    xw = P.din("xw", [NT, D, WW])
    vec = P.din("vec", [128, 5, KC])
    fl = P.din("fl", [128, NT, 2])
    vec_sb = P.sb("vec_sb", [128, 5, KC])
    fl_sb = P.sb("fl_sb", [128, NT, 2])
    gs = P.sb("gs", [128, 2, KC])
    P.dma(vec_sb[:], vec[:], w=["vec"])
    P.dma(fl_sb[:], fl[:], w=["fl"])
    for s in range(2):
        P.v("scalar_tensor_tensor", out=gs[:, s, :], in0=vec_sb[:, 2 + 2 * s, :], scalar=1.0,
            in1=vec_sb[:, 0, :], op0=ALU.add, op1=ALU.mult, r=["vec"], w=["gs"])
    xt = P.sb("xt", [128, KC, WW])
    ht = P.sb("ht", [128, KC, WW])
    if even:
        w = P.din("w", [D, NA_EVEN * 128])
        cs = P.din("cs", [NT, 128, 2, WW])
        cw = P.din("cw", [128, 24, 4])
        oqkv = P.dout("oqkv", [12, 128, TCORE])
        ox0 = P.dout("ox0", [8, 128, TCORE])
        ovx = P.dout("ovx", [8, 128, TCORE])
        cw_sb = P.sb("cw_sb", [128, 24, 4])
        P.dma(cw_sb[:], cw[:], w=["cw"])
        cs_sb = P.sb("cs_sb", [128, 2, WW])
        hold = P.sb("hold", [128, WW])
        hold2 = P.sb("hold2", [128, TO])
    else:
        oh = P.dout("oh", [KC, 128, TCORE])
    for ti in range(NT):
        s, nt, c0 = tile_info(ti)
        Wt = nt + 2
        P.dma(xt[:, :, :], xw[ti].rearrange("(kc p) w -> p kc w", p=128), w=[("xt", kc) for kc in range(KC)])
        norm_mod(P, T, xt, "xt", ht, "ht", Wt,
                 lambda kc, s=s: gs[:, s, kc:kc + 1], lambda kc, s=s: vec_sb[:, 1 + 2 * s, kc:kc + 1])
        if not even:
            P.store(oh[:, :, c0:c0 + nt].rearrange("kc p t -> p kc t"), ht[:, :, 1:nt + 1],
                    r=[("ht", kc) for kc in range(KC)])
            continue
        P.dma(cs_sb[:], cs[ti], w=["cs"])

        def evac(oc, pp, pkey, ti=ti, nt=nt, Wt=Wt, c0=c0):
            if oc < 20:
                if oc % 2 == 0:
                    P.act(hold[:, :Wt], pp[:, :Wt], AF.Copy, r=[pkey], w=["hold"])
                else:
                    t1, k1 = P.rot("t1", 2, [128, WW])
                    osb, ok = P.rot("osb", 3, [128, WW])
                    P.v("tensor_tensor", out=t1[:, :Wt], in0=hold[:, :Wt], in1=cs_sb[:, 0, :Wt], op=ALU.mult,
                        r=["hold", "cs"], w=[k1], eng="gpsimd")
                    P.v("tensor_tensor", out=osb[:, :Wt], in0=pp[:, :Wt], in1=cs_sb[:, 1, :Wt], op=ALU.mult,
                        r=[pkey, "cs"], w=[ok])
                    P.v("tensor_tensor", out=osb[:, :Wt], in0=osb[:, :Wt], in1=t1[:, :Wt], op=ALU.add,
                        r=[ok, k1], w=[ok])
                    P.store(oqkv[oc // 2, :, c0:c0 + nt], osb[:, 1:nt + 1], r=[ok])
            elif oc < 22:
                osb, ok = P.rot("osb", 3, [128, WW])
                P.act(osb[:, :Wt], pp[:, :Wt], AF.Copy, r=[pkey], w=[ok])
                P.store(oqkv[10 + oc - 20, :, c0:c0 + nt], osb[:, 1:nt + 1], r=[ok])
            elif oc < 30:
                acc, ak = conv3(P, pp, pkey, Wt, nt, fl_sb, ti, cw_sb, oc - 22)
                P.store(ox0[oc - 22, :, c0:c0 + nt], acc[:, :nt], r=[ak])
            else:
                j = (oc - 30) // 2
                acc, ak = conv3(P, pp, pkey, Wt, nt, fl_sb, ti, cw_sb, oc - 22)
                if (oc - 30) % 2 == 0:
                    P.v("tensor_copy", out=hold2[:, :nt], in_=acc[:, :nt], r=[ak], w=["hold2"], eng="gpsimd")
                else:
                    P.v("tensor_tensor", out=acc[:, :nt], in0=acc[:, :nt], in1=hold2[:, :nt], op=ALU.mult,
                        r=[ak, "hold2"], w=[ak], eng="gpsimd")
                    P.store(ovx[j, :, c0:c0 + nt], acc[:, :nt], r=[ak])

        T.gemm(w, KC, 0, NA_EVEN, 256, lambda kc, Wt=Wt: ht[:, kc, :Wt], lambda kc: [("ht", kc)], Wt, evac)
    return P


def build_D(even):
    P = Prog()
    T = TS(P, 43 * 128)
    xw = P.din("xw", [NT, D, WW])
    yw = P.din("yw", [NT, D, WW])
    if not even:
        yw2 = P.din("yw2", [NT, D, WW])
    vec = P.din("vec", [128, 13, KC])
    fl = P.din("fl", [128, NT, 2])
    fcw = P.din("fcw", [128, 86, 4])
    wa = P.din("wa", [D, 2048 if even else 4096])
    wup = P.din("wup", [D, 2 * DFF])
    wdn = P.din("wdn", [DFF, D])
    ox = P.dout("ox", [KC, 128, TCORE])
    vec_sb = P.sb("vec_sb", [128, 13, KC])
    fl_sb = P.sb("fl_sb", [128, NT, 2])
    fcw_sb = P.sb("fcw_sb", [128, 86, 4])
    P.dma(vec_sb[:], vec[:], w=["vec"])
    P.dma(fl_sb[:], fl[:], w=["fl"])
    P.dma(fcw_sb[:], fcw[:], w=["cw"])
    mg1 = P.sb("mg1", [128, 2, KC])
    gs2 = P.sb("gs2", [128, 2, KC])
    mg3 = P.sb("mg3", [128, 2, KC])
    for s in range(2):
        P.v("tensor_tensor", out=mg1[:, s, :], in0=vec_sb[:, 3 + 4 * s, :], in1=vec_sb[:, 0, :], op=ALU.mult,
            r=["vec"], w=["gs"])
        P.v("scalar_tensor_tensor", out=gs2[:, s, :], in0=vec_sb[:, 5 + 4 * s, :], scalar=1.0,
            in1=vec_sb[:, 1, :], op0=ALU.add, op1=ALU.mult, r=["vec"], w=["gs"])
        P.v("tensor_tensor", out=mg3[:, s, :], in0=vec_sb[:, 6 + 4 * s, :], in1=vec_sb[:, 2, :], op=ALU.mult,
            r=["vec"], w=["gs"])
    xt = P.sb("xt", [128, KC, WW])
    yt = P.sb("yt", [128, KC, WW])
    tt = P.sb("tt", [128, KC, WW])
    a_sb = P.sb("a_sb", [128, 43, TO])
    hold = P.sb("hold", [128, WW])
    allk = lambda nm: [(nm, kc) for kc in range(KC)]
    for ti in range(NT):
        s, nt, c0 = tile_info(ti)
        Wt = nt + 2
        P.dma(xt[:, :, :], xw[ti].rearrange("(kc p) w -> p kc w", p=128), w=allk("xt"))
        P.dma(yt[:, :, :], yw[ti].rearrange("(kc p) w -> p kc w", p=128), w=allk("yt"))
        if not even:
            P.dma(tt[:, :, :], yw2[ti].rearrange("(kc p) w -> p kc w", p=128), w=allk("tt"))
            P.v("tensor_tensor", out=yt[:, :, :Wt], in0=yt[:, :, :Wt], in1=tt[:, :, :Wt], op=ALU.add,
                r=allk("yt") + allk("tt"), w=allk("yt"), eng="gpsimd")
            gc = 2.0 * math.sqrt(2.0 / math.pi)
            for kc in range(KC):
                g1_, gk1 = P.rot("ge1", 2, [128, WW])
                P.act(g1_[:, :Wt], yt[:, kc, :Wt], AF.Square, r=[("yt", kc)], w=[gk1])
                P.v("tensor_scalar", out=g1_[:, :Wt], in0=g1_[:, :Wt], scalar1=0.044715, scalar2=1.0,
                    op0=ALU.mult, op1=ALU.add, r=[gk1], w=[gk1])
                P.v("tensor_tensor", out=g1_[:, :Wt], in0=g1_[:, :Wt], in1=yt[:, kc, :Wt], op=ALU.mult,
                    r=[gk1, ("yt", kc)], w=[gk1])
                P.act(g1_[:, :Wt], g1_[:, :Wt], AF.Sigmoid, scale=gc, r=[gk1], w=[gk1])
                P.v("tensor_tensor", out=yt[:, kc, :Wt], in0=yt[:, kc, :Wt], in1=g1_[:, :Wt], op=ALU.mult,
                    r=[gk1, ("yt", kc)], w=[("yt", kc)], eng="gpsimd")

        def evac_a(oc, pp, pkey, Wt=Wt):
            if even:
                P.act(tt[:, oc, :Wt], pp[:, :Wt], AF.Copy, r=[pkey], w=[("tt", oc)])
            else:
                j = oc // 2
                if oc % 2 == 0:
                    P.act(hold[:, :Wt], pp[:, :Wt], AF.Identity, bias=vec_sb[:, 11, j:j + 1], scale=1.0,
                          r=[pkey, "vec"], w=["hold"])
                else:
                    sg, sk = P.rot("sg", 2, [128, WW])
                    P.act(sg[:, :Wt], pp[:, :Wt], AF.Sigmoid, bias=vec_sb[:, 12, j:j + 1], scale=1.0,
                          r=[pkey, "vec"], w=[sk])
                    P.v("tensor_tensor", out=tt[:, j, :Wt], in0=hold[:, :Wt], in1=sg[:, :Wt], op=ALU.mult,
                        r=["hold", sk], w=[("tt", j)])

        T.gemm(wa, KC, 0, 16 if even else 32, 256, lambda kc, Wt=Wt: yt[:, kc, :Wt], lambda kc: [("yt", kc)],
               Wt, evac_a)
        T.rstd_of(tt, allk("tt"), 0, Wt)
        for kc in range(KC):
            P.v("scalar_tensor_tensor", out=tt[:, kc, :Wt], in0=tt[:, kc, :Wt], scalar=mg1[:, s, kc:kc + 1],
                in1=T.rstd[:, :Wt], op0=ALU.mult, op1=ALU.mult, r=[("tt", kc), "rstd", "gs"], w=[("tt", kc)])
            P.v("tensor_tensor", out=xt[:, kc, :Wt], in0=xt[:, kc, :Wt], in1=tt[:, kc, :Wt], op=ALU.add,
                r=[("xt", kc), ("tt", kc)], w=[("xt", kc)], eng="gpsimd")
        norm_mod(P, T, xt, "xt", yt, "yt", Wt,
                 lambda kc, s=s: gs2[:, s, kc:kc + 1], lambda kc, s=s: vec_sb[:, 4 + 4 * s, kc:kc + 1])

        def evac_up(oc, pp, pkey, ti=ti, nt=nt, Wt=Wt):
            j = oc // 2
            acc, ak = conv3(P, pp, pkey, Wt, nt, fl_sb, ti, fcw_sb, oc)
            if oc % 2 == 0:
                P.act(hold[:, :nt], acc[:, :nt], AF.Silu, r=[ak], w=["hold"])
            else:
                P.v("tensor_tensor", out=a_sb[:, j, :nt], in0=acc[:, :nt], in1=hold[:, :nt], op=ALU.mult,
                    r=[ak, "hold"], w=[("a", j)], eng="gpsimd")

        T.gemm(wup, KC, 0, 86, 256, lambda kc, Wt=Wt: yt[:, kc, :Wt], lambda kc: [("yt", kc)], Wt, evac_up)

        def evac_dn(oc, pp, pkey, nt=nt):
            P.act(tt[:, oc, :nt], pp[:, :nt], AF.Copy, r=[pkey], w=[("tt", oc)])

        T.gemm(wdn, 43, 0, KC, 128, lambda kc, nt=nt: a_sb[:, kc, :nt], lambda kc: [("a", kc)], nt, evac_dn)
        T.rstd_of(tt, allk("tt"), 0, nt)
        for kc in range(KC):
            P.v("scalar_tensor_tensor", out=tt[:, kc, :nt], in0=tt[:, kc, :nt], scalar=mg3[:, s, kc:kc + 1],
                in1=T.rstd[:, :nt], op0=ALU.mult, op1=ALU.mult, r=[("tt", kc), "rstd", "gs"], w=[("tt", kc)])
            P.v("tensor_tensor", out=tt[:, kc, :nt], in0=tt[:, kc, :nt], in1=xt[:, kc, 1:nt + 1], op=ALU.add,
                r=[("xt", kc), ("tt", kc)], w=[("tt", kc)], eng="gpsimd")
        P.store(ox[:, :, c0:c0 + nt].rearrange("kc p t -> p kc t"), tt[:, :, :nt], r=allk("tt"))
    return P


def col16(vv):
    return np.ascontiguousarray(np.asarray(vv, np.float32).reshape(-1, 128).T)


def windows(lat, ctx, i):
    b, q = divmod(i, 4)
    F = lat.shape[-1]
    out = np.zeros((NT, F, WW), np.float32)
    lp = np.pad(lat[b], ((1, 1), (0, 0)))
    for t in range(NTL):
        s0 = 2048 * q + TO * t
        out[t] = lp[s0:s0 + WW].T
    cp = np.pad(ctx[b], ((1, 1), (0, 0)))
    s0 = 64 * q
    out[NTL, :, :66] = cp[s0:s0 + 66].T
    return out


def halo_flags(i):
    b, q = divmod(i, 4)
    f = np.zeros((NT, 2), np.float32)
    for t in range(NTL):
        s0 = 2048 * q + TO * t
        f[t, 0] = 1.0 if s0 > 0 else 0.0
        f[t, 1] = 1.0 if s0 + TO < SEQ else 0.0
    f[NTL, 0] = 1.0 if q > 0 else 0.0
    f[NTL, 1] = 1.0 if q < 3 else 0.0
    return np.ascontiguousarray(np.broadcast_to(f[None], (128, NT, 2)))


def unshard(outs):
    nch = outs[0].shape[0]
    F = nch * 128
    lat = np.empty((NB, SEQ, F), np.float32)
    ctx = np.empty((NB, NCTX, F), np.float32)
    for i, o in enumerate(outs):
        b, q = divmod(i, 4)
        m = o.reshape(F, TCORE)
        lat[b, 2048 * q:2048 * (q + 1)] = m[:, :2048].T
        ctx[b, 64 * q:64 * (q + 1)] = m[:, 2048:].T
    return lat, ctx


def host_L0(c, c_ctx, w_mod, b_mod):
    s = np.stack([c[0], c[1], c_ctx]).astype(np.float32)
    sT = np.ascontiguousarray(s.T.reshape(KC, 128, 3).transpose(1, 0, 2))
    P = build_L0()
    maps = []
    for i in range(NCORE):
        cols = slice(1536 * i, 1536 * (i + 1))
        maps.append({"sT": sT, "w": np.ascontiguousarray(w_mod[:, :, cols]),
                     "bias": np.ascontiguousarray(np.broadcast_to(b_mod[None, :, cols], (3, 4, 1536)))})
    res = run_prog(P, maps)
    return np.concatenate([r["o"] for r in res], axis=2)


def mod_cols(m_all, l, b):
    return [col16(m_all[b, l, k * D:(k + 1) * D]) for k in range(6)]


def rope_tables():
    pos = np.arange(SEQ)
    row = (pos // GRID_W).astype(np.float32)
    col = (pos % GRID_W).astype(np.float32)
    inv = (np.float32(10000.0) ** (-np.arange(0, 64, 2, dtype=np.float32) / np.float32(64))).astype(np.float32)
    ar = row[:, None] * inv[None, :]
    ac = col[:, None] * inv[None, :]
    cr, sr, cc, sc = np.cos(ar), np.sin(ar), np.cos(ac), np.sin(ac)
    COS = np.concatenate([cr, cr, cc, cc], axis=1).T.astype(np.float32)
    SIN = np.concatenate([-sr, sr, -sc, sc], axis=1).T.astype(np.float32)
    return COS, SIN


def a_even_perm():
    idx = []
    sw = np.arange(128) ^ 32
    for h in range(8):
        idx.append(128 * h + np.arange(128))
        idx.append(128 * h + sw)
    for g in range(2):
        idx.append(1024 + 128 * g + np.arange(128))
        idx.append(1024 + 128 * g + sw)
    for g in range(2):
        idx.append(1280 + 128 * g + np.arange(128))
    zc = []
    for j in range(8):
        idx.append(1536 + 128 * j + np.arange(128))
        zc.append(128 * j)
    for j in range(8):
        idx.append(1536 + 1024 + 128 * j + np.arange(128))
        zc.append(1024 + 128 * j)
        idx.append(1536 + 2048 + 128 * j + np.arange(128))
        zc.append(2048 + 128 * j)
    return np.concatenate(idx), zc


def host_A(l, x_lat, x_ctx, m_all, norm_g, ev=None):
    even = ev is not None
    P = build_A(even)
    g0 = col16(norm_g[l, 0])
    mc = mod_cols(m_all, l, 2)
    if even:
        w_in, conv_w, conv_b = ev
        idx, zc = a_even_perm()
        wperm = np.ascontiguousarray(w_in[:, idx])
        cw = np.zeros((128, 24, 4), np.float32)
        for ci, z0 in enumerate(zc):
            cw[:, ci, 0:3] = conv_w[:, z0:z0 + 128].T
            cw[:, ci, 3] = conv_b[z0:z0 + 128]
        COS, SIN = rope_tables()
    maps = []
    for i in range(NCORE):
        b, q = divmod(i, 4)
        ml = mod_cols(m_all, l, b)
        vec = np.ascontiguousarray(np.stack([g0, ml[0], ml[1], mc[0], mc[1]], axis=1))
        m = {"xw": windows(x_lat, x_ctx, i), "vec": vec, "fl": halo_flags(i)}
        if even:
            cs = np.zeros((NT, 128, 2, WW), np.float32)
            cs[NTL, :, 0, :] = 1.0
            cp = np.pad(COS, ((0, 0), (1, 1)))
            sp = np.pad(SIN, ((0, 0), (1, 1)))
            for t in range(NTL):
                s0 = 2048 * q + TO * t
                cs[t, :, 0, :] = cp[:, s0:s0 + WW]
                cs[t, :, 1, :] = sp[:, s0:s0 + WW]
            m.update({"w": wperm, "cs": cs, "cw": cw})
        maps.append(m)
    res = run_prog(P, maps)
    if not even:
        return unshard([r["oh"] for r in res])
    return (unshard([r["oqkv"] for r in res]), unshard([r["ox0"] for r in res]), unshard([r["ovx"] for r in res]))


def build_B(nqb=16, nh=8, ctxu=True):
    P = Prog()
    qT = P.din("qT", [8, 128, 2048])
    qcT = P.din("qcT", [8, 128, 64])
    kT = P.din("kT", [2, 128, 2304])
    vB = P.din("vB", [128, 2, 18, 128])
    kcT = P.din("kcT", [2, 128, 256])
    vC = P.din("vC", [128, 2, 2, 128])
    sink = P.din("sink", [128, 8])
    mask = P.din("mask", [128, 16, 384])
    ident = P.din("ident", [128, 128])
    oA = P.dout("oA", [TCORE, 1024])
    q_sb = P.sb("q_sb", [128, 8, 2048])
    qc_sb = P.sb("qc_sb", [128, 8, 64])
    k_sb = P.sb("k_sb", [128, 2, 2304])
    v_sb = P.sb("v_sb", [128, 2, 18, 128])
    kc_sb = P.sb("kc_sb", [128, 2, 256])
    vc_sb = P.sb("vc_sb", [128, 2, 2, 128])
    sink_sb = P.sb("sink_sb", [128, 8])
    mask_sb = P.sb("mask_sb", [128, 16, 384])
    id_sb = P.sb("id_sb", [128, 128])
    for h in range(8):
        P.dma(q_sb[:, h, :], qT[h], w=[("q", h)])
    P.dma(qc_sb[:], qcT.rearrange("h d t -> d h t"), w=["qc"])
    P.dma(k_sb[:], kT.rearrange("g d t -> d g t"), w=["k"])
    P.dma(v_sb[:], vB[:], w=["v"])
    P.dma(kc_sb[:], kcT.rearrange("g d t -> d g t"), w=["kc"])
    P.dma(vc_sb[:], vC[:], w=["vc"])
    P.dma(sink_sb[:], sink[:], w=["sink"])
    P.dma(mask_sb[:], mask[:], w=["mask"])
    P.dma(id_sb[:], ident[:], w=["id"])
    scale = 128.0 ** -0.5
    cp = [0]

    def unit(qp, ncols_band, lhsT_q, qkeys, g, h, qb, row0):
        nk = ncols_band + 256
        nkb = nk // 128
        sm, sk = P.rot("sm", 2, [128, 640])
        ska, skb = (sk, "a"), (sk, "b")
        psB, kB = P.rot("psB", 2, [128, 512], psum=True)
        if ncols_band:
            psA, kA = P.rot("psA", 2, [128, 512], psum=True)
            P.mm(psA[:qp, :384], lhsT_q, k_sb[:, g, qb * 128:qb * 128 + 384], True, True,
                 r=qkeys + ["k"], w=[kA])
            P.v("scalar_tensor_tensor", out=sm[:qp, :384], in0=psA[:qp, :384], scalar=scale,
                in1=mask_sb[:qp, qb, :], op0=ALU.mult, op1=ALU.add, r=[kA, "mask"], w=[ska])
        P.mm(psB[:qp, :256], lhsT_q, kc_sb[:, g, :], True, True, r=qkeys + ["kc"], w=[kB])
        P.act(sm[:qp, ncols_band:nk], psB[:qp, :256], AF.Copy, scale=scale, r=[kB], w=[skb])
        mx, mk = P.rot("mx", 4, [128, 4])
        P.v("tensor_reduce", out=mx[:qp, 0:1], in_=sm[:qp, :nk], axis=AX.X, op=ALU.max, r=[ska, skb], w=[mk])
        P.v("tensor_tensor", out=mx[:qp, 0:1], in0=mx[:qp, 0:1], in1=sink_sb[:qp, h:h + 1], op=ALU.max,
            r=[mk, "sink"], w=[mk])
        P.v("tensor_scalar", out=mx[:qp, 1:2], in0=mx[:qp, 0:1], scalar1=-1.0, scalar2=None, op0=ALU.mult,
            r=[mk], w=[mk])
        P.act(sm[:qp, :nk], sm[:qp, :nk], AF.Exp, bias=mx[:qp, 1:2], scale=1.0, accum_out=mx[:qp, 2:3],
              r=[ska, skb, mk], w=[ska, skb, mk])
        P.act(mx[:qp, 3:4], sink_sb[:qp, h:h + 1], AF.Exp, bias=mx[:qp, 1:2], scale=1.0, r=["sink", mk], w=[mk])
        P.v("tensor_tensor", out=mx[:qp, 2:3], in0=mx[:qp, 2:3], in1=mx[:qp, 3:4], op=ALU.add, r=[mk], w=[mk])
        P.v("reciprocal", out=mx[:qp, 3:4], in_=mx[:qp, 2:3], r=[mk], w=[mk])
        eT, ek = P.rot("eT", 2, [128, 5, 128])
        for kb in range(nkb):
            pT, pk = P.rot("psT", 2, [128, 512], psum=True)
            P.tr(pT[:, :qp], sm[:qp, kb * 128:(kb + 1) * 128], id_sb[:qp, :qp], r=[ska, skb, "id"], w=[pk])
            cp[0] += 1
            if cp[0] % 2 == 0:
                P.act(eT[:, kb, :qp], pT[:, :qp], AF.Copy, r=[pk], w=[(ek, kb)])
            else:
                P.v("tensor_copy", out=eT[:, kb, :qp], in_=pT[:, :qp], r=[pk], w=[(ek, kb)])
        pO, ok = P.rot("psO", 2, [128, 512], psum=True)
        for kb in range(nkb):
            if ncols_band and kb < 3:
                vv = v_sb[:, g, qb + kb, :]
            else:
                vv = vc_sb[:, g, kb - (3 if ncols_band else 0), :]
            P.mm(pO[:qp, :128], eT[:, kb, :qp], vv, kb == 0, kb == nkb - 1, r=[(ek, kb), "v", "vc"], w=[ok])
        osb, osk = P.rot("osb", 3, [128, 128])
        P.v("tensor_scalar", out=osb[:qp, :], in0=pO[:qp, :128], scalar1=mx[:qp, 3:4], scalar2=None,
            op0=ALU.mult, r=[ok, mk], w=[osk])
        P.store(oA[row0:row0 + qp, h * 128:(h + 1) * 128], osb[:qp, :], r=[osk])

    for qb in range(nqb):
        for h in range(nh):
            unit(128, 384, q_sb[:, h, qb * 128:(qb + 1) * 128], [("q", h)], h // 4, h, qb, qb * 128)
    if ctxu:
        for h in range(nh):
            unit(64, 0, qc_sb[:, h, :], ["qc"], h // 4, h, 0, 2048)
    return P


def host_B(qkv_l, qkv_c, sinkv):
    P = build_B()
    maps = []
    ident = np.eye(128, dtype=np.float32)
    qi = np.arange(128)[:, None]
    kj = np.arange(384)[None, :]
    band = np.abs(kj - 128 - qi) <= 128
    for i in range(NCORE):
        b, q = divmod(i, 4)
        r0 = 2048 * q
        ql = qkv_l[b, r0:r0 + 2048, :1024]
        qT = np.ascontiguousarray(ql.reshape(2048, 8, 128).transpose(1, 2, 0))
        qc = qkv_c[b, 64 * q:64 * q + 64, :1024]
        qcT = np.ascontiguousarray(qc.reshape(64, 8, 128).transpose(1, 2, 0))
        kp = np.pad(qkv_l[b, :, 1024:1280], ((128, 128), (0, 0)))[r0:r0 + 2304]
        kT = np.ascontiguousarray(kp.reshape(2304, 2, 128).transpose(1, 2, 0))
        vp = np.pad(qkv_l[b, :, 1280:1536], ((128, 128), (0, 0)))[r0:r0 + 2304]
        vB = np.ascontiguousarray(vp.reshape(18, 128, 2, 128).transpose(1, 2, 0, 3))
        kcT = np.ascontiguousarray(qkv_c[b, :, 1024:1280].reshape(256, 2, 128).transpose(1, 2, 0))
        vC = np.ascontiguousarray(qkv_c[b, :, 1280:1536].reshape(2, 128, 2, 128).transpose(1, 2, 0, 3))
        mask = np.zeros((128, 16, 384), np.float32)
        for qb in range(16):
            kpos = r0 + 128 * (qb - 1) + kj
            valid = band & (kpos >= 0) & (kpos < SEQ)
            mask[:, qb, :] = np.where(valid, np.float32(0.0), np.float32(-1e30))
        maps.append({"qT": qT, "qcT": qcT, "kT": kT, "vB": vB, "kcT": kcT, "vC": vC,
                     "sink": np.ascontiguousarray(np.broadcast_to(sinkv[None, :], (128, 8))).astype(np.float32),
                     "mask": mask, "ident": ident})
    res = run_prog(P, maps)
    ya_l = np.empty((NB, SEQ, 1024), np.float32)
    ya_c = np.empty((NB, NCTX, 1024), np.float32)
    for i, r in enumerate(res):
        b, q = divmod(i, 4)
        ya_l[b, 2048 * q:2048 * q + 2048] = r["oA"][:2048]
        ya_c[b, 64 * q:64 * q + 64] = r["oA"][2048:]
    return ya_l, ya_c


def build_C():
    P = Prog()
    nc = P.nc
    vx_l = P.din("vx_l", [128, 128, 2, 64])
    x0_l = P.din("x0_l", [128, 128, 2, 64])
    vx_c = P.din("vx_c", [128, 128, 2, 2])
    x0_c = P.din("x0_c", [128, 128, 2, 2])
    zin = {(8192, 0): P.din("zf8", [33, 8192]), (8192, 1): P.din("zr8", [33, 8192]),
           (256, 0): P.din("zf2", [33, 256]), (256, 1): P.din("zr2", [33, 256])}
    din = {(8192, 0): P.din("df8", [128, 8192]), (8192, 1): P.din("dr8", [128, 8192]),
           (256, 0): P.din("df2", [128, 256]), (256, 1): P.din("dr2", [128, 256])}
    w1 = P.din("w1", [33, 64])
    w2 = P.din("w2", [64, 64])
    w3f = P.din("w3f", [64, 128])
    w3b = P.din("w3b", [64, 128])
    fb = P.din("fb", [64, 3])
    hbias = P.din("hbias", [128, 1])
    yb_l = P.dout("yb_l", [128, 128, 2, 64])
    yb_c = P.dout("yb_c", [128, 128, 2, 2])
    kl = {8192: P.dscr("kline8", [128, 16384]), 256: P.dscr("kline2", [128, 512])}
    w1_sb = P.sb("w1_sb", [33, 64])
    w2_sb = P.sb("w2_sb", [64, 64])
    w3_sb = [P.sb("w3f_sb", [64, 128]), P.sb("w3b_sb", [64, 128])]
    fb_sb = P.sb("fb_sb", [64, 3])
    hb_sb = P.sb("hbias_sb", [128, 1])
    sp = P.sb("sp", [64, 3])
    P.dma(w1_sb[:], w1[:], w=["w"])
    P.dma(w2_sb[:], w2[:], w=["w"])
    P.dma(w3_sb[0][:], w3f[:], w=["w"])
    P.dma(w3_sb[1][:], w3b[:], w=["w"])
    P.dma(fb_sb[:], fb[:], w=["fb"])
    P.dma(hb_sb[:], hbias[:], w=["hbias"])
    P.v("tensor_scalar", out=sp[:, 0:1], in0=fb_sb[:, 0:1], scalar1=1.0 / TWO_PI, scalar2=None, op0=ALU.mult,
        r=["fb"], w=["sp"])
    for k in (1, 2):
        P.v("tensor_scalar", out=sp[:, k:k + 1], in0=fb_sb[:, k:k + 1], scalar1=sp[:, 0:1], scalar2=64.0,
            op0=ALU.mult, op1=ALU.add, r=["fb", "sp"], w=["sp"])
    filt = [P.sb("filt_f", [128, 8192]), P.sb("filt_r", [128, 8192])]

    def sinpipe(ps, pkey, L, s2col):
        y, yk = P.rot("sy", 2, [64, 512])
        yi, ik = P.rot("syi", 2, [64, 512], dt=I32)
        f, fk = P.rot("sf", 2, [64, 512])
        h, hk = P.rot("sh", 3, [64, 512])
        P.v("tensor_scalar", out=y[:, :L], in0=ps[:64, :L], scalar1=sp[:, 0:1], scalar2=sp[:, s2col:s2col + 1],
            op0=ALU.mult, op1=ALU.add, r=[pkey, "sp"], w=[yk])
        P.v("tensor_copy", out=yi[:, :L], in_=y[:, :L], r=[yk], w=[ik])
        P.v("scalar_tensor_tensor", out=f[:, :L], in0=yi[:, :L], scalar=-1.0, in1=y[:, :L], op0=ALU.mult,
            op1=ALU.add, r=[ik, yk], w=[fk])
        P.v("scalar_tensor_tensor", out=y[:, :L], in0=f[:, :L], scalar=0.5, in1=f[:, :L], op0=ALU.is_gt,
            op1=ALU.subtract, r=[fk], w=[yk])
        P.act(h[:, :L], y[:, :L], AF.Sin, scale=-TWO_PI, r=[yk], w=[hk])
        return h, hk

    for n in (8192, 256):
        for d in (0, 1):
            for t0 in range(0, n, 512):
                L = min(512, n - t0)
                zs, zk = P.rot("zs", 2, [33, 512])
                P.dma(zs[:, :L], zin[(n, d)][:, t0:t0 + L], w=[zk])
                ps1, k1 = P.rot("fps", 3, [128, 512], psum=True)
                P.mm(ps1[:64, :L], w1_sb[:], zs[:, :L], True, True, r=["w", zk], w=[k1])
                h1, hk1 = sinpipe(ps1, k1, L, 1)
                ps2, k2 = P.rot("fps", 3, [128, 512], psum=True)
                P.mm(ps2[:64, :L], w2_sb[:], h1[:, :L], True, True, r=["w", hk1], w=[k2])
                h2, hk2 = sinpipe(ps2, k2, L, 2)
                ps3, k3 = P.rot("fps", 3, [128, 512], psum=True)
                P.mm(ps3[:, :L], w3_sb[d][:], h2[:, :L], True, True, r=["w", hk2], w=[k3])
                dt_, dk = P.rot("dect", 2, [128, 512])
                P.dma(dt_[:, :L], din[(n, d)][:, t0:t0 + L], w=[dk])
                P.v("tensor_tensor", out=filt[d][:, t0:t0 + L], in0=ps3[:, :L], in1=dt_[:, :L], op=ALU.mult,
                    r=[k3, dk], w=[("filt", d)])
        nrm = P.sb("nrm%d" % n, [128, 4])
        P.v("tensor_reduce", out=nrm[:, 0:1], in_=filt[0][:, :n], axis=AX.X, op=ALU.add, apply_absolute_value=True,
            r=[("filt", 0)], w=["nrm"])
        P.v("tensor_reduce", out=nrm[:, 1:2], in_=filt[1][:, :n - 1], axis=AX.X, op=ALU.add,
            apply_absolute_value=True, r=[("filt", 1)], w=["nrm"])
        P.v("tensor_tensor", out=nrm[:, 2:3], in0=nrm[:, 0:1], in1=nrm[:, 1:2], op=ALU.add, r=["nrm"], w=["nrm"])
        P.v("reciprocal", out=nrm[:, 3:4], in_=nrm[:, 2:3], r=["nrm"], w=["nrm"])
        for d in (0, 1):
            P.v("tensor_scalar", out=filt[d][:, :n], in0=filt[d][:, :n], scalar1=nrm[:, 3:4], scalar2=None,
                op0=ALU.mult, r=[("filt", d), "nrm"], w=[("filt", d)])
        P.v("tensor_tensor", out=filt[0][:, 0:1], in0=filt[0][:, 0:1], in1=hb_sb[:, 0:1], op=ALU.add,
            r=[("filt", 0), "hbias"], w=[("filt", 0)])
        kla = kl[n].ap()
        P.dma(kla[:, 0:n - 1], filt[1][:, 0:n - 1], r=[("filt", 1)], w=[("kl", n, 1)])
        P.dma(kla[:, n - 1:2 * n - 1], filt[0][:, 0:n], r=[("filt", 0)], w=[("kl", n, 0)])

    qn = [0]
    for n, nblk, vx, x0, yb in ((8192, 64, vx_l, x0_l, yb_l), (256, 2, vx_c, x0_c, yb_c)):
        pad = nblk - 1
        vps = [P.sb("vp%d_%d" % (n, i), [128, 2, nblk + 2 * pad]) for i in range(2)]
        for i in range(2):
            P.v("memset", ap=vps[i][:], constant=0.0, w=[("vp", n, i)], eng="gpsimd")
        for c in range(128):
            vp, vk = vps[c % 2], ("vp", n, c % 2)
            P.dma(vp[:, :, pad:pad + nblk], vx[:, c], w=[vk])
            x0t, xk = P.rot("x0t%d" % n, 2, [128, 2, nblk])
            P.dma(x0t[:], x0[:, c], w=[xk])
            acc, ak = P.rot("hacc", 2, [128, 512], psum=True)
            accv = acc[:, 0:2 * nblk].rearrange("p (b k) -> p b k", b=2)
            for dl in range(-pad, pad + 1):
                hb, hk = P.rot("hb", 10, [128, 128])
                src = bass.AP(kl[n], c * (2 * n) + (n - 1) + 128 * dl - 127, [[1, 128], [1, 128]])
                qn[0] += 1
                P.dma(hb[:], src, r=[("kl", n, 0), ("kl", n, 1)], w=[hk], q=("sync" if qn[0] % 2 else "scalar"))
                P.mm(accv, hb[:], vp[:, :, pad - dl:pad - dl + nblk], dl == -pad, dl == pad,
                     r=[hk, vk], w=[ak])
            ysb, yk = P.rot("ysb%d" % n, 2, [128, 2, nblk])
            P.v("tensor_tensor", out=ysb[:], in0=accv, in1=x0t[:], op=ALU.mult, r=[ak, xk], w=[yk])
            P.store(yb[:, c], ysb[:], r=[yk])
    return P


def hyena_tables(n):
    f32 = np.float32
    t = np.linspace(0.0, 1.0, n, dtype=f32)[:, None]
    w = (f32(2.0 * math.pi / n) * np.arange(n, dtype=f32)).astype(f32)
    bands = np.linspace(1e-4, 15, 16, dtype=f32)
    ang = (w[:, None] * bands[None, :]).astype(f32)
    z = np.concatenate([t, np.cos(ang), -np.sin(ang)], axis=-1).astype(f32)
    max_decay = math.log(1e-2) / 0.3
    min_decay = math.log(1e-2) / 1.5
    deltas = np.linspace(min_decay, max_decay, 1024, dtype=f32)
    decay = np.exp(-t * np.abs(deltas)[None, :]).astype(f32)
    return z, decay


def host_C(vx_l, x0_l, vx_c, x0_c, hp):
    f_w1, f_b1, f_w2, f_b2, f_w3, f_freq, hy_bias = hp
    P = build_C()
    z8, d8 = hyena_tables(SEQ)
    z2, d2 = hyena_tables(NCTX)
    fb = np.ascontiguousarray(np.stack([f_freq, f_b1, f_b2], axis=1)).astype(np.float32)
    maps = []

    def lay(a, nblk, rev):
        a = a.reshape(NB, nblk, 128, 128)
        if rev:
            a = a[:, :, ::-1, :]
        return np.ascontiguousarray(a.transpose(2, 3, 0, 1))

    for i in range(NCORE):
        cs = slice(128 * i, 128 * (i + 1))
        maps.append({
            "vx_l": lay(vx_l[:, :, cs], 64, True), "x0_l": lay(x0_l[:, :, cs], 64, False),
            "vx_c": lay(vx_c[:, :, cs], 2, True), "x0_c": lay(x0_c[:, :, cs], 2, False),
            "zf8": np.ascontiguousarray(z8.T), "zr8": np.ascontiguousarray(z8[::-1].T),
            "zf2": np.ascontiguousarray(z2.T), "zr2": np.ascontiguousarray(z2[::-1].T),
            "df8": np.ascontiguousarray(d8[:, cs].T), "dr8": np.ascontiguousarray(d8[::-1, cs].T),
            "df2": np.ascontiguousarray(d2[:, cs].T), "dr2": np.ascontiguousarray(d2[::-1, cs].T),
            "w1": np.ascontiguousarray(f_w1), "w2": np.ascontiguousarray(f_w2),
            "w3f": np.ascontiguousarray(f_w3[:, cs]), "w3b": np.ascontiguousarray(f_w3[:, 1024 + 128 * i:1024 + 128 * (i + 1)]),
            "fb": fb, "hbias": np.ascontiguousarray(hy_bias[cs].reshape(128, 1)),
        })
    res = run_prog(P, maps)
    yb_l = np.empty((NB, SEQ, 1024), np.float32)
    yb_c = np.empty((NB, NCTX, 1024), np.float32)
    for i, r in enumerate(res):
        cs = slice(128 * i, 128 * (i + 1))
        yb_l[:, :, cs] = r["yb_l"].transpose(2, 3, 0, 1).reshape(NB, SEQ, 128)
        yb_c[:, :, cs] = r["yb_c"].transpose(2, 3, 0, 1).reshape(NB, NCTX, 128)
    return yb_l, yb_c


NV = SEQ + NCTX


def build_S5():
    P = Prog()
    u = P.din("u", [2, 32, 8, 2, NV])
    pv = P.din("pv", [128, 16, 3])
    bex = P.din("bex", [128, 16, 2, 32])
    cex = P.din("cex", [128, 16, 2, 32])
    dd = P.din("dd", [32, 8, 32])
    ident = P.din("ident", [128, 128])
    iota = P.din("iota", [128, 512])
    y = P.dout("y", [2, 32, 8, 2, NV])
    pv_sb = P.sb("pv_sb", [128, 16, 3])
    bex_sb = P.sb("bex_sb", [128, 16, 2, 32])
    cex_sb = P.sb("cex_sb", [128, 16, 2, 32])
    dd_sb = P.sb("dd_sb", [32, 8, 32])
    id_sb = P.sb("id_sb", [128, 128])
    io_sb = P.sb("io_sb", [128, 512])
    P.dma(pv_sb[:], pv[:], w=["pv"])
    P.dma(bex_sb[:], bex[:], w=["bex"])
    P.dma(cex_sb[:], cex[:], w=["cex"])
    P.dma(dd_sb[:], dd[:], w=["dd"])
    P.dma(id_sb[:], ident[:], w=["id"])
    P.dma(io_sb[:], iota[:], w=["iota"])
    halfpi = P.sb("halfpi", [128, 1])
    P.v("memset", ap=halfpi[:], constant=math.pi / 2, w=["halfpi"])
    ones = P.sb("ones512", [128, 512])
    P.v("memset", ap=ones[:], constant=1.0, w=["ones"])
    cn = [0]

    def col(name):
        cn[0] += 1
        return P.sb("c_%s_%d" % (name, cn[0]), [128, 16]), ("col", cn[0])

    def tt(out, ok, a, ak, b, bk, op, eng="vector"):
        P.v("tensor_tensor", out=out, in0=a, in1=b, op=op, r=[ak, bk], w=[ok], eng=eng)

    def wrap_turns(dst, dk, src, sk, shape, name):
        ti_, tik = P.rot("wi_" + name, 2, shape, dt=I32)
        f, fk = P.rot("wf_" + name, 2, shape)
        P.v("tensor_copy", out=ti_[:], in_=src, r=[sk], w=[tik])
        P.v("scalar_tensor_tensor", out=f[:], in0=ti_[:], scalar=-1.0, in1=src, op0=ALU.mult, op1=ALU.add,
            r=[tik, sk], w=[fk])
        P.v("scalar_tensor_tensor", out=dst, in0=f[:], scalar=0.5, in1=f[:], op0=ALU.is_gt, op1=ALU.subtract,
            r=[fk], w=[dk])
        P.v("scalar_tensor_tensor", out=dst, in0=dst, scalar=0.5, in1=dst, op0=ALU.is_gt, op1=ALU.subtract,
            r=[dk], w=[dk])

    def sincos(sin_t, sk_, cos_t, ck_, c, ck, shape, name):
        ab, abk = P.rot("ab_" + name, 2, shape)
        P.act(sin_t, c, AF.Sin, scale=TWO_PI, r=[ck], w=[sk_])
        P.v("scalar_tensor_tensor", out=ab[:], in0=c, scalar=-1.0, in1=c, op0=ALU.mult, op1=ALU.max,
            r=[ck], w=[abk])
        P.act(cos_t, ab[:], AF.Sin, scale=-TWO_PI, bias=halfpi[:, 0:1], r=[abk, "halfpi"], w=[ck_])

    lre = pv_sb[:, :, 0]
    lim = pv_sb[:, :, 1]
    dtc, dtk = col("dt")
    P.act(dtc[:], pv_sb[:, :, 2], AF.Exp, r=["pv"], w=[dtk])
    a_, a_k = col("a")
    tt(a_[:], a_k, lre, "pv", dtc[:], dtk, ALU.mult)
    th, thk = col("th")
    tt(th[:], thk, lim, "pv", dtc[:], dtk, ALU.mult)
    rr, rk = col("r")
    P.act(rr[:], a_[:], AF.Exp, r=[a_k], w=[rk])
    ph, phk = col("ph")
    P.v("tensor_scalar", out=ph[:], in0=th[:], scalar1=1.0 / TWO_PI, scalar2=None, op0=ALU.mult, r=[thk], w=[phk])
    phi, phik = col("phi")
    wrap_turns(phi[:], phik, ph[:], phk, [128, 16], "p")
    sn, snk = col("sn")
    cs_, csk = col("cs")
    sincos(sn[:], snk, cs_[:], csk, phi[:], phik, [128, 16], "p")
    nr, nrk = col("nr")
    tt(nr[:], nrk, rr[:], rk, cs_[:], csk, ALU.mult)
    P.v("tensor_scalar", out=nr[:], in0=nr[:], scalar1=-1.0, scalar2=None, op0=ALU.add, r=[nrk], w=[nrk])
    ni, nik = col("ni")
    tt(ni[:], nik, rr[:], rk, sn[:], snk, ALU.mult)
    den, denk = col("den")
    t0_, t0k = col("t0")
    tt(den[:], denk, lre, "pv", lre, "pv", ALU.mult)
    tt(t0_[:], t0k, lim, "pv", lim, "pv", ALU.mult)
    tt(den[:], denk, den[:], denk, t0_[:], t0k, ALU.add)
    inv, invk = col("inv")
    P.v("reciprocal", out=inv[:], in_=den[:], r=[denk], w=[invk])
    wre, wrek = col("wre")
    wim, wimk = col("wim")
    t1_, t1k = col("t1")
    tt(wre[:], wrek, nr[:], nrk, lre, "pv", ALU.mult)
    tt(t1_[:], t1k, ni[:], nik, lim, "pv", ALU.mult)
    tt(wre[:], wrek, wre[:], wrek, t1_[:], t1k, ALU.add)
    tt(wre[:], wrek, wre[:], wrek, inv[:], invk, ALU.mult)
    tt(wim[:], wimk, ni[:], nik, lre, "pv", ALU.mult)
    tt(t1_[:], t1k, nr[:], nrk, lim, "pv", ALU.mult)
    tt(wim[:], wimk, wim[:], wimk, t1_[:], t1k, ALU.subtract)
    tt(wim[:], wimk, wim[:], wimk, inv[:], invk, ALU.mult)
    nwim, nwimk = col("nwim")
    P.v("tensor_scalar", out=nwim[:], in0=wim[:], scalar1=-1.0, scalar2=None, op0=ALU.mult, r=[wimk], w=[nwimk])
    bbar = P.sb("bbar", [128, 16, 2, 32])
    BT = P.sb("BT", [32, 16, 2, 128])
    ncim = P.sb("ncim", [128, 16, 32])
    P.v("tensor_scalar", out=ncim[:], in0=cex_sb[:, :, 1, :], scalar1=-1.0, scalar2=None, op0=ALU.mult,
        r=["cex"], w=["ncim"])
    rT = P.sb("rT", [128, 16, 512])
    psT = P.ps("psT", [128, 512])
    for s in range(16):
        P.v("tensor_scalar", out=bbar[:, s, 0, :], in0=bex_sb[:, s, 0, :], scalar1=wre[:, s:s + 1], scalar2=None,
            op0=ALU.mult, r=["bex", wrek], w=[("bbar", s, 0)])
        P.v("scalar_tensor_tensor", out=bbar[:, s, 0, :], in0=bex_sb[:, s, 1, :], scalar=nwim[:, s:s + 1],
            in1=bbar[:, s, 0, :], op0=ALU.mult, op1=ALU.add, r=["bex", nwimk, ("bbar", s, 0)], w=[("bbar", s, 0)])
        P.v("tensor_scalar", out=bbar[:, s, 1, :], in0=bex_sb[:, s, 1, :], scalar1=wre[:, s:s + 1], scalar2=None,
            op0=ALU.mult, r=["bex", wrek], w=[("bbar", s, 1)])
        P.v("scalar_tensor_tensor", out=bbar[:, s, 1, :], in0=bex_sb[:, s, 0, :], scalar=wim[:, s:s + 1],
            in1=bbar[:, s, 1, :], op0=ALU.mult, op1=ALU.add, r=["bex", wimk, ("bbar", s, 1)], w=[("bbar", s, 1)])
        for ri in range(2):
            P.tr(psT[:32, :128], bbar[:, s, ri, :], id_sb[:], r=[("bbar", s, ri), "id"], w=["psT"])
            P.act(BT[:, s, ri, :], psT[:32, :128], AF.Copy, r=["psT"], w=[("BT", s)])
        P.v("tensor_scalar", out=rT[:, s, :], in0=ones[:], scalar1=rr[:, s:s + 1], scalar2=None, op0=ALU.mult,
            r=["ones", rk], w=[("rT", s)], eng="gpsimd")

    chunks = [(t0, min(512, NV - t0)) for t0 in range(0, NV, 512)]
    sh = [128, 512]
    for dr in range(2):
        for tau in range(8):
            s = dr * 8 + tau
            prev = {0: None, 1: None}
            for (t0, L) in chunks:
                ang, angk = P.rot("ang", 2, sh)
                P.v("tensor_scalar", out=ang[:, :L], in0=io_sb[:, :L], scalar1=float(t0), scalar2=phi[:, s:s + 1],
                    op0=ALU.add, op1=ALU.mult, r=["iota", phik], w=[angk])
                cc, cck = P.rot("cc", 2, sh)
                ti_, tik = P.rot("wi_m", 2, sh, dt=I32)
                f, fk = P.rot("wf_m", 2, sh)
                P.v("tensor_copy", out=ti_[:, :L], in_=ang[:, :L], r=[angk], w=[tik])
                P.v("scalar_tensor_tensor", out=f[:, :L], in0=ti_[:, :L], scalar=-1.0, in1=ang[:, :L], op0=ALU.mult,
                    op1=ALU.add, r=[tik, angk], w=[fk])
                P.v("scalar_tensor_tensor", out=cc[:, :L], in0=f[:, :L], scalar=0.5, in1=f[:, :L], op0=ALU.is_gt,
                    op1=ALU.subtract, r=[fk], w=[cck])
                P.v("scalar_tensor_tensor", out=cc[:, :L], in0=cc[:, :L], scalar=0.5, in1=cc[:, :L], op0=ALU.is_gt,
                    op1=ALU.subtract, r=[cck], w=[cck])
                sinT, sink_ = P.rot("sinT", 2, sh)
                cosT, cosk = P.rot("cosT", 2, sh)
                ab, abk = P.rot("ab_m", 2, sh)
                P.act(sinT[:, :L], cc[:, :L], AF.Sin, scale=TWO_PI, r=[cck], w=[sink_])
                P.v("scalar_tensor_tensor", out=ab[:, :L], in0=cc[:, :L], scalar=-1.0, in1=cc[:, :L], op0=ALU.mult,
                    op1=ALU.max, r=[cck], w=[abk])
                P.act(cosT[:, :L], ab[:, :L], AF.Sin, scale=-TWO_PI, bias=halfpi[:, 0:1], r=[abk, "halfpi"], w=[cosk])
                for b in range(2):
                    ut, uk = P.rot("ut", 3, [32, 512])
                    P.dma(ut[:, :L], u[dr, :, tau, b, t0:t0 + L], w=[uk])
                    pre_, prk = P.rot("pbre", 2, [128, 512], psum=True)
                    pim, pik = P.rot("pbim", 2, [128, 512], psum=True)
                    P.mm(pre_[:, :L], BT[:, s, 0, :], ut[:, :L], True, True, r=[("BT", s), uk], w=[prk])
                    P.mm(pim[:, :L], BT[:, s, 1, :], ut[:, :L], True, True, r=[("BT", s), uk], w=[pik])
                    t1, k1 = P.rot("m1", 2, sh)
                    t2, k2 = P.rot("m2", 2, sh)
                    t3, k3 = P.rot("m3", 2, sh)
                    t4, k4 = P.rot("m4", 2, sh)
                    tt(t1[:, :L], k1, pre_[:, :L], prk, cosT[:, :L], cosk, ALU.mult)
                    tt(t2[:, :L], k2, pim[:, :L], pik, sinT[:, :L], sink_, ALU.mult)
                    tt(t3[:, :L], k3, pim[:, :L], pik, cosT[:, :L], cosk, ALU.mult)
                    tt(t4[:, :L], k4, pre_[:, :L], prk, sinT[:, :L], sink_, ALU.mult)
                    tt(t1[:, :L], k1, t1[:, :L], k1, t2[:, :L], k2, ALU.add, eng="gpsimd")
                    tt(t3[:, :L], k3, t3[:, :L], k3, t4[:, :L], k4, ALU.subtract, eng="gpsimd")
                    sre, srk = P.rot("sre%d" % b, 2, sh)
                    sim, sik = P.rot("sim%d" % b, 2, sh)
                    if prev[b] is None:
                        ire, iim, ikeys = 0.0, 0.0, []
                    else:
                        (pre_t, pre_k, pim_t, pim_k, pL) = prev[b]
                        ire, iim, ikeys = pre_t[:, pL - 1:pL], pim_t[:, pL - 1:pL], [pre_k, pim_k]
                    P.v("tensor_tensor_scan", out=sre[:, :L], data0=rT[:, s, :L], data1=t1[:, :L], initial=ire,
                        op0=ALU.mult, op1=ALU.add, r=[("rT", s), k1] + ikeys, w=[srk])
                    P.v("tensor_tensor_scan", out=sim[:, :L], data0=rT[:, s, :L], data1=t3[:, :L], initial=iim,
                        op0=ALU.mult, op1=ALU.add, r=[("rT", s), k3] + ikeys, w=[sik])
                    prev[b] = (sre, srk, sim, sik, L)
                    d1, dk1 = P.rot("d1", 2, sh)
                    d2, dk2 = P.rot("d2", 2, sh)
                    d3, dk3 = P.rot("d3", 2, sh)
                    d4, dk4 = P.rot("d4", 2, sh)
                    tt(d1[:, :L], dk1, sre[:, :L], srk, cosT[:, :L], cosk, ALU.mult)
                    tt(d2[:, :L], dk2, sim[:, :L], sik, sinT[:, :L], sink_, ALU.mult, eng="gpsimd")
                    tt(d3[:, :L], dk3, sim[:, :L], sik, cosT[:, :L], cosk, ALU.mult)
                    tt(d4[:, :L], dk4, sre[:, :L], srk, sinT[:, :L], sink_, ALU.mult, eng="gpsimd")
                    tt(d1[:, :L], dk1, d1[:, :L], dk1, d2[:, :L], dk2, ALU.subtract, eng="gpsimd")
                    tt(d3[:, :L], dk3, d3[:, :L], dk3, d4[:, :L], dk4, ALU.add, eng="gpsimd")
                    py, pyk = P.rot("psy", 2, [128, 512], psum=True)
                    P.mm(py[:32, :L], cex_sb[:, s, 0, :], d1[:, :L], True, False, r=["cex", dk1], w=[pyk])
                    P.mm(py[:32, :L], ncim[:, s, :], d3[:, :L], False, dr == 1, r=["ncim", dk3], w=[pyk])
                    if dr == 0:
                        P.mm(py[:32, :L], dd_sb[:, tau, :], ut[:, :L], False, True, r=["dd", uk], w=[pyk])
                    ysb, yk = P.rot("ysb", 3, [32, 512])
                    P.act(ysb[:, :L], py[:32, :L], AF.Copy, r=[pyk], w=[yk])
                    P.store(y[dr, :, tau, b, t0:t0 + L], ysb[:, :L], r=[yk])
    return P


def host_S5(hl, hc, sp):
    lam_re, lam_im, log_step, b_re, b_im, c_re, c_im, dvec = sp
    P = build_S5()
    uv = [np.concatenate([hc, hl], axis=1), np.concatenate([hc[:, ::-1], hl[:, ::-1]], axis=1)]
    ident = np.eye(128, dtype=np.float32)
    iota = np.ascontiguousarray(np.broadcast_to(np.arange(512, dtype=np.float32)[None], (128, 512)))
    maps = []
    for i in range(NCORE):
        f0 = 256 * i
        u = np.empty((2, 32, 8, 2, NV), np.float32)
        for dr in range(2):
            u[dr] = uv[dr][:, :, f0:f0 + 256].reshape(NB, NV, 8, 32).transpose(3, 2, 0, 1)
        pv = np.zeros((128, 16, 3), np.float32)
        bex = np.zeros((128, 16, 2, 32), np.float32)
        cex = np.zeros((128, 16, 2, 32), np.float32)
        dd = np.zeros((32, 8, 32), np.float32)
        for dr in range(2):
            for tau in range(8):
                s = dr * 8 + tau
                for g2 in range(2):
                    g = 16 * i + 2 * tau + g2
                    rows = slice(64 * g2, 64 * g2 + 64)
                    cols = slice(16 * g2, 16 * g2 + 16)
                    pv[rows, s, 0] = lam_re[dr, g]
                    pv[rows, s, 1] = lam_im[dr, g]
                    pv[rows, s, 2] = log_step[dr, g]
                    bex[rows, s, 0, cols] = b_re[dr, g]
                    bex[rows, s, 1, cols] = b_im[dr, g]
                    cex[rows, s, 0, cols] = c_re[dr, g].T
                    cex[rows, s, 1, cols] = c_im[dr, g].T
        for tau in range(8):
            dd[np.arange(32), tau, np.arange(32)] = dvec[f0 + 32 * tau:f0 + 32 * tau + 32]
        maps.append({"u": u, "pv": pv, "bex": bex, "cex": cex, "dd": dd, "ident": ident, "iota": iota})
    res = run_prog(P, maps)
    yv = [np.empty((NB, NV, D), np.float32) for _ in range(2)]
    for i, r in enumerate(res):
        f0 = 256 * i
        for dr in range(2):
            yv[dr][:, :, f0:f0 + 256] = r["y"][dr].transpose(2, 3, 1, 0).reshape(NB, NV, 256)
    yf_c, yf_l = yv[0][:, :NCTX], yv[0][:, NCTX:]
    yr_c, yr_l = yv[1][:, :NCTX][:, ::-1], yv[1][:, NCTX:][:, ::-1]
    return (np.ascontiguousarray(yf_l), np.ascontiguousarray(yf_c), np.ascontiguousarray(yr_l), np.ascontiguousarray(yr_c))


def host_D(l, x_lat, x_ctx, y_lat, y_ctx, m_all, norm_g, wa, wup, wdn, fconv_w, fconv_b, y2=None, b_glu=None):
    even = y2 is None
    P = build_D(even)
    g1, g2, g3 = col16(norm_g[l, 1]), col16(norm_g[l, 2]), col16(norm_g[l, 3])
    mc = mod_cols(m_all, l, 2)
    up_idx = []
    fcw = np.zeros((128, 86, 4), np.float32)
    for j in range(43):
        for k, c0 in enumerate((128 * j, DFF + 128 * j)):
            up_idx.append(c0 + np.arange(128))
            fcw[:, 2 * j + k, 0:3] = fconv_w[:, c0:c0 + 128].T
            fcw[:, 2 * j + k, 3] = fconv_b[c0:c0 + 128]
    wup_p = np.ascontiguousarray(wup[:, np.concatenate(up_idx)])
    if even:
        wa_p = np.ascontiguousarray(wa)
        ba = bg = np.zeros((128, KC), np.float32)
    else:
        a_idx = []
        for j in range(16):
            a_idx.append(128 * j + np.arange(128))
            a_idx.append(2048 + 128 * j + np.arange(128))
        wa_p = np.ascontiguousarray(wa[:, np.concatenate(a_idx)])
        ba, bg = col16(b_glu[:2048]), col16(b_glu[2048:])
    wdn_p = np.ascontiguousarray(wdn)
    maps = []
    for i in range(NCORE):
        b, q = divmod(i, 4)
        ml = mod_cols(m_all, l, b)
        vec = np.ascontiguousarray(np.stack([g1, g2, g3, ml[2], ml[3], ml[4], ml[5], mc[2], mc[3], mc[4], mc[5], ba, bg],
                                            axis=1))
        m = {"xw": windows(x_lat, x_ctx, i), "yw": windows(y_lat, y_ctx, i), "vec": vec, "fl": halo_flags(i),
             "fcw": fcw, "wa": wa_p, "wup": wup_p, "wdn": wdn_p}
        if not even:
            m["yw2"] = windows(y2[0], y2[1], i)
        maps.append(m)
    res = run_prog(P, maps)
    return unshard([r["ox"] for r in res])


def kernel(**inputs):
    inp = {k: np.asarray(v, dtype=np.float32) for k, v in inputs.items()}
    ng = inp["norm_g"]
    m_all = host_L0(inp["c"], inp["c_ctx"], inp["w_mod"], inp["b_mod"])
    xl, xc = inp["x"], inp["ctx"]
    for l in range(4):
        j = l // 2
        if l % 2 == 0:
            (qkv_l, qkv_c), (x0_l, x0_c), (vx_l, vx_c) = host_A(
                l, xl, xc, m_all, ng, (inp["ab_w_in"][j], inp["hy_conv_w"][j], inp["hy_conv_b"][j]))
            ya_l, ya_c = host_B(qkv_l, qkv_c, inp["attn_sink"][j])
            yb_l, yb_c = host_C(vx_l, x0_l, vx_c, x0_c,
                                (inp["hy_f_w1"][j], inp["hy_f_b1"][j], inp["hy_f_w2"][j], inp["hy_f_b2"][j],
                                 inp["hy_f_w3"][j], inp["hy_f_freq"][j], inp["hy_bias"][j]))
            yl = np.concatenate([ya_l, yb_l], axis=2)
            yc = np.concatenate([ya_c, yb_c], axis=2)
            xl, xc = host_D(l, xl, xc, yl, yc, m_all, ng, inp["ab_w_out"][j], inp["ffn_w_up"][l],
                            inp["ffn_w_down"][l], inp["ffn_conv_w"][l], inp["ffn_conv_b"][l])
        else:
            hl, hc = host_A(l, xl, xc, m_all, ng)
            yf_l, yf_c, yr_l, yr_c = host_S5(
                hl, hc, (inp["s5_lam_re"][j], inp["s5_lam_im"][j], inp["s5_log_step"][j], inp["s5_b_re"][j],
                         inp["s5_b_im"][j], inp["s5_c_re"][j], inp["s5_c_im"][j], inp["s5_d"][j]))
            xl, xc = host_D(l, xl, xc, yf_l, yf_c, m_all, ng, inp["s5_w_glu"][j], inp["ffn_w_up"][l],
                            inp["ffn_w_down"][l], inp["ffn_conv_w"][l], inp["ffn_conv_b"][l],
                            y2=(yr_l, yr_c), b_glu=inp["s5_b_glu"][j])
    return np.ascontiguousarray(xl.astype(np.float32))
```

```python
import math
import numpy as np
from contextlib import ExitStack
import concourse.bass as bass
import concourse.mybir as mybir
from concourse.bass_utils import run_bass_kernel_spmd

F32 = mybir.dt.float32
F32R = mybir.dt.float32r
I32 = mybir.dt.int32
AF = mybir.ActivationFunctionType
ALU = mybir.AluOpType
AX = mybir.AxisListType

DMA_R = 14
ENGS = ("sync", "scalar", "vector", "gpsimd", "tensor")

D = 2048
KC = 16
SEQ = 8192
NCTX = 256
NB = 2
NCORE = 8
TO = 256
WW = TO + 2
NTL = 8
NT = 9
TCORE = 2112
DFF = 5504
GRID_W = 64
EPS = 1e-6
TWO_PI = 2.0 * math.pi


class Prog:
    def __init__(self):
        self.nc = bass.Bass("TRN2", target_bir_lowering=False)
        self.es = ExitStack()
        self.ops = []
        self.outkeys = []
        self.n = 0
        self.rots = {}

    def din(self, name, shape, dt=F32):
        return self.nc.dram_tensor(name, list(shape), dt, kind="ExternalInput").ap()

    def dout(self, name, shape, dt=F32):
        return self.nc.dram_tensor(name, list(shape), dt, kind="ExternalOutput").ap()

    def dscr(self, name, shape, dt=F32):
        return self.nc.dram_tensor(name, list(shape), dt, kind="Internal")

    def sb(self, name, shape, dt=F32):
        return self.es.enter_context(self.nc.sbuf_tensor(name, list(shape), dt))

    def ps(self, name, shape, dt=F32):
        return self.es.enter_context(self.nc.psum_tensor(name, list(shape), dt))

    def rot(self, name, n, shape, dt=F32, psum=False):
        if name not in self.rots:
            mk = self.ps if psum else self.sb
            self.rots[name] = [[mk("%s_%d" % (name, i), shape, dt) for i in range(n)], 0]
        lst = self.rots[name]
        i = lst[1] % len(lst[0])
        lst[1] += 1
        return lst[0][i], (name, i)

    def op(self, eng, fn, r=(), w=(), dma=False):
        self.ops.append((eng, fn, tuple(r), tuple(w), dma))

    def dma(self, out, in_, r=(), w=(), q="sync", **kw):
        self.op(q, lambda e: e.dma_start(out=out, in_=in_, **kw), r, w, dma=True)

    def store(self, out, in_, r=(), q="sync", **kw):
        self.n += 1
        k = ("__out", self.n)
        self.outkeys.append(k)
        self.dma(out, in_, r=r, w=(k,), q=q, **kw)

    def mm(self, out, lhsT, rhs, start, stop, r=(), w=()):
        self.op("tensor", lambda e: e.matmul(out, lhsT, rhs, start=start, stop=stop), r, w)

    def tr(self, out, in_, ident, r=(), w=()):
        self.op("tensor", lambda e: e.transpose(out, in_, ident), r, w)

    def act(self, out, in_, func, r=(), w=(), **kw):
        self.op("scalar", lambda e: e.activation(out=out, in_=in_, func=func, **kw), r, w)

    def v(self, name, r=(), w=(), eng="vector", **kw):
        self.op(eng, lambda e: getattr(e, name)(**kw), r, w)

    def build(self):
        nc = self.nc
        ops = self.ops
        ops.append(("sync", None, tuple(self.outkeys), (), False))
        N = len(ops)
        last_w = {}
        rd_c = {}
        rd_d = {}
        deps = [None] * N
        for i, (eng, fn, r, w, dma) in enumerate(ops):
            d = set()
            for k in r:
                j = last_w.get(k)
                if j is not None:
                    d.add(j)
            for k in w:
                j = last_w.get(k)
                if j is not None:
                    d.add(j)
                for j in rd_c.get(k, {}).values():
                    d.add(j)
                for j in rd_d.get(k, ()):
                    d.add(j)
            for k in r:
                if dma:
                    rd_d.setdefault(k, []).append(i)
                else:
                    rd_c.setdefault(k, {})[eng] = i
            for k in w:
                last_w[k] = i
                rd_c[k] = {}
                rd_d[k] = []
            d.discard(i)
            deps[i] = d
        waited_c = {e: {} for e in ENGS}
        waited_d = {e: set() for e in ENGS}
        final = [None] * N
        signaled = set()
        for i, (eng, fn, r, w, dma) in enumerate(ops):
            best = {}
            dl = []
            for j in deps[i]:
                je, _, _, _, jd = ops[j]
                if jd:
                    if j not in waited_d[eng]:
                        dl.append(j)
                        waited_d[eng].add(j)
                else:
                    if je == eng and eng == "tensor" and not dma:
                        continue
                    if j > best.get(je, -1):
                        best[je] = j
            cl = []
            for je, j in best.items():
                if waited_c[eng].get(je, -1) >= j:
                    continue
                waited_c[eng][je] = j
                cl.append(j)
                signaled.add(j)
            final[i] = (cl, dl)
        sem = {e: self.es.enter_context(nc.semaphore("s_" + e)) for e in ENGS}
        dq = {}
        for q in ("sync", "scalar", "gpsimd"):
            dq[q] = [self.es.enter_context(nc.semaphore("d_%s_%d" % (q, t))) for t in range(DMA_R)]
        cnt = {e: 0 for e in ENGS}
        dcnt = {q: 0 for q in dq}
        sig = [None] * N
        pre = [None] * N
        for i, (eng, fn, r, w, dma) in enumerate(ops):
            if dma:
                n = dcnt[eng]
                dcnt[eng] += 1
                s = dq[eng][n % DMA_R]
                sig[i] = (s, 16 * (n // DMA_R + 1))
                if n >= DMA_R:
                    pre[i] = (s, 16 * (n // DMA_R))
            elif i in signaled:
                cnt[eng] += 1
                sig[i] = (sem[eng], cnt[eng])
        per = {e: [] for e in ENGS}
        for i, o in enumerate(ops):
            per[o[0]].append(i)
        self.stats = {e: len(per[e]) for e in per}

        def emit(e, name):
            for i in per[name]:
                eng, fn, r, w, dma = ops[i]
                cl, dl = final[i]
                if pre[i] is not None:
                    e.wait_ge(pre[i][0], pre[i][1])
                for j in cl + dl:
                    e.wait_ge(sig[j][0], sig[j][1])
                if fn is None:
                    continue
                ins = fn(e)
                if dma:
                    ins.then_inc(sig[i][0], 16)
                elif sig[i] is not None:
                    ins.then_inc(sig[i][0], 1)

        with nc.Block() as block:
            @block.sync
            def _(e):
                emit(e, "sync")

            @block.scalar
            def _(e):
                emit(e, "scalar")

            @block.vector
            def _(e):
                emit(e, "vector")

            @block.gpsimd
            def _(e):
                emit(e, "gpsimd")

            @block.tensor
            def _(e):
                emit(e, "tensor")
        self.es.close()
        return nc


_N_LAUNCH = [0]


def run_prog(P, in_maps):
    nc = P.build()
    res = run_bass_kernel_spmd(nc, in_maps, core_ids=list(range(NCORE)))
    _N_LAUNCH[0] += 1
    return res.results


class TS:
    def __init__(self, P, wb_elems, sq=None):
        self.P = P
        self.ones = P.sb("ones", [128, 128])
        P.v("memset", ap=self.ones[:], constant=1.0, w=["ones"])
        self.eps = P.sb("eps", [128, 1])
        P.v("memset", ap=self.eps[:], constant=EPS, w=["eps"])
        self.sq = sq if sq is not None else P.sb("sq", [128, KC, WW])
        self.ps_stat = P.ps("ps_stat", [128, 512])
        self.rt = P.sb("rt", [128, WW])
        self.rstd = P.sb("rstd", [128, WW])
        self.wbuf = [P.sb("wbuf%d" % i, [128, wb_elems]) for i in range(2)]
        self.wbr = [P.sb("wbr%d" % i, [128, wb_elems]) for i in range(2)]
        self.hr = P.sb("hr", [128, KC, WW])
        self.pp = [P.ps("pp%d" % i, [128, 512]) for i in range(4)]
        self.wcnt = 0
        self.pcnt = 0

    def rstd_of(self, src, skeys, c0, ncol, sqkeys=("sq",)):
        P = self.P
        sqk = list(sqkeys)
        P.act(self.sq[:, :, :ncol], src[:, :, c0:c0 + ncol], AF.Square, r=skeys, w=sqk)
        for kc in range(KC):
            P.mm(self.ps_stat[:, :ncol], self.ones[:], self.sq[:, kc, :ncol], kc == 0, kc == KC - 1,
                 r=["ones"] + sqk, w=["ps_stat"])
        P.act(self.rt[:, :ncol], self.ps_stat[:, :ncol], AF.Sqrt, bias=self.eps[:, 0:1], scale=1.0 / D,
              r=["ps_stat", "eps"], w=["rt"])
        P.v("reciprocal", out=self.rstd[:, :ncol], in_=self.rt[:, :ncol], r=["rt"], w=["rstd"])

    def gemm(self, w_t, kch, nchunks, sw, rhs, rkeys, ncol, evac):
        P = self.P
        per = sw // 128
        for si, s0 in enumerate(range(0, nchunks, per)):
            b = self.wcnt % 2
            self.wcnt += 1
            wbf = self.wbuf[b][:, 0:kch * sw]
            wrf = self.wbr[b][:, 0:kch * sw]
            P.dma(wbf, w_t[si], w=[("wb", b)])
            if self.wcnt % 2 == 0:
                P.act(wrf.bitcast(F32R), wbf, AF.Copy, r=[("wb", b)], w=[("wr", b)])
            else:
                P.v("tensor_copy", out=wrf.bitcast(F32R), in_=wbf, r=[("wb", b)], w=[("wr", b)], eng="gpsimd")
            wb = wrf.rearrange("p (kc n) -> p kc n", kc=kch)
            for o in range(per):
                oc = s0 + o
                pi = self.pcnt % 4
                self.pcnt += 1
                pp = self.pp[pi]
                for kc in range(kch):
                    P.mm(pp[:, :ncol], wb[:, kc, o * 128:(o + 1) * 128].bitcast(F32R), rhs(kc), kc == 0,
                         kc == kch - 1, r=[("wr", b)] + rkeys(kc), w=[("pp", pi)])
                evac(oc, pp, ("pp", pi))


def tile_w(w, kch, sw):
    ns = w.shape[1] // sw
    return np.ascontiguousarray(w.reshape(kch, 128, ns, sw).transpose(2, 1, 0, 3).reshape(ns, 128, kch * sw))


def tile_info(ti):
    if ti < NTL:
        return 0, TO, TO * ti
    return 1, 64, 2048


def build_L0():
    P = Prog()
    sT = P.din("sT", [128, KC, 3])
    w = P.din("w", [4, D, 1536])
    bias = P.din("bias", [3, 4, 1536])
    o = P.dout("o", [3, 4, 1536])
    s_sb = P.sb("s_sb", [128, KC, 3])
    b_sb = P.sb("b_sb", [3, 4, 1536])
    o_sb = P.sb("o_sb", [3, 4, 1536])
    wb = [P.sb("wb%d" % i, [128, KC, 512]) for i in range(2)]
    pp = [P.ps("pp%d" % i, [128, 512]) for i in range(2)]
    P.dma(s_sb[:], sT[:], w=["s"])
    P.dma(b_sb[:], bias[:], w=["b"])
    P.act(s_sb[:], s_sb[:], AF.Silu, r=["s"], w=["s"])
    it = 0
    for l in range(4):
        for n0 in range(0, 1536, 512):
            b = it % 2
            it += 1
            P.dma(wb[b][:], w[l, :, n0:n0 + 512].rearrange("(kc p) n -> p kc n", p=128), w=[("wb", b)])
            for kc in range(KC):
                P.mm(pp[b][:3, :], s_sb[:, kc, :], wb[b][:, kc, :], kc == 0, kc == KC - 1,
                     r=["s", ("wb", b)], w=[("pp", b)])
            P.v("tensor_tensor", out=o_sb[:, l, n0:n0 + 512], in0=pp[b][:3, :], in1=b_sb[:, l, n0:n0 + 512],
                op=ALU.add, r=[("pp", b), "b"], w=["o"])
    P.store(o[:], o_sb[:], r=["o"])
    return P


NA_EVEN = 46


def norm_mod(P, T, src, skey, dst, dkey, Wt, gs_col, sh_col, r32=False, sqkeys=("sq",)):
    T.rstd_of(src, [(skey, kc) for kc in range(KC)], 0, Wt, sqkeys)
    for kc in range(KC):
        if r32:
            tmp, tk = P.rot("nmtmp", 2, [128, WW])
            P.v("scalar_tensor_tensor", out=tmp[:, :Wt], in0=src[:, kc, :Wt], scalar=gs_col(kc),
                in1=T.rstd[:, :Wt], op0=ALU.mult, op1=ALU.mult, r=[(skey, kc), "rstd", "gs"], w=[tk])
            P.act(T.hr[:, kc, :Wt].bitcast(F32R), tmp[:, :Wt], AF.Identity, bias=sh_col(kc), scale=1.0,
                  r=[tk, "vec"], w=[("hr", kc)])
        else:
            P.v("scalar_tensor_tensor", out=dst[:, kc, :Wt], in0=src[:, kc, :Wt], scalar=gs_col(kc),
                in1=T.rstd[:, :Wt], op0=ALU.mult, op1=ALU.mult,
                r=[(skey, kc), "rstd", "gs"], w=[(dkey, kc)])
            P.act(dst[:, kc, :Wt], dst[:, kc, :Wt], AF.Identity, bias=sh_col(kc), scale=1.0,
                  r=[(dkey, kc), "vec"], w=[(dkey, kc)])


def conv3(P, pp, pkey, Wt, nt, fl_sb, ti, cw_sb, ci):
    zs, zk = P.rot("zs", 2, [128, WW])
    acc, ak = P.rot("acc", 2, [128, TO])
    P.act(zs[:, :Wt], pp[:, :Wt], AF.Copy, r=[pkey], w=[zk])
    P.v("tensor_scalar", out=zs[:, 0:1], in0=zs[:, 0:1], scalar1=fl_sb[:, ti, 0:1], scalar2=None,
        op0=ALU.mult, r=[zk, "fl"], w=[zk])
    P.v("tensor_scalar", out=zs[:, nt + 1:nt + 2], in0=zs[:, nt + 1:nt + 2], scalar1=fl_sb[:, ti, 1:2],
        scalar2=None, op0=ALU.mult, r=[zk, "fl"], w=[zk])
    P.v("tensor_scalar", out=acc[:, :nt], in0=zs[:, 1:nt + 1], scalar1=cw_sb[:, ci, 1:2],
        scalar2=cw_sb[:, ci, 3:4], op0=ALU.mult, op1=ALU.add, r=[zk, "cw"], w=[ak])
    P.v("scalar_tensor_tensor", out=acc[:, :nt], in0=zs[:, 0:nt], scalar=cw_sb[:, ci, 0:1],
        in1=acc[:, :nt], op0=ALU.mult, op1=ALU.add, r=[zk, "cw", ak], w=[ak])
    P.v("scalar_tensor_tensor", out=acc[:, :nt], in0=zs[:, 2:nt + 2], scalar=cw_sb[:, ci, 2:3],
        in1=acc[:, :nt], op0=ALU.mult, op1=ALU.add, r=[zk, "cw", ak], w=[ak])
    return acc, ak


def build_A(even):
    P = Prog()
    T = TS(P, KC * 256)
    xw = P.din("xw", [NT, D, WW])
    vec = P.din("vec", [128, 5, KC])
    fl = P.din("fl", [128, NT, 2])
    vec_sb = P.sb("vec_sb", [128, 5, KC])
    fl_sb = P.sb("fl_sb", [128, NT, 2])
    gs = P.sb("gs", [128, 2, KC])
    P.dma(vec_sb[:], vec[:], w=["vec"])
    P.dma(fl_sb[:], fl[:], w=["fl"])
    for s in range(2):
        P.v("scalar_tensor_tensor", out=gs[:, s, :], in0=vec_sb[:, 2 + 2 * s, :], scalar=1.0,
            in1=vec_sb[:, 0, :], op0=ALU.add, op1=ALU.mult, r=["vec"], w=["gs"])
    xt = P.sb("xt", [128, KC, WW])
    ht = None if even else P.sb("ht", [128, KC, WW])
    if even:
        w = P.din("w", [NA_EVEN // 2, 128, KC * 256])
        cs = P.din("cs", [NT, 128, 2, WW])
        cw = P.din("cw", [128, 24, 4])
        oqkv = P.dout("oqkv", [12, 128, TCORE])
        ox0 = P.dout("ox0", [8, 128, TCORE])
        ovx = P.dout("ovx", [8, 128, TCORE])
        cw_sb = P.sb("cw_sb", [128, 24, 4])
        P.dma(cw_sb[:], cw[:], w=["cw"])
        cs_sb = P.sb("cs_sb", [128, 2, WW])
        hold = P.sb("hold", [128, WW])
        hold2 = P.sb("hold2", [128, TO])
    else:
        oh = P.dout("oh", [KC, 128, TCORE])
    for ti in range(NT):
        s, nt, c0 = tile_info(ti)
        Wt = nt + 2
        P.dma(xt[:, :, :], xw[ti].rearrange("(kc p) w -> p kc w", p=128), w=[("xt", kc) for kc in range(KC)])
        norm_mod(P, T, xt, "xt", ht, "ht", Wt,
                 lambda kc, s=s: gs[:, s, kc:kc + 1], lambda kc, s=s: vec_sb[:, 1 + 2 * s, kc:kc + 1], r32=even)
        if not even:
            P.store(oh[:, :, c0:c0 + nt].rearrange("kc p t -> p kc t"), ht[:, :, 1:nt + 1],
                    r=[("ht", kc) for kc in range(KC)])
            continue
        P.dma(cs_sb[:], cs[ti], w=["cs"])

        def evac(oc, pp, pkey, ti=ti, nt=nt, Wt=Wt, c0=c0):
            if oc < 20:
                if oc % 2 == 0:
                    P.act(hold[:, :Wt], pp[:, :Wt], AF.Copy, r=[pkey], w=["hold"])
                else:
                    t1, k1 = P.rot("t1", 2, [128, WW])
                    osb, ok = P.rot("osb", 3, [128, WW])
                    P.v("tensor_tensor", out=t1[:, :Wt], in0=hold[:, :Wt], in1=cs_sb[:, 0, :Wt], op=ALU.mult,
                        r=["hold", "cs"], w=[k1], eng="gpsimd")
                    P.v("tensor_tensor", out=osb[:, :Wt], in0=pp[:, :Wt], in1=cs_sb[:, 1, :Wt], op=ALU.mult,
                        r=[pkey, "cs"], w=[ok])
                    P.v("tensor_tensor", out=osb[:, :Wt], in0=osb[:, :Wt], in1=t1[:, :Wt], op=ALU.add,
                        r=[ok, k1], w=[ok])
                    P.store(oqkv[oc // 2, :, c0:c0 + nt], osb[:, 1:nt + 1], r=[ok])
            elif oc < 22:
                osb, ok = P.rot("osb", 3, [128, WW])
                P.act(osb[:, :Wt], pp[:, :Wt], AF.Copy, r=[pkey], w=[ok])
                P.store(oqkv[10 + oc - 20, :, c0:c0 + nt], osb[:, 1:nt + 1], r=[ok])
            elif oc < 30:
                acc, ak = conv3(P, pp, pkey, Wt, nt, fl_sb, ti, cw_sb, oc - 22)
                P.store(ox0[oc - 22, :, c0:c0 + nt], acc[:, :nt], r=[ak])
            else:
                j = (oc - 30) // 2
                acc, ak = conv3(P, pp, pkey, Wt, nt, fl_sb, ti, cw_sb, oc - 22)
                if (oc - 30) % 2 == 0:
                    P.v("tensor_copy", out=hold2[:, :nt], in_=acc[:, :nt], r=[ak], w=["hold2"], eng="gpsimd")
                else:
                    P.v("tensor_tensor", out=acc[:, :nt], in0=acc[:, :nt], in1=hold2[:, :nt], op=ALU.mult,
                        r=[ak, "hold2"], w=[ak], eng="gpsimd")
                    P.store(ovx[j, :, c0:c0 + nt], acc[:, :nt], r=[ak])

        T.gemm(w, KC, NA_EVEN, 256, lambda kc, Wt=Wt: T.hr[:, kc, :Wt].bitcast(F32R), lambda kc: [("hr", kc)], Wt,
               evac)
    return P


def build_D(even):
    P = Prog()
    yt = P.sb("yt", [128, KC, WW])
    T = TS(P, 43 * 128, sq=yt)
    YK = [("yt", kc) for kc in range(KC)]
    xw = P.din("xw", [NT, D, WW])
    yw = P.din("yw", [NT, D, WW])
    if not even:
        yw2 = P.din("yw2", [NT, D, WW])
    vec = P.din("vec", [128, 13, KC])
    fl = P.din("fl", [128, NT, 2])
    fcw = P.din("fcw", [128, 86, 4])
    wa = P.din("wa", [8 if even else 16, 128, KC * 256])
    wup = P.din("wup", [43, 128, KC * 256])
    wdn = P.din("wdn", [16, 128, 43 * 128])
    ox = P.dout("ox", [KC, 128, TCORE])
    vec_sb = P.sb("vec_sb", [128, 13, KC])
    fl_sb = P.sb("fl_sb", [128, NT, 2])
    fcw_sb = P.sb("fcw_sb", [128, 86, 4])
    P.dma(vec_sb[:], vec[:], w=["vec"])
    P.dma(fl_sb[:], fl[:], w=["fl"])
    P.dma(fcw_sb[:], fcw[:], w=["cw"])
    mg1 = P.sb("mg1", [128, 2, KC])
    gs2 = P.sb("gs2", [128, 2, KC])
    mg3 = P.sb("mg3", [128, 2, KC])
    for s in range(2):
        P.v("tensor_tensor", out=mg1[:, s, :], in0=vec_sb[:, 3 + 4 * s, :], in1=vec_sb[:, 0, :], op=ALU.mult,
            r=["vec"], w=["gs"])
        P.v("scalar_tensor_tensor", out=gs2[:, s, :], in0=vec_sb[:, 5 + 4 * s, :], scalar=1.0,
            in1=vec_sb[:, 1, :], op0=ALU.add, op1=ALU.mult, r=["vec"], w=["gs"])
        P.v("tensor_tensor", out=mg3[:, s, :], in0=vec_sb[:, 6 + 4 * s, :], in1=vec_sb[:, 2, :], op=ALU.mult,
            r=["vec"], w=["gs"])
    xt = P.sb("xt", [128, KC, WW])
    tt = P.sb("tt", [128, KC, WW])
    a_sb = P.sb("a_sb", [128, 43, TO])
    hold = P.sb("hold", [128, WW])
    allk = lambda nm: [(nm, kc) for kc in range(KC)]
    for ti in range(NT):
        s, nt, c0 = tile_info(ti)
        Wt = nt + 2
        P.dma(xt[:, :, :], xw[ti].rearrange("(kc p) w -> p kc w", p=128), w=allk("xt"))
        P.dma(yt[:, :, :], yw[ti].rearrange("(kc p) w -> p kc w", p=128), w=allk("yt"))
        if even:
            for kc in range(KC):
                P.v("tensor_copy", out=T.hr[:, kc, :Wt].bitcast(F32R), in_=yt[:, kc, :Wt], r=[("yt", kc)],
                    w=[("hr", kc)], eng="gpsimd")
        if not even:
            P.dma(tt[:, :, :], yw2[ti].rearrange("(kc p) w -> p kc w", p=128), w=allk("tt"))
            P.v("tensor_tensor", out=yt[:, :, :Wt], in0=yt[:, :, :Wt], in1=tt[:, :, :Wt], op=ALU.add,
                r=allk("yt") + allk("tt"), w=allk("yt"), eng="gpsimd")
            gc = 2.0 * math.sqrt(2.0 / math.pi)
            for kc in range(KC):
                g1_, gk1 = P.rot("sg", 1, [128, WW])
                P.act(g1_[:, :Wt], yt[:, kc, :Wt], AF.Square, r=[("yt", kc)], w=[gk1])
                P.v("tensor_scalar", out=g1_[:, :Wt], in0=g1_[:, :Wt], scalar1=0.044715, scalar2=1.0,
                    op0=ALU.mult, op1=ALU.add, r=[gk1], w=[gk1])
                P.v("tensor_tensor", out=g1_[:, :Wt], in0=g1_[:, :Wt], in1=yt[:, kc, :Wt], op=ALU.mult,
                    r=[gk1, ("yt", kc)], w=[gk1])
                P.act(g1_[:, :Wt], g1_[:, :Wt], AF.Sigmoid, scale=gc, r=[gk1], w=[gk1])
                P.v("tensor_tensor", out=T.hr[:, kc, :Wt].bitcast(F32R), in0=yt[:, kc, :Wt], in1=g1_[:, :Wt],
                    op=ALU.mult, r=[gk1, ("yt", kc)], w=[("hr", kc)])

        def evac_a(oc, pp, pkey, Wt=Wt):
            if even:
                P.act(tt[:, oc, :Wt], pp[:, :Wt], AF.Copy, r=[pkey], w=[("tt", oc)])
            else:
                j = oc // 2
                if oc % 2 == 0:
                    P.act(hold[:, :Wt], pp[:, :Wt], AF.Identity, bias=vec_sb[:, 11, j:j + 1], scale=1.0,
                          r=[pkey, "vec"], w=["hold"])
                else:
                    sg, sk = P.rot("sg", 1, [128, WW])
                    P.act(sg[:, :Wt], pp[:, :Wt], AF.Sigmoid, bias=vec_sb[:, 12, j:j + 1], scale=1.0,
                          r=[pkey, "vec"], w=[sk])
                    P.v("tensor_tensor", out=tt[:, j, :Wt], in0=hold[:, :Wt], in1=sg[:, :Wt], op=ALU.mult,
                        r=["hold", sk], w=[("tt", j)])

        T.gemm(wa, KC, 16 if even else 32, 256, lambda kc, Wt=Wt: T.hr[:, kc, :Wt].bitcast(F32R),
               lambda kc: [("hr", kc)], Wt, evac_a)
        T.rstd_of(tt, allk("tt"), 0, Wt, YK)
        for kc in range(KC):
            P.v("scalar_tensor_tensor", out=tt[:, kc, :Wt], in0=tt[:, kc, :Wt], scalar=mg1[:, s, kc:kc + 1],
                in1=T.rstd[:, :Wt], op0=ALU.mult, op1=ALU.mult, r=[("tt", kc), "rstd", "gs"], w=[("tt", kc)])
            P.v("tensor_tensor", out=xt[:, kc, :Wt], in0=xt[:, kc, :Wt], in1=tt[:, kc, :Wt], op=ALU.add,
                r=[("xt", kc), ("tt", kc)], w=[("xt", kc)], eng="gpsimd")
        norm_mod(P, T, xt, "xt", None, None, Wt,
                 lambda kc, s=s: gs2[:, s, kc:kc + 1], lambda kc, s=s: vec_sb[:, 4 + 4 * s, kc:kc + 1], r32=True,
                 sqkeys=YK)

        def evac_up(oc, pp, pkey, ti=ti, nt=nt, Wt=Wt):
            j = oc // 2
            acc, ak = conv3(P, pp, pkey, Wt, nt, fl_sb, ti, fcw_sb, oc)
            if oc % 2 == 0:
                P.act(hold[:, :nt], acc[:, :nt], AF.Silu, r=[ak], w=["hold"])
            else:
                P.v("tensor_tensor", out=a_sb[:, j, :nt].bitcast(F32R), in0=acc[:, :nt], in1=hold[:, :nt],
                    op=ALU.mult, r=[ak, "hold"], w=[("a", j)], eng="gpsimd")

        T.gemm(wup, KC, 86, 256, lambda kc, Wt=Wt: T.hr[:, kc, :Wt].bitcast(F32R), lambda kc: [("hr", kc)], Wt,
               evac_up)

        def evac_dn(oc, pp, pkey, nt=nt):
            P.act(tt[:, oc, :nt], pp[:, :nt], AF.Copy, r=[pkey], w=[("tt", oc)])

        T.gemm(wdn, 43, KC, 128, lambda kc, nt=nt: a_sb[:, kc, :nt].bitcast(F32R), lambda kc: [("a", kc)], nt,
               evac_dn)
        T.rstd_of(tt, allk("tt"), 0, nt, YK)
        for kc in range(KC):
            P.v("scalar_tensor_tensor", out=tt[:, kc, :nt], in0=tt[:, kc, :nt], scalar=mg3[:, s, kc:kc + 1],
                in1=T.rstd[:, :nt], op0=ALU.mult, op1=ALU.mult, r=[("tt", kc), "rstd", "gs"], w=[("tt", kc)])
            P.v("tensor_tensor", out=tt[:, kc, :nt], in0=tt[:, kc, :nt], in1=xt[:, kc, 1:nt + 1], op=ALU.add,
                r=[("xt", kc), ("tt", kc)], w=[("tt", kc)], eng="gpsimd")
        P.store(ox[:, :, c0:c0 + nt].rearrange("kc p t -> p kc t"), tt[:, :, :nt], r=allk("tt"))
    return P


def col16(vv):
    return np.ascontiguousarray(np.asarray(vv, np.float32).reshape(-1, 128).T)


def windows(lat, ctx, i):
    b, q = divmod(i, 4)
    F = lat.shape[-1]
    out = np.zeros((NT, F, WW), np.float32)
    lp = np.pad(lat[b], ((1, 1), (0, 0)))
    for t in range(NTL):
        s0 = 2048 * q + TO * t
        out[t] = lp[s0:s0 + WW].T
    cp = np.pad(ctx[b], ((1, 1), (0, 0)))
    s0 = 64 * q
    out[NTL, :, :66] = cp[s0:s0 + 66].T
    return out


def halo_flags(i):
    b, q = divmod(i, 4)
    f = np.zeros((NT, 2), np.float32)
    for t in range(NTL):
        s0 = 2048 * q + TO * t
        f[t, 0] = 1.0 if s0 > 0 else 0.0
        f[t, 1] = 1.0 if s0 + TO < SEQ else 0.0
    f[NTL, 0] = 1.0 if q > 0 else 0.0
    f[NTL, 1] = 1.0 if q < 3 else 0.0
    return np.ascontiguousarray(np.broadcast_to(f[None], (128, NT, 2)))


def unshard(outs):
    nch = outs[0].shape[0]
    F = nch * 128
    lat = np.empty((NB, SEQ, F), np.float32)
    ctx = np.empty((NB, NCTX, F), np.float32)
    for i, o in enumerate(outs):
        b, q = divmod(i, 4)
        m = o.reshape(F, TCORE)
        lat[b, 2048 * q:2048 * (q + 1)] = m[:, :2048].T
        ctx[b, 64 * q:64 * (q + 1)] = m[:, 2048:].T
    return lat, ctx


def host_L0(c, c_ctx, w_mod, b_mod):
    s = np.stack([c[0], c[1], c_ctx]).astype(np.float32)
    sT = np.ascontiguousarray(s.T.reshape(KC, 128, 3).transpose(1, 0, 2))
    P = build_L0()
    maps = []
    for i in range(NCORE):
        cols = slice(1536 * i, 1536 * (i + 1))
        maps.append({"sT": sT, "w": np.ascontiguousarray(w_mod[:, :, cols]),
                     "bias": np.ascontiguousarray(np.broadcast_to(b_mod[None, :, cols], (3, 4, 1536)))})
    res = run_prog(P, maps)
    return np.concatenate([r["o"] for r in res], axis=2)


def mod_cols(m_all, l, b):
    return [col16(m_all[b, l, k * D:(k + 1) * D]) for k in range(6)]


def rope_tables():
    pos = np.arange(SEQ)
    row = (pos // GRID_W).astype(np.float32)
    col = (pos % GRID_W).astype(np.float32)
    inv = (np.float32(10000.0) ** (-np.arange(0, 64, 2, dtype=np.float32) / np.float32(64))).astype(np.float32)
    ar = row[:, None] * inv[None, :]
    ac = col[:, None] * inv[None, :]
    cr, sr, cc, sc = np.cos(ar), np.sin(ar), np.cos(ac), np.sin(ac)
    COS = np.concatenate([cr, cr, cc, cc], axis=1).T.astype(np.float32)
    SIN = np.concatenate([-sr, sr, -sc, sc], axis=1).T.astype(np.float32)
    return COS, SIN


def a_even_perm():
    idx = []
    sw = np.arange(128) ^ 32
    for h in range(8):
        idx.append(128 * h + np.arange(128))
        idx.append(128 * h + sw)
    for g in range(2):
        idx.append(1024 + 128 * g + np.arange(128))
        idx.append(1024 + 128 * g + sw)
    for g in range(2):
        idx.append(1280 + 128 * g + np.arange(128))
    zc = []
    for j in range(8):
        idx.append(1536 + 128 * j + np.arange(128))
        zc.append(128 * j)
    for j in range(8):
        idx.append(1536 + 1024 + 128 * j + np.arange(128))
        zc.append(1024 + 128 * j)
        idx.append(1536 + 2048 + 128 * j + np.arange(128))
        zc.append(2048 + 128 * j)
    return np.concatenate(idx), zc


def host_A(l, x_lat, x_ctx, m_all, norm_g, ev=None):
    even = ev is not None
    P = build_A(even)
    g0 = col16(norm_g[l, 0])
    mc = mod_cols(m_all, l, 2)
    if even:
        w_in, conv_w, conv_b = ev
        idx, zc = a_even_perm()
        wperm = tile_w(w_in[:, idx], KC, 256)
        cw = np.zeros((128, 24, 4), np.float32)
        for ci, z0 in enumerate(zc):
            cw[:, ci, 0:3] = conv_w[:, z0:z0 + 128].T
            cw[:, ci, 3] = conv_b[z0:z0 + 128]
        COS, SIN = rope_tables()
    maps = []
    for i in range(NCORE):
        b, q = divmod(i, 4)
        ml = mod_cols(m_all, l, b)
        vec = np.ascontiguousarray(np.stack([g0, ml[0], ml[1], mc[0], mc[1]], axis=1))
        m = {"xw": windows(x_lat, x_ctx, i), "vec": vec, "fl": halo_flags(i)}
        if even:
            cs = np.zeros((NT, 128, 2, WW), np.float32)
            cs[NTL, :, 0, :] = 1.0
            cp = np.pad(COS, ((0, 0), (1, 1)))
            sp = np.pad(SIN, ((0, 0), (1, 1)))
            for t in range(NTL):
                s0 = 2048 * q + TO * t
                cs[t, :, 0, :] = cp[:, s0:s0 + WW]
                cs[t, :, 1, :] = sp[:, s0:s0 + WW]
            m.update({"w": wperm, "cs": cs, "cw": cw})
        maps.append(m)
    res = run_prog(P, maps)
    if not even:
        return unshard([r["oh"] for r in res])
    return (unshard([r["oqkv"] for r in res]), unshard([r["ox0"] for r in res]), unshard([r["ovx"] for r in res]))


def build_B(nqb=16, nh=8, ctxu=True):
    P = Prog()
    qT = P.din("qT", [8, 128, 2048])
    qcT = P.din("qcT", [8, 128, 64])
    kT = P.din("kT", [2, 128, 2304])
    vB = P.din("vB", [128, 2, 18, 128])
    kcT = P.din("kcT", [2, 128, 256])
    vC = P.din("vC", [128, 2, 2, 128])
    sink = P.din("sink", [128, 8])
    mask = P.din("mask", [128, 16, 384])
    ident = P.din("ident", [128, 128])
    oA = P.dout("oA", [TCORE, 1024])
    q_sb = P.sb("q_sb", [128, 8, 2048])
    qc_sb = P.sb("qc_sb", [128, 8, 64])
    k_sb = P.sb("k_sb", [128, 2, 2304])
    v_sb = P.sb("v_sb", [128, 2, 18, 128])
    kc_sb = P.sb("kc_sb", [128, 2, 256])
    vc_sb = P.sb("vc_sb", [128, 2, 2, 128])
    sink_sb = P.sb("sink_sb", [128, 8])
    mask_sb = P.sb("mask_sb", [128, 16, 384])
    id_sb = P.sb("id_sb", [128, 128])
    for h in range(8):
        P.dma(q_sb[:, h, :], qT[h], w=[("q", h)])
    P.dma(qc_sb[:], qcT.rearrange("h d t -> d h t"), w=["qc"])
    P.dma(k_sb[:], kT.rearrange("g d t -> d g t"), w=["k"])
    P.dma(v_sb[:], vB[:], w=["v"])
    P.dma(kc_sb[:], kcT.rearrange("g d t -> d g t"), w=["kc"])
    P.dma(vc_sb[:], vC[:], w=["vc"])
    P.dma(sink_sb[:], sink[:], w=["sink"])
    P.dma(mask_sb[:], mask[:], w=["mask"])
    P.dma(id_sb[:], ident[:], w=["id"])
    scale = 128.0 ** -0.5
    cp = [0]

    def unit(qp, ncols_band, lhsT_q, qkeys, g, h, qb, row0):
        nk = ncols_band + 256
        nkb = nk // 128
        sm, sk = P.rot("sm", 2, [128, 640])
        ska, skb = (sk, "a"), (sk, "b")
        psB, kB = P.rot("psB", 2, [128, 512], psum=True)
        if ncols_band:
            psA, kA = P.rot("psA", 2, [128, 512], psum=True)
            P.mm(psA[:qp, :384], lhsT_q, k_sb[:, g, qb * 128:qb * 128 + 384], True, True,
                 r=qkeys + ["k"], w=[kA])
            P.v("scalar_tensor_tensor", out=sm[:qp, :384], in0=psA[:qp, :384], scalar=scale,
                in1=mask_sb[:qp, qb, :], op0=ALU.mult, op1=ALU.add, r=[kA, "mask"], w=[ska])
        P.mm(psB[:qp, :256], lhsT_q, kc_sb[:, g, :], True, True, r=qkeys + ["kc"], w=[kB])
        P.act(sm[:qp, ncols_band:nk], psB[:qp, :256], AF.Copy, scale=scale, r=[kB], w=[skb])
        mx, mk = P.rot("mx", 4, [128, 4])
        P.v("tensor_reduce", out=mx[:qp, 0:1], in_=sm[:qp, :nk], axis=AX.X, op=ALU.max, r=[ska, skb], w=[mk])
        P.v("tensor_tensor", out=mx[:qp, 0:1], in0=mx[:qp, 0:1], in1=sink_sb[:qp, h:h + 1], op=ALU.max,
            r=[mk, "sink"], w=[mk])
        P.v("tensor_scalar", out=mx[:qp, 1:2], in0=mx[:qp, 0:1], scalar1=-1.0, scalar2=None, op0=ALU.mult,
            r=[mk], w=[mk])
        P.act(sm[:qp, :nk], sm[:qp, :nk], AF.Exp, bias=mx[:qp, 1:2], scale=1.0, accum_out=mx[:qp, 2:3],
              r=[ska, skb, mk], w=[ska, skb, mk])
        P.act(mx[:qp, 3:4], sink_sb[:qp, h:h + 1], AF.Exp, bias=mx[:qp, 1:2], scale=1.0, r=["sink", mk], w=[mk])
        P.v("tensor_tensor", out=mx[:qp, 2:3], in0=mx[:qp, 2:3], in1=mx[:qp, 3:4], op=ALU.add, r=[mk], w=[mk])
        P.v("reciprocal", out=mx[:qp, 3:4], in_=mx[:qp, 2:3], r=[mk], w=[mk])
        eT, ek = P.rot("eT", 2, [128, 5, 128])
        for kb in range(nkb):
            pT, pk = P.rot("psT", 2, [128, 512], psum=True)
            P.tr(pT[:, :qp], sm[:qp, kb * 128:(kb + 1) * 128], id_sb[:qp, :qp], r=[ska, skb, "id"], w=[pk])
            cp[0] += 1
            if cp[0] % 2 == 0:
                P.act(eT[:, kb, :qp], pT[:, :qp], AF.Copy, r=[pk], w=[(ek, kb)])
            else:
                P.v("tensor_copy", out=eT[:, kb, :qp], in_=pT[:, :qp], r=[pk], w=[(ek, kb)])
        pO, ok = P.rot("psO", 2, [128, 512], psum=True)
        for kb in range(nkb):
            if ncols_band and kb < 3:
                vv = v_sb[:, g, qb + kb, :]
            else:
                vv = vc_sb[:, g, kb - (3 if ncols_band else 0), :]
            P.mm(pO[:qp, :128], eT[:, kb, :qp], vv, kb == 0, kb == nkb - 1, r=[(ek, kb), "v", "vc"], w=[ok])
        osb, osk = P.rot("osb", 3, [128, 128])
        P.v("tensor_scalar", out=osb[:qp, :], in0=pO[:qp, :128], scalar1=mx[:qp, 3:4], scalar2=None,
            op0=ALU.mult, r=[ok, mk], w=[osk])
        P.store(oA[row0:row0 + qp, h * 128:(h + 1) * 128], osb[:qp, :], r=[osk])

    for qb in range(nqb):
        for h in range(nh):
            unit(128, 384, q_sb[:, h, qb * 128:(qb + 1) * 128], [("q", h)], h // 4, h, qb, qb * 128)
    if ctxu:
        for h in range(nh):
            unit(64, 0, qc_sb[:, h, :], ["qc"], h // 4, h, 0, 2048)
    return P


def host_B(qkv_l, qkv_c, sinkv):
    P = build_B()
    maps = []
    ident = np.eye(128, dtype=np.float32)
    qi = np.arange(128)[:, None]
    kj = np.arange(384)[None, :]
    band = np.abs(kj - 128 - qi) <= 128
    for i in range(NCORE):
        b, q = divmod(i, 4)
        r0 = 2048 * q
        ql = qkv_l[b, r0:r0 + 2048, :1024]
        qT = np.ascontiguousarray(ql.reshape(2048, 8, 128).transpose(1, 2, 0))
        qc = qkv_c[b, 64 * q:64 * q + 64, :1024]
        qcT = np.ascontiguousarray(qc.reshape(64, 8, 128).transpose(1, 2, 0))
        kp = np.pad(qkv_l[b, :, 1024:1280], ((128, 128), (0, 0)))[r0:r0 + 2304]
        kT = np.ascontiguousarray(kp.reshape(2304, 2, 128).transpose(1, 2, 0))
        vp = np.pad(qkv_l[b, :, 1280:1536], ((128, 128), (0, 0)))[r0:r0 + 2304]
        vB = np.ascontiguousarray(vp.reshape(18, 128, 2, 128).transpose(1, 2, 0, 3))
        kcT = np.ascontiguousarray(qkv_c[b, :, 1024:1280].reshape(256, 2, 128).transpose(1, 2, 0))
        vC = np.ascontiguousarray(qkv_c[b, :, 1280:1536].reshape(2, 128, 2, 128).transpose(1, 2, 0, 3))
        mask = np.zeros((128, 16, 384), np.float32)
        for qb in range(16):
            kpos = r0 + 128 * (qb - 1) + kj
            valid = band & (kpos >= 0) & (kpos < SEQ)
            mask[:, qb, :] = np.where(valid, np.float32(0.0), np.float32(-1e30))
        maps.append({"qT": qT, "qcT": qcT, "kT": kT, "vB": vB, "kcT": kcT, "vC": vC,
                     "sink": np.ascontiguousarray(np.broadcast_to(sinkv[None, :], (128, 8))).astype(np.float32),
                     "mask": mask, "ident": ident})
    res = run_prog(P, maps)
    ya_l = np.empty((NB, SEQ, 1024), np.float32)
    ya_c = np.empty((NB, NCTX, 1024), np.float32)
    for i, r in enumerate(res):
        b, q = divmod(i, 4)
        ya_l[b, 2048 * q:2048 * q + 2048] = r["oA"][:2048]
        ya_c[b, 64 * q:64 * q + 64] = r["oA"][2048:]
    return ya_l, ya_c


def build_C():
    P = Prog()
    nc = P.nc
    vx_l = P.din("vx_l", [128, 128, 2, 64])
    x0_l = P.din("x0_l", [128, 128, 2, 64])
    vx_c = P.din("vx_c", [128, 128, 2, 2])
    x0_c = P.din("x0_c", [128, 128, 2, 2])
    zin = {(8192, 0): P.din("zf8", [33, 8192]), (8192, 1): P.din("zr8", [33, 8192]),
           (256, 0): P.din("zf2", [33, 256]), (256, 1): P.din("zr2", [33, 256])}
    din = {(8192, 0): P.din("df8", [128, 8192]), (8192, 1): P.din("dr8", [128, 8192]),
           (256, 0): P.din("df2", [128, 256]), (256, 1): P.din("dr2", [128, 256])}
    w1 = P.din("w1", [33, 64])
    w2 = P.din("w2", [64, 64])
    w3f = P.din("w3f", [64, 128])
    w3b = P.din("w3b", [64, 128])
    fb = P.din("fb", [64, 3])
    hbias = P.din("hbias", [128, 1])
    yb_l = P.dout("yb_l", [128, 128, 2, 64])
    yb_c = P.dout("yb_c", [128, 128, 2, 2])
    kl = {8192: P.dscr("kline8", [128, 16384]), 256: P.dscr("kline2", [128, 512])}
    w1_sb = P.sb("w1_sb", [33, 64])
    w2_sb = P.sb("w2_sb", [64, 64])
    w3_sb = [P.sb("w3f_sb", [64, 128]), P.sb("w3b_sb", [64, 128])]
    fb_sb = P.sb("fb_sb", [64, 3])
    hb_sb = P.sb("hbias_sb", [128, 1])
    sp = P.sb("sp", [64, 3])
    P.dma(w1_sb[:], w1[:], w=["w"])
    P.dma(w2_sb[:], w2[:], w=["w"])
    P.dma(w3_sb[0][:], w3f[:], w=["w"])
    P.dma(w3_sb[1][:], w3b[:], w=["w"])
    P.dma(fb_sb[:], fb[:], w=["fb"])
    P.dma(hb_sb[:], hbias[:], w=["hbias"])
    P.v("tensor_scalar", out=sp[:, 0:1], in0=fb_sb[:, 0:1], scalar1=1.0 / TWO_PI, scalar2=None, op0=ALU.mult,
        r=["fb"], w=["sp"])
    for k in (1, 2):
        P.v("tensor_scalar", out=sp[:, k:k + 1], in0=fb_sb[:, k:k + 1], scalar1=sp[:, 0:1], scalar2=64.0,
            op0=ALU.mult, op1=ALU.add, r=["fb", "sp"], w=["sp"])
    filt = [P.sb("filt_f", [128, 8192]), P.sb("filt_r", [128, 8192])]

    def sinpipe(ps, pkey, L, s2col):
        y, yk = P.rot("sy", 2, [64, 512])
        yi, ik = P.rot("syi", 2, [64, 512], dt=I32)
        f, fk = P.rot("sf", 2, [64, 512])
        h, hk = P.rot("sh", 3, [64, 512])
        P.v("tensor_scalar", out=y[:, :L], in0=ps[:64, :L], scalar1=sp[:, 0:1], scalar2=sp[:, s2col:s2col + 1],
            op0=ALU.mult, op1=ALU.add, r=[pkey, "sp"], w=[yk])
        P.v("tensor_copy", out=yi[:, :L], in_=y[:, :L], r=[yk], w=[ik])
        P.v("scalar_tensor_tensor", out=f[:, :L], in0=yi[:, :L], scalar=-1.0, in1=y[:, :L], op0=ALU.mult,
            op1=ALU.add, r=[ik, yk], w=[fk])
        P.v("scalar_tensor_tensor", out=y[:, :L], in0=f[:, :L], scalar=0.5, in1=f[:, :L], op0=ALU.is_gt,
            op1=ALU.subtract, r=[fk], w=[yk])
        P.act(h[:, :L], y[:, :L], AF.Sin, scale=-TWO_PI, r=[yk], w=[hk])
        return h, hk

    for n in (8192, 256):
        for d in (0, 1):
            for t0 in range(0, n, 512):
                L = min(512, n - t0)
                zs, zk = P.rot("zs", 2, [33, 512])
                P.dma(zs[:, :L], zin[(n, d)][:, t0:t0 + L], w=[zk])
                ps1, k1 = P.rot("fps", 3, [128, 512], psum=True)
                P.mm(ps1[:64, :L], w1_sb[:], zs[:, :L], True, True, r=["w", zk], w=[k1])
                h1, hk1 = sinpipe(ps1, k1, L, 1)
                ps2, k2 = P.rot("fps", 3, [128, 512], psum=True)
                P.mm(ps2[:64, :L], w2_sb[:], h1[:, :L], True, True, r=["w", hk1], w=[k2])
                h2, hk2 = sinpipe(ps2, k2, L, 2)
                ps3, k3 = P.rot("fps", 3, [128, 512], psum=True)
                P.mm(ps3[:, :L], w3_sb[d][:], h2[:, :L], True, True, r=["w", hk2], w=[k3])
                dt_, dk = P.rot("dect", 2, [128, 512])
                P.dma(dt_[:, :L], din[(n, d)][:, t0:t0 + L], w=[dk])
                P.v("tensor_tensor", out=filt[d][:, t0:t0 + L], in0=ps3[:, :L], in1=dt_[:, :L], op=ALU.mult,
                    r=[k3, dk], w=[("filt", d)])
        nrm = P.sb("nrm%d" % n, [128, 4])
        P.v("tensor_reduce", out=nrm[:, 0:1], in_=filt[0][:, :n], axis=AX.X, op=ALU.add, apply_absolute_value=True,
            r=[("filt", 0)], w=["nrm"])
        P.v("tensor_reduce", out=nrm[:, 1:2], in_=filt[1][:, :n - 1], axis=AX.X, op=ALU.add,
            apply_absolute_value=True, r=[("filt", 1)], w=["nrm"])
        P.v("tensor_tensor", out=nrm[:, 2:3], in0=nrm[:, 0:1], in1=nrm[:, 1:2], op=ALU.add, r=["nrm"], w=["nrm"])
        P.v("reciprocal", out=nrm[:, 3:4], in_=nrm[:, 2:3], r=["nrm"], w=["nrm"])
        for d in (0, 1):
            P.v("tensor_scalar", out=filt[d][:, :n], in0=filt[d][:, :n], scalar1=nrm[:, 3:4], scalar2=None,
                op0=ALU.mult, r=[("filt", d), "nrm"], w=[("filt", d)])
        P.v("tensor_tensor", out=filt[0][:, 0:1], in0=filt[0][:, 0:1], in1=hb_sb[:, 0:1], op=ALU.add,
            r=[("filt", 0), "hbias"], w=[("filt", 0)])
        kla = kl[n].ap()
        P.dma(kla[:, 0:n - 1], filt[1][:, 0:n - 1], r=[("filt", 1)], w=[("kl", n, 1)])
        P.dma(kla[:, n - 1:2 * n - 1], filt[0][:, 0:n], r=[("filt", 0)], w=[("kl", n, 0)])

    qn = [0]
    for n, nblk, vx, x0, yb in ((8192, 64, vx_l, x0_l, yb_l), (256, 2, vx_c, x0_c, yb_c)):
        pad = nblk - 1
        vps = [P.sb("vp%d_%d" % (n, i), [128, 2, nblk + 2 * pad]) for i in range(2)]
        zer = P.sb("zer%d" % n, [128, 2, nblk + 2 * pad])
        P.v("memset", ap=zer[:], constant=0.0, w=[("zer", n)], eng="gpsimd")
        for i in range(2):
            P.v("tensor_copy", out=vps[i][:].bitcast(F32R), in_=zer[:], r=[("zer", n)], w=[("vp", n, i)], eng="gpsimd")
        lags = list(range(-pad, pad + 1))
        for c in range(128):
            vp, vk = vps[c % 2], ("vp", n, c % 2)
            vst, vsk = P.rot("vst%d" % n, 2, [128, 2, nblk])
            P.dma(vst[:], vx[:, c], w=[vsk])
            P.v("tensor_copy", out=vp[:, :, pad:pad + nblk].bitcast(F32R), in_=vst[:], r=[vsk], w=[vk], eng="gpsimd")
            x0t, xk = P.rot("x0t%d" % n, 2, [128, 2, nblk])
            P.dma(x0t[:], x0[:, c], w=[xk])
            acc, ak = P.rot("hacc", 2, [128, 512], psum=True)
            accv = acc[:, 0:2 * nblk].rearrange("p (b k) -> p b k", b=2)
            for g0 in range(0, len(lags), 4):
                grp = lags[g0:g0 + 4]
                wd = 128 * len(grp)
                hst, hsk = P.rot("hst", 6, [128, 512])
                hb, hk = P.rot("hb", 4, [128, 512])
                src = bass.AP(kl[n], c * (2 * n) + (n - 1) + 128 * grp[0] - 127, [[1, 128], [1, wd]])
                P.dma(hst[:, :wd], src, r=[("kl", n, 0), ("kl", n, 1)], w=[hsk])
                qn[0] += 1
                if qn[0] % 2:
                    P.act(hb[:, :wd].bitcast(F32R), hst[:, :wd], AF.Copy, r=[hsk], w=[hk])
                else:
                    P.v("tensor_copy", out=hb[:, :wd].bitcast(F32R), in_=hst[:, :wd], r=[hsk], w=[hk])
                for r_, dl in enumerate(grp):
                    P.mm(accv, hb[:, 128 * r_:128 * r_ + 128].bitcast(F32R),
                         vp[:, :, pad - dl:pad - dl + nblk].bitcast(F32R), dl == -pad, dl == pad,
                         r=[hk, vk], w=[ak])
            ysb, yk = P.rot("ysb%d" % n, 2, [128, 2, nblk])
            P.v("tensor_tensor", out=ysb[:], in0=accv, in1=x0t[:], op=ALU.mult, r=[ak, xk], w=[yk])
            P.store(yb[:, c], ysb[:], r=[yk])
    return P


def hyena_tables(n):
    f32 = np.float32
    t = np.linspace(0.0, 1.0, n, dtype=f32)[:, None]
    w = (f32(2.0 * math.pi / n) * np.arange(n, dtype=f32)).astype(f32)
    bands = np.linspace(1e-4, 15, 16, dtype=f32)
    ang = (w[:, None] * bands[None, :]).astype(f32)
    z = np.concatenate([t, np.cos(ang), -np.sin(ang)], axis=-1).astype(f32)
    max_decay = math.log(1e-2) / 0.3
    min_decay = math.log(1e-2) / 1.5
    deltas = np.linspace(min_decay, max_decay, 1024, dtype=f32)
    decay = np.exp(-t * np.abs(deltas)[None, :]).astype(f32)
    return z, decay


def host_C(vx_l, x0_l, vx_c, x0_c, hp):
    f_w1, f_b1, f_w2, f_b2, f_w3, f_freq, hy_bias = hp
    P = build_C()
    z8, d8 = hyena_tables(SEQ)
    z2, d2 = hyena_tables(NCTX)
    fb = np.ascontiguousarray(np.stack([f_freq, f_b1, f_b2], axis=1)).astype(np.float32)
    maps = []

    def lay(a, nblk, rev):
        a = a.reshape(NB, nblk, 128, 128)
        if rev:
            a = a[:, :, ::-1, :]
        return np.ascontiguousarray(a.transpose(2, 3, 0, 1))

    for i in range(NCORE):
        cs = slice(128 * i, 128 * (i + 1))
        maps.append({
            "vx_l": lay(vx_l[:, :, cs], 64, True), "x0_l": lay(x0_l[:, :, cs], 64, False),
            "vx_c": lay(vx_c[:, :, cs], 2, True), "x0_c": lay(x0_c[:, :, cs], 2, False),
            "zf8": np.ascontiguousarray(z8.T), "zr8": np.ascontiguousarray(z8[::-1].T),
            "zf2": np.ascontiguousarray(z2.T), "zr2": np.ascontiguousarray(z2[::-1].T),
            "df8": np.ascontiguousarray(d8[:, cs].T), "dr8": np.ascontiguousarray(d8[::-1, cs].T),
            "df2": np.ascontiguousarray(d2[:, cs].T), "dr2": np.ascontiguousarray(d2[::-1, cs].T),
            "w1": np.ascontiguousarray(f_w1), "w2": np.ascontiguousarray(f_w2),
            "w3f": np.ascontiguousarray(f_w3[:, cs]), "w3b": np.ascontiguousarray(f_w3[:, 1024 + 128 * i:1024 + 128 * (i + 1)]),
            "fb": fb, "hbias": np.ascontiguousarray(hy_bias[cs].reshape(128, 1)),
        })
    res = run_prog(P, maps)
    yb_l = np.empty((NB, SEQ, 1024), np.float32)
    yb_c = np.empty((NB, NCTX, 1024), np.float32)
    for i, r in enumerate(res):
        cs = slice(128 * i, 128 * (i + 1))
        yb_l[:, :, cs] = r["yb_l"].transpose(2, 3, 0, 1).reshape(NB, SEQ, 128)
        yb_c[:, :, cs] = r["yb_c"].transpose(2, 3, 0, 1).reshape(NB, NCTX, 128)
    return yb_l, yb_c


NV = SEQ + NCTX


def build_S5():
    P = Prog()
    u = P.din("u", [2, 32, 8, 2, NV])
    pv = P.din("pv", [128, 16, 3])
    bex = P.din("bex", [128, 16, 2, 32])
    cex = P.din("cex", [128, 16, 2, 32])
    dd = P.din("dd", [32, 8, 32])
    ident = P.din("ident", [128, 128])
    iota = P.din("iota", [128, 512])
    y = P.dout("y", [2, 32, 8, 2, NV])
    pv_sb = P.sb("pv_sb", [128, 16, 3])
    bex_sb = P.sb("bex_sb", [128, 16, 2, 32])
    cex_sb = P.sb("cex_sb", [128, 16, 2, 32])
    dd_sb = P.sb("dd_sb", [32, 8, 32])
    id_sb = P.sb("id_sb", [128, 128])
    io_sb = P.sb("io_sb", [128, 512])
    P.dma(pv_sb[:], pv[:], w=["pv"])
    P.dma(bex_sb[:], bex[:], w=["bex"])
    P.dma(cex_sb[:], cex[:], w=["cex"])
    P.dma(dd_sb[:], dd[:], w=["dd"])
    P.dma(id_sb[:], ident[:], w=["id"])
    P.dma(io_sb[:], iota[:], w=["iota"])
    halfpi = P.sb("halfpi", [128, 1])
    P.v("memset", ap=halfpi[:], constant=math.pi / 2, w=["halfpi"])
    ones = P.sb("ones512", [128, 512])
    P.v("memset", ap=ones[:], constant=1.0, w=["ones"])
    cn = [0]

    def col(name):
        cn[0] += 1
        return P.sb("c_%s_%d" % (name, cn[0]), [128, 16]), ("col", cn[0])

    def tt(out, ok, a, ak, b, bk, op, eng="vector"):
        P.v("tensor_tensor", out=out, in0=a, in1=b, op=op, r=[ak, bk], w=[ok], eng=eng)

    def wrap_turns(dst, dk, src, sk, shape, name):
        ti_, tik = P.rot("wi_" + name, 2, shape, dt=I32)
        f, fk = P.rot("wf_" + name, 2, shape)
        P.v("tensor_copy", out=ti_[:], in_=src, r=[sk], w=[tik])
        P.v("scalar_tensor_tensor", out=f[:], in0=ti_[:], scalar=-1.0, in1=src, op0=ALU.mult, op1=ALU.add,
            r=[tik, sk], w=[fk])
        P.v("scalar_tensor_tensor", out=dst, in0=f[:], scalar=0.5, in1=f[:], op0=ALU.is_gt, op1=ALU.subtract,
            r=[fk], w=[dk])
        P.v("scalar_tensor_tensor", out=dst, in0=dst, scalar=0.5, in1=dst, op0=ALU.is_gt, op1=ALU.subtract,
            r=[dk], w=[dk])

    def sincos(sin_t, sk_, cos_t, ck_, c, ck, shape, name):
        ab, abk = P.rot("ab_" + name, 2, shape)
        P.act(sin_t, c, AF.Sin, scale=TWO_PI, r=[ck], w=[sk_])
        P.v("scalar_tensor_tensor", out=ab[:], in0=c, scalar=-1.0, in1=c, op0=ALU.mult, op1=ALU.max,
            r=[ck], w=[abk])
        P.act(cos_t, ab[:], AF.Sin, scale=-TWO_PI, bias=halfpi[:, 0:1], r=[abk, "halfpi"], w=[ck_])

    lre = pv_sb[:, :, 0]
    lim = pv_sb[:, :, 1]
    dtc, dtk = col("dt")
    P.act(dtc[:], pv_sb[:, :, 2], AF.Exp, r=["pv"], w=[dtk])
    a_, a_k = col("a")
    tt(a_[:], a_k, lre, "pv", dtc[:], dtk, ALU.mult)
    th, thk = col("th")
    tt(th[:], thk, lim, "pv", dtc[:], dtk, ALU.mult)
    rr, rk = col("r")
    P.act(rr[:], a_[:], AF.Exp, r=[a_k], w=[rk])
    ph, phk = col("ph")
    P.v("tensor_scalar", out=ph[:], in0=th[:], scalar1=1.0 / TWO_PI, scalar2=None, op0=ALU.mult, r=[thk], w=[phk])
    phi, phik = col("phi")
    wrap_turns(phi[:], phik, ph[:], phk, [128, 16], "p")
    sn, snk = col("sn")
    cs_, csk = col("cs")
    sincos(sn[:], snk, cs_[:], csk, phi[:], phik, [128, 16], "p")
    nr, nrk = col("nr")
    tt(nr[:], nrk, rr[:], rk, cs_[:], csk, ALU.mult)
    P.v("tensor_scalar", out=nr[:], in0=nr[:], scalar1=-1.0, scalar2=None, op0=ALU.add, r=[nrk], w=[nrk])
    ni, nik = col("ni")
    tt(ni[:], nik, rr[:], rk, sn[:], snk, ALU.mult)
    den, denk = col("den")
    t0_, t0k = col("t0")
    tt(den[:], denk, lre, "pv", lre, "pv", ALU.mult)
    tt(t0_[:], t0k, lim, "pv", lim, "pv", ALU.mult)
    tt(den[:], denk, den[:], denk, t0_[:], t0k, ALU.add)
    inv, invk = col("inv")
    P.v("reciprocal", out=inv[:], in_=den[:], r=[denk], w=[invk])
    wre, wrek = col("wre")
    wim, wimk = col("wim")
    t1_, t1k = col("t1")
    tt(wre[:], wrek, nr[:], nrk, lre, "pv", ALU.mult)
    tt(t1_[:], t1k, ni[:], nik, lim, "pv", ALU.mult)
    tt(wre[:], wrek, wre[:], wrek, t1_[:], t1k, ALU.add)
    tt(wre[:], wrek, wre[:], wrek, inv[:], invk, ALU.mult)
    tt(wim[:], wimk, ni[:], nik, lre, "pv", ALU.mult)
    tt(t1_[:], t1k, nr[:], nrk, lim, "pv", ALU.mult)
    tt(wim[:], wimk, wim[:], wimk, t1_[:], t1k, ALU.subtract)
    tt(wim[:], wimk, wim[:], wimk, inv[:], invk, ALU.mult)
    nwim, nwimk = col("nwim")
    P.v("tensor_scalar", out=nwim[:], in0=wim[:], scalar1=-1.0, scalar2=None, op0=ALU.mult, r=[wimk], w=[nwimk])
    bbar = P.sb("bbar", [128, 16, 2, 32])
    BT = P.sb("BT", [32, 16, 2, 128])
    ncim = P.sb("ncim", [128, 16, 32])
    P.v("tensor_scalar", out=ncim[:], in0=cex_sb[:, :, 1, :], scalar1=-1.0, scalar2=None, op0=ALU.mult,
        r=["cex"], w=["ncim"])
    rT = P.sb("rT", [128, 16, 512])
    psT = P.ps("psT", [128, 512])
    for s in range(16):
        P.v("tensor_scalar", out=bbar[:, s, 0, :], in0=bex_sb[:, s, 0, :], scalar1=wre[:, s:s + 1], scalar2=None,
            op0=ALU.mult, r=["bex", wrek], w=[("bbar", s, 0)])
        P.v("scalar_tensor_tensor", out=bbar[:, s, 0, :], in0=bex_sb[:, s, 1, :], scalar=nwim[:, s:s + 1],
            in1=bbar[:, s, 0, :], op0=ALU.mult, op1=ALU.add, r=["bex", nwimk, ("bbar", s, 0)], w=[("bbar", s, 0)])
        P.v("tensor_scalar", out=bbar[:, s, 1, :], in0=bex_sb[:, s, 1, :], scalar1=wre[:, s:s + 1], scalar2=None,
            op0=ALU.mult, r=["bex", wrek], w=[("bbar", s, 1)])
        P.v("scalar_tensor_tensor", out=bbar[:, s, 1, :], in0=bex_sb[:, s, 0, :], scalar=wim[:, s:s + 1],
            in1=bbar[:, s, 1, :], op0=ALU.mult, op1=ALU.add, r=["bex", wimk, ("bbar", s, 1)], w=[("bbar", s, 1)])
        for ri in range(2):
            P.tr(psT[:32, :128], bbar[:, s, ri, :], id_sb[:], r=[("bbar", s, ri), "id"], w=["psT"])
            P.act(BT[:, s, ri, :], psT[:32, :128], AF.Copy, r=["psT"], w=[("BT", s)])
        P.v("tensor_scalar", out=rT[:, s, :], in0=ones[:], scalar1=rr[:, s:s + 1], scalar2=None, op0=ALU.mult,
            r=["ones", rk], w=[("rT", s)], eng="gpsimd")

    chunks = [(t0, min(512, NV - t0)) for t0 in range(0, NV, 512)]
    sh = [128, 512]
    for dr in range(2):
        for tau in range(8):
            s = dr * 8 + tau
            prev = {0: None, 1: None}
            for (t0, L) in chunks:
                ang, angk = P.rot("ang", 2, sh)
                P.v("tensor_scalar", out=ang[:, :L], in0=io_sb[:, :L], scalar1=float(t0), scalar2=phi[:, s:s + 1],
                    op0=ALU.add, op1=ALU.mult, r=["iota", phik], w=[angk])
                cc, cck = P.rot("cc", 2, sh)
                ti_, tik = P.rot("wi_m", 2, sh, dt=I32)
                f, fk = P.rot("wf_m", 2, sh)
                P.v("tensor_copy", out=ti_[:, :L], in_=ang[:, :L], r=[angk], w=[tik])
                P.v("scalar_tensor_tensor", out=f[:, :L], in0=ti_[:, :L], scalar=-1.0, in1=ang[:, :L], op0=ALU.mult,
                    op1=ALU.add, r=[tik, angk], w=[fk])
                P.v("scalar_tensor_tensor", out=cc[:, :L], in0=f[:, :L], scalar=0.5, in1=f[:, :L], op0=ALU.is_gt,
                    op1=ALU.subtract, r=[fk], w=[cck])
                P.v("scalar_tensor_tensor", out=cc[:, :L], in0=cc[:, :L], scalar=0.5, in1=cc[:, :L], op0=ALU.is_gt,
                    op1=ALU.subtract, r=[cck], w=[cck])
                sinT, sink_ = P.rot("sinT", 2, sh)
                cosT, cosk = P.rot("cosT", 2, sh)
                ab, abk = P.rot("ab_m", 2, sh)
                P.act(sinT[:, :L], cc[:, :L], AF.Sin, scale=TWO_PI, r=[cck], w=[sink_])
                P.v("scalar_tensor_tensor", out=ab[:, :L], in0=cc[:, :L], scalar=-1.0, in1=cc[:, :L], op0=ALU.mult,
                    op1=ALU.max, r=[cck], w=[abk])
                P.act(cosT[:, :L], ab[:, :L], AF.Sin, scale=-TWO_PI, bias=halfpi[:, 0:1], r=[abk, "halfpi"], w=[cosk])
                for b in range(2):
                    ut, uk = P.rot("ut", 3, [32, 512])
                    P.dma(ut[:, :L], u[dr, :, tau, b, t0:t0 + L], w=[uk])
                    pre_, prk = P.rot("pbre", 2, [128, 512], psum=True)
                    pim, pik = P.rot("pbim", 2, [128, 512], psum=True)
                    P.mm(pre_[:, :L], BT[:, s, 0, :], ut[:, :L], True, True, r=[("BT", s), uk], w=[prk])
                    P.mm(pim[:, :L], BT[:, s, 1, :], ut[:, :L], True, True, r=[("BT", s), uk], w=[pik])
                    t1, k1 = P.rot("m1", 2, sh)
                    t2, k2 = P.rot("m2", 2, sh)
                    t3, k3 = P.rot("m3", 2, sh)
                    t4, k4 = P.rot("m4", 2, sh)
                    tt(t1[:, :L], k1, pre_[:, :L], prk, cosT[:, :L], cosk, ALU.mult)
                    tt(t2[:, :L], k2, pim[:, :L], pik, sinT[:, :L], sink_, ALU.mult)
                    tt(t3[:, :L], k3, pim[:, :L], pik, cosT[:, :L], cosk, ALU.mult)
                    tt(t4[:, :L], k4, pre_[:, :L], prk, sinT[:, :L], sink_, ALU.mult)
                    tt(t1[:, :L], k1, t1[:, :L], k1, t2[:, :L], k2, ALU.add, eng="gpsimd")
                    tt(t3[:, :L], k3, t3[:, :L], k3, t4[:, :L], k4, ALU.subtract, eng="gpsimd")
                    sre, srk = P.rot("sre%d" % b, 2, sh)
                    sim, sik = P.rot("sim%d" % b, 2, sh)
                    if prev[b] is None:
                        ire, iim, ikeys = 0.0, 0.0, []
                    else:
                        (pre_t, pre_k, pim_t, pim_k, pL) = prev[b]
                        ire, iim, ikeys = pre_t[:, pL - 1:pL], pim_t[:, pL - 1:pL], [pre_k, pim_k]
                    P.v("tensor_tensor_scan", out=sre[:, :L], data0=rT[:, s, :L], data1=t1[:, :L], initial=ire,
                        op0=ALU.mult, op1=ALU.add, r=[("rT", s), k1] + ikeys, w=[srk])
                    P.v("tensor_tensor_scan", out=sim[:, :L], data0=rT[:, s, :L], data1=t3[:, :L], initial=iim,
                        op0=ALU.mult, op1=ALU.add, r=[("rT", s), k3] + ikeys, w=[sik])
                    prev[b] = (sre, srk, sim, sik, L)
                    d1, dk1 = P.rot("d1", 2, sh)
                    d2, dk2 = P.rot("d2", 2, sh)
                    d3, dk3 = P.rot("d3", 2, sh)
                    d4, dk4 = P.rot("d4", 2, sh)
                    tt(d1[:, :L], dk1, sre[:, :L], srk, cosT[:, :L], cosk, ALU.mult)
                    tt(d2[:, :L], dk2, sim[:, :L], sik, sinT[:, :L], sink_, ALU.mult, eng="gpsimd")
                    tt(d3[:, :L], dk3, sim[:, :L], sik, cosT[:, :L], cosk, ALU.mult)
                    tt(d4[:, :L], dk4, sre[:, :L], srk, sinT[:, :L], sink_, ALU.mult, eng="gpsimd")
                    tt(d1[:, :L], dk1, d1[:, :L], dk1, d2[:, :L], dk2, ALU.subtract, eng="gpsimd")
                    tt(d3[:, :L], dk3, d3[:, :L], dk3, d4[:, :L], dk4, ALU.add, eng="gpsimd")
                    py, pyk = P.rot("psy", 2, [128, 512], psum=True)
                    P.mm(py[:32, :L], cex_sb[:, s, 0, :], d1[:, :L], True, False, r=["cex", dk1], w=[pyk])
                    P.mm(py[:32, :L], ncim[:, s, :], d3[:, :L], False, dr == 1, r=["ncim", dk3], w=[pyk])
                    if dr == 0:
                        P.mm(py[:32, :L], dd_sb[:, tau, :], ut[:, :L], False, True, r=["dd", uk], w=[pyk])
                    ysb, yk = P.rot("ysb", 3, [32, 512])
                    P.act(ysb[:, :L], py[:32, :L], AF.Copy, r=[pyk], w=[yk])
                    P.store(y[dr, :, tau, b, t0:t0 + L], ysb[:, :L], r=[yk])
    return P


def host_S5(hl, hc, sp):
    lam_re, lam_im, log_step, b_re, b_im, c_re, c_im, dvec = sp
    P = build_S5()
    uv = [np.concatenate([hc, hl], axis=1), np.concatenate([hc[:, ::-1], hl[:, ::-1]], axis=1)]
    ident = np.eye(128, dtype=np.float32)
    iota = np.ascontiguousarray(np.broadcast_to(np.arange(512, dtype=np.float32)[None], (128, 512)))
    maps = []
    for i in range(NCORE):
        f0 = 256 * i
        u = np.empty((2, 32, 8, 2, NV), np.float32)
        for dr in range(2):
            u[dr] = uv[dr][:, :, f0:f0 + 256].reshape(NB, NV, 8, 32).transpose(3, 2, 0, 1)
        pv = np.zeros((128, 16, 3), np.float32)
        bex = np.zeros((128, 16, 2, 32), np.float32)
        cex = np.zeros((128, 16, 2, 32), np.float32)
        dd = np.zeros((32, 8, 32), np.float32)
        for dr in range(2):
            for tau in range(8):
                s = dr * 8 + tau
                for g2 in range(2):
                    g = 16 * i + 2 * tau + g2
                    rows = slice(64 * g2, 64 * g2 + 64)
                    cols = slice(16 * g2, 16 * g2 + 16)
                    pv[rows, s, 0] = lam_re[dr, g]
                    pv[rows, s, 1] = lam_im[dr, g]
                    pv[rows, s, 2] = log_step[dr, g]
                    bex[rows, s, 0, cols] = b_re[dr, g]
                    bex[rows, s, 1, cols] = b_im[dr, g]
                    cex[rows, s, 0, cols] = c_re[dr, g].T
                    cex[rows, s, 1, cols] = c_im[dr, g].T
        for tau in range(8):
            dd[np.arange(32), tau, np.arange(32)] = dvec[f0 + 32 * tau:f0 + 32 * tau + 32]
        maps.append({"u": u, "pv": pv, "bex": bex, "cex": cex, "dd": dd, "ident": ident, "iota": iota})
    res = run_prog(P, maps)
    yv = [np.empty((NB, NV, D), np.float32) for _ in range(2)]
    for i, r in enumerate(res):
        f0 = 256 * i
        for dr in range(2):
            yv[dr][:, :, f0:f0 + 256] = r["y"][dr].transpose(2, 3, 1, 0).reshape(NB, NV, 256)
    yf_c, yf_l = yv[0][:, :NCTX], yv[0][:, NCTX:]
    yr_c, yr_l = yv[1][:, :NCTX][:, ::-1], yv[1][:, NCTX:][:, ::-1]
    return (np.ascontiguousarray(yf_l), np.ascontiguousarray(yf_c), np.ascontiguousarray(yr_l), np.ascontiguousarray(yr_c))


def host_D(l, x_lat, x_ctx, y_lat, y_ctx, m_all, norm_g, wa, wup, wdn, fconv_w, fconv_b, y2=None, b_glu=None):
    even = y2 is None
    P = build_D(even)
    g1, g2, g3 = col16(norm_g[l, 1]), col16(norm_g[l, 2]), col16(norm_g[l, 3])
    mc = mod_cols(m_all, l, 2)
    up_idx = []
    fcw = np.zeros((128, 86, 4), np.float32)
    for j in range(43):
        for k, c0 in enumerate((128 * j, DFF + 128 * j)):
            up_idx.append(c0 + np.arange(128))
            fcw[:, 2 * j + k, 0:3] = fconv_w[:, c0:c0 + 128].T
            fcw[:, 2 * j + k, 3] = fconv_b[c0:c0 + 128]
    wup_p = tile_w(wup[:, np.concatenate(up_idx)], KC, 256)
    if even:
        wa_p = tile_w(wa, KC, 256)
        ba = bg = np.zeros((128, KC), np.float32)
    else:
        a_idx = []
        for j in range(16):
            a_idx.append(128 * j + np.arange(128))
            a_idx.append(2048 + 128 * j + np.arange(128))
        wa_p = tile_w(wa[:, np.concatenate(a_idx)], KC, 256)
        ba, bg = col16(b_glu[:2048]), col16(b_glu[2048:])
    wdn_p = tile_w(wdn, 43, 128)
    maps = []
    for i in range(NCORE):
        b, q = divmod(i, 4)
        ml = mod_cols(m_all, l, b)
        vec = np.ascontiguousarray(np.stack([g1, g2, g3, ml[2], ml[3], ml[4], ml[5], mc[2], mc[3], mc[4], mc[5], ba, bg],
                                            axis=1))
        m = {"xw": windows(x_lat, x_ctx, i), "yw": windows(y_lat, y_ctx, i), "vec": vec, "fl": halo_flags(i),
             "fcw": fcw, "wa": wa_p, "wup": wup_p, "wdn": wdn_p}
        if not even:
            m["yw2"] = windows(y2[0], y2[1], i)
        maps.append(m)
    res = run_prog(P, maps)
    return unshard([r["ox"] for r in res])


def kernel(**inputs):
    inp = {k: np.asarray(v, dtype=np.float32) for k, v in inputs.items()}
    ng = inp["norm_g"]
    m_all = host_L0(inp["c"], inp["c_ctx"], inp["w_mod"], inp["b_mod"])
    xl, xc = inp["x"], inp["ctx"]
    for l in range(4):
        j = l // 2
        if l % 2 == 0:
            (qkv_l, qkv_c), (x0_l, x0_c), (vx_l, vx_c) = host_A(
                l, xl, xc, m_all, ng, (inp["ab_w_in"][j], inp["hy_conv_w"][j], inp["hy_conv_b"][j]))
            ya_l, ya_c = host_B(qkv_l, qkv_c, inp["attn_sink"][j])
            yb_l, yb_c = host_C(vx_l, x0_l, vx_c, x0_c,
                                (inp["hy_f_w1"][j], inp["hy_f_b1"][j], inp["hy_f_w2"][j], inp["hy_f_b2"][j],
                                 inp["hy_f_w3"][j], inp["hy_f_freq"][j], inp["hy_bias"][j]))
            yl = np.concatenate([ya_l, yb_l], axis=2)
            yc = np.concatenate([ya_c, yb_c], axis=2)
            xl, xc = host_D(l, xl, xc, yl, yc, m_all, ng, inp["ab_w_out"][j], inp["ffn_w_up"][l],
                            inp["ffn_w_down"][l], inp["ffn_conv_w"][l], inp["ffn_conv_b"][l])
        else:
            hl, hc = host_A(l, xl, xc, m_all, ng)
            yf_l, yf_c, yr_l, yr_c = host_S5(
                hl, hc, (inp["s5_lam_re"][j], inp["s5_lam_im"][j], inp["s5_log_step"][j], inp["s5_b_re"][j],
                         inp["s5_b_im"][j], inp["s5_c_re"][j], inp["s5_c_im"][j], inp["s5_d"][j]))
            xl, xc = host_D(l, xl, xc, yf_l, yf_c, m_all, ng, inp["s5_w_glu"][j], inp["ffn_w_up"][l],
                            inp["ffn_w_down"][l], inp["ffn_conv_w"][l], inp["ffn_conv_b"][l],
                            y2=(yr_l, yr_c), b_glu=inp["s5_b_glu"][j])
    return np.ascontiguousarray(xl.astype(np.float32))
```

```python
import math
import numpy as np
from contextlib import ExitStack
import concourse.bass as bass
import concourse.mybir as mybir
from concourse.bass_utils import run_bass_kernel_spmd

F32 = mybir.dt.float32
F32R = mybir.dt.float32r
I32 = mybir.dt.int32
AF = mybir.ActivationFunctionType
ALU = mybir.AluOpType
AX = mybir.AxisListType

DMA_R = 14
ENGS = ("sync", "scalar", "vector", "gpsimd", "tensor")

D = 2048
KC = 16
SEQ = 8192
NCTX = 256
NB = 2
NCORE = 8
TO = 256
WW = TO + 2
NTL = 8
NT = 9
TCORE = 2112
DFF = 5504
GRID_W = 64
EPS = 1e-6
TWO_PI = 2.0 * math.pi


class Prog:
    def __init__(self):
        self.nc = bass.Bass("TRN2", target_bir_lowering=False)
        self.es = ExitStack()
        self.ops = []
        self.outkeys = []
        self.n = 0
        self.rots = {}

    def din(self, name, shape, dt=F32):
        return self.nc.dram_tensor(name, list(shape), dt, kind="ExternalInput").ap()

    def dout(self, name, shape, dt=F32):
        return self.nc.dram_tensor(name, list(shape), dt, kind="ExternalOutput").ap()

    def dscr(self, name, shape, dt=F32):
        return self.nc.dram_tensor(name, list(shape), dt, kind="Internal")

    def sb(self, name, shape, dt=F32):
        return self.es.enter_context(self.nc.sbuf_tensor(name, list(shape), dt))

    def ps(self, name, shape, dt=F32):
        return self.es.enter_context(self.nc.psum_tensor(name, list(shape), dt))

    def rot(self, name, n, shape, dt=F32, psum=False):
        if name not in self.rots:
            mk = self.ps if psum else self.sb
            self.rots[name] = [[mk("%s_%d" % (name, i), shape, dt) for i in range(n)], 0]
        lst = self.rots[name]
        i = lst[1] % len(lst[0])
        lst[1] += 1
        return lst[0][i], (name, i)

    def op(self, eng, fn, r=(), w=(), dma=False):
        self.ops.append((eng, fn, tuple(r), tuple(w), dma))

    def dma(self, out, in_, r=(), w=(), q="sync", **kw):
        self.op(q, lambda e: e.dma_start(out=out, in_=in_, **kw), r, w, dma=True)

    def store(self, out, in_, r=(), q="sync", **kw):
        self.n += 1
        k = ("__out", self.n)
        self.outkeys.append(k)
        self.dma(out, in_, r=r, w=(k,), q=q, **kw)

    def mm(self, out, lhsT, rhs, start, stop, r=(), w=()):
        self.op("tensor", lambda e: e.matmul(out, lhsT, rhs, start=start, stop=stop), r, w)

    def tr(self, out, in_, ident, r=(), w=()):
        self.op("tensor", lambda e: e.transpose(out, in_, ident), r, w)

    def act(self, out, in_, func, r=(), w=(), **kw):
        self.op("scalar", lambda e: e.activation(out=out, in_=in_, func=func, **kw), r, w)

    def v(self, name, r=(), w=(), eng="vector", **kw):
        self.op(eng, lambda e: getattr(e, name)(**kw), r, w)

    def build(self):
        nc = self.nc
        ops = self.ops
        ops.append(("sync", None, tuple(self.outkeys), (), False))
        N = len(ops)
        last_w = {}
        rd_c = {}
        rd_d = {}
        deps = [None] * N
        for i, (eng, fn, r, w, dma) in enumerate(ops):
            d = set()
            for k in r:
                j = last_w.get(k)
                if j is not None:
                    d.add(j)
            for k in w:
                j = last_w.get(k)
                if j is not None:
                    d.add(j)
                for j in rd_c.get(k, {}).values():
                    d.add(j)
                for j in rd_d.get(k, ()):
                    d.add(j)
            for k in r:
                if dma:
                    rd_d.setdefault(k, []).append(i)
                else:
                    rd_c.setdefault(k, {})[eng] = i
            for k in w:
                last_w[k] = i
                rd_c[k] = {}
                rd_d[k] = []
            d.discard(i)
            deps[i] = d
        waited_c = {e: {} for e in ENGS}
        waited_d = {e: set() for e in ENGS}
        final = [None] * N
        signaled = set()
        for i, (eng, fn, r, w, dma) in enumerate(ops):
            best = {}
            dl = []
            for j in deps[i]:
                je, _, _, _, jd = ops[j]
                if jd:
                    if j not in waited_d[eng]:
                        dl.append(j)
                        waited_d[eng].add(j)
                else:
                    if je == eng and eng == "tensor" and not dma:
                        continue
                    if j > best.get(je, -1):
                        best[je] = j
            cl = []
            for je, j in best.items():
                if waited_c[eng].get(je, -1) >= j:
                    continue
                waited_c[eng][je] = j
                cl.append(j)
                signaled.add(j)
            final[i] = (cl, dl)
        sem = {e: self.es.enter_context(nc.semaphore("s_" + e)) for e in ENGS}
        dq = {}
        for q in ("sync", "scalar", "gpsimd"):
            dq[q] = [self.es.enter_context(nc.semaphore("d_%s_%d" % (q, t))) for t in range(DMA_R)]
        cnt = {e: 0 for e in ENGS}
        dcnt = {q: 0 for q in dq}
        sig = [None] * N
        pre = [None] * N
        for i, (eng, fn, r, w, dma) in enumerate(ops):
            if dma:
                n = dcnt[eng]
                dcnt[eng] += 1
                s = dq[eng][n % DMA_R]
                sig[i] = (s, 16 * (n // DMA_R + 1))
                if n >= DMA_R:
                    pre[i] = (s, 16 * (n // DMA_R))
            elif i in signaled:
                cnt[eng] += 1
                sig[i] = (sem[eng], cnt[eng])
        per = {e: [] for e in ENGS}
        for i, o in enumerate(ops):
            per[o[0]].append(i)
        self.stats = {e: len(per[e]) for e in per}

        def emit(e, name):
            for i in per[name]:
                eng, fn, r, w, dma = ops[i]
                cl, dl = final[i]
                if pre[i] is not None:
                    e.wait_ge(pre[i][0], pre[i][1])
                for j in cl + dl:
                    e.wait_ge(sig[j][0], sig[j][1])
                if fn is None:
                    continue
                ins = fn(e)
                if dma:
                    ins.then_inc(sig[i][0], 16)
                elif sig[i] is not None:
                    ins.then_inc(sig[i][0], 1)

        with nc.Block() as block:
            @block.sync
            def _(e):
                emit(e, "sync")

            @block.scalar
            def _(e):
                emit(e, "scalar")

            @block.vector
            def _(e):
                emit(e, "vector")

            @block.gpsimd
            def _(e):
                emit(e, "gpsimd")

            @block.tensor
            def _(e):
                emit(e, "tensor")
        self.es.close()
        return nc


_N_LAUNCH = [0]


def run_prog(P, in_maps):
    nc = P.build()
    res = run_bass_kernel_spmd(nc, in_maps, core_ids=list(range(NCORE)))
    _N_LAUNCH[0] += 1
    return res.results


class TS:
    def __init__(self, P, wb_elems, sq=None):
        self.P = P
        self.ones = P.sb("ones", [128, 128])
        P.v("memset", ap=self.ones[:], constant=1.0, w=["ones"])
        self.eps = P.sb("eps", [128, 1])
        P.v("memset", ap=self.eps[:], constant=EPS, w=["eps"])
        self.sq = sq if sq is not None else P.sb("sq", [128, KC, WW])
        self.ps_stat = P.ps("ps_stat", [128, 512])
        self.rt = P.sb("rt", [128, WW])
        self.rstd = P.sb("rstd", [128, WW])
        self.wbuf = [P.sb("wbuf%d" % i, [128, wb_elems]) for i in range(2)]
        self.wbr = [P.sb("wbr%d" % i, [128, wb_elems]) for i in range(2)]
        self.hr = P.sb("hr", [128, KC, WW])
        self.pp = [P.ps("pp%d" % i, [128, 512]) for i in range(4)]
        self.wcnt = 0
        self.pcnt = 0

    def rstd_of(self, src, skeys, c0, ncol, sqkeys=("sq",)):
        P = self.P
        sqk = list(sqkeys)
        P.act(self.sq[:, :, :ncol], src[:, :, c0:c0 + ncol], AF.Square, r=skeys, w=sqk)
        for kc in range(KC):
            P.mm(self.ps_stat[:, :ncol], self.ones[:], self.sq[:, kc, :ncol], kc == 0, kc == KC - 1,
                 r=["ones"] + sqk, w=["ps_stat"])
        P.act(self.rt[:, :ncol], self.ps_stat[:, :ncol], AF.Sqrt, bias=self.eps[:, 0:1], scale=1.0 / D,
              r=["ps_stat", "eps"], w=["rt"])
        P.v("reciprocal", out=self.rstd[:, :ncol], in_=self.rt[:, :ncol], r=["rt"], w=["rstd"])

    def gemm(self, w_t, kch, nchunks, sw, rhs, rkeys, ncol, evac):
        P = self.P
        per = sw // 128
        for si, s0 in enumerate(range(0, nchunks, per)):
            b = self.wcnt % 2
            self.wcnt += 1
            wbf = self.wbuf[b][:, 0:kch * sw]
            wrf = self.wbr[b][:, 0:kch * sw]
            P.dma(wbf, w_t[si], w=[("wb", b)])
            if self.wcnt % 2 == 0:
                P.act(wrf.bitcast(F32R), wbf, AF.Copy, r=[("wb", b)], w=[("wr", b)])
            else:
                P.v("tensor_copy", out=wrf.bitcast(F32R), in_=wbf, r=[("wb", b)], w=[("wr", b)], eng="gpsimd")
            wb = wrf.rearrange("p (kc n) -> p kc n", kc=kch)
            for o in range(per):
                oc = s0 + o
                pi = self.pcnt % 4
                self.pcnt += 1
                pp = self.pp[pi]
                for kc in range(kch):
                    P.mm(pp[:, :ncol], wb[:, kc, o * 128:(o + 1) * 128].bitcast(F32R), rhs(kc), kc == 0,
                         kc == kch - 1, r=[("wr", b)] + rkeys(kc), w=[("pp", pi)])
                evac(oc, pp, ("pp", pi))


def tile_w(w, kch, sw):
    ns = w.shape[1] // sw
    return np.ascontiguousarray(w.reshape(kch, 128, ns, sw).transpose(2, 1, 0, 3).reshape(ns, 128, kch * sw))


def tile_info(ti):
    if ti < NTL:
        return 0, TO, TO * ti
    return 1, 64, 2048


def build_L0():
    P = Prog()
    sT = P.din("sT", [128, KC, 3])
    w = P.din("w", [4, D, 1536])
    bias = P.din("bias", [3, 4, 1536])
    o = P.dout("o", [3, 4, 1536])
    s_sb = P.sb("s_sb", [128, KC, 3])
    b_sb = P.sb("b_sb", [3, 4, 1536])
    o_sb = P.sb("o_sb", [3, 4, 1536])
    wb = [P.sb("wb%d" % i, [128, KC, 512]) for i in range(2)]
    pp = [P.ps("pp%d" % i, [128, 512]) for i in range(2)]
    P.dma(s_sb[:], sT[:], w=["s"])
    P.dma(b_sb[:], bias[:], w=["b"])
    P.act(s_sb[:], s_sb[:], AF.Silu, r=["s"], w=["s"])
    it = 0
    for l in range(4):
        for n0 in range(0, 1536, 512):
            b = it % 2
            it += 1
            P.dma(wb[b][:], w[l, :, n0:n0 + 512].rearrange("(kc p) n -> p kc n", p=128), w=[("wb", b)])
            for kc in range(KC):
                P.mm(pp[b][:3, :], s_sb[:, kc, :], wb[b][:, kc, :], kc == 0, kc == KC - 1,
                     r=["s", ("wb", b)], w=[("pp", b)])
            P.v("tensor_tensor", out=o_sb[:, l, n0:n0 + 512], in0=pp[b][:3, :], in1=b_sb[:, l, n0:n0 + 512],
                op=ALU.add, r=[("pp", b), "b"], w=["o"])
    P.store(o[:], o_sb[:], r=["o"])
    return P


NA_EVEN = 46


def norm_mod(P, T, src, skey, dst, dkey, Wt, gs_col, sh_col, r32=False, sqkeys=("sq",)):
    T.rstd_of(src, [(skey, kc) for kc in range(KC)], 0, Wt, sqkeys)
    for kc in range(KC):
        if r32:
            tmp, tk = P.rot("nmtmp", 2, [128, WW])
            P.v("scalar_tensor_tensor", out=tmp[:, :Wt], in0=src[:, kc, :Wt], scalar=gs_col(kc),
                in1=T.rstd[:, :Wt], op0=ALU.mult, op1=ALU.mult, r=[(skey, kc), "rstd", "gs"], w=[tk])
            P.act(T.hr[:, kc, :Wt].bitcast(F32R), tmp[:, :Wt], AF.Identity, bias=sh_col(kc), scale=1.0,
                  r=[tk, "vec"], w=[("hr", kc)])
        else:
            P.v("scalar_tensor_tensor", out=dst[:, kc, :Wt], in0=src[:, kc, :Wt], scalar=gs_col(kc),
                in1=T.rstd[:, :Wt], op0=ALU.mult, op1=ALU.mult,
                r=[(skey, kc), "rstd", "gs"], w=[(dkey, kc)])
            P.act(dst[:, kc, :Wt], dst[:, kc, :Wt], AF.Identity, bias=sh_col(kc), scale=1.0,
                  r=[(dkey, kc), "vec"], w=[(dkey, kc)])


def conv3(P, pp, pkey, Wt, nt, fl_sb, ti, cw_sb, ci):
    acc, ak = P.rot("acc", 2, [128, TO])
    P.act(acc[:, :nt], pp[:, 1:nt + 1], AF.Identity, scale=cw_sb[:, ci, 1:2], bias=cw_sb[:, ci, 3:4],
          r=[pkey, "cw"], w=[ak])
    P.v("scalar_tensor_tensor", out=acc[:, :nt], in0=pp[:, 0:nt], scalar=cw_sb[:, ci, 0:1],
        in1=acc[:, :nt], op0=ALU.mult, op1=ALU.add, r=[pkey, "cw", ak], w=[ak])
    P.v("scalar_tensor_tensor", out=acc[:, :nt], in0=pp[:, 2:nt + 2], scalar=cw_sb[:, ci, 2:3],
        in1=acc[:, :nt], op0=ALU.mult, op1=ALU.add, r=[pkey, "cw", ak], w=[ak])
    return acc, ak


def zero_halo(P, T, fl_sb, ti, nt):
    hk = [("hr", kc) for kc in range(KC)]
    for col, f in ((0, 0), (nt + 1, 1)):
        P.v("tensor_scalar", out=T.hr[:, :, col:col + 1].bitcast(F32R), in0=T.hr[:, :, col:col + 1],
            scalar1=fl_sb[:, ti, f:f + 1], scalar2=None, op0=ALU.mult, r=hk + ["fl"], w=hk)


def build_A(even):
    P = Prog()
    T = TS(P, KC * 256)
    xw = P.din("xw", [NT, D, WW])
    vec = P.din("vec", [128, 5, KC])
    fl = P.din("fl", [128, NT, 2])
    vec_sb = P.sb("vec_sb", [128, 5, KC])
    fl_sb = P.sb("fl_sb", [128, NT, 2])
    gs = P.sb("gs", [128, 2, KC])
    P.dma(vec_sb[:], vec[:], w=["vec"])
    P.dma(fl_sb[:], fl[:], w=["fl"])
    for s in range(2):
        P.v("scalar_tensor_tensor", out=gs[:, s, :], in0=vec_sb[:, 2 + 2 * s, :], scalar=1.0,
            in1=vec_sb[:, 0, :], op0=ALU.add, op1=ALU.mult, r=["vec"], w=["gs"])
    xt = P.sb("xt", [128, KC, WW])
    ht = None if even else P.sb("ht", [128, KC, WW])
    if even:
        w = P.din("w", [NA_EVEN // 2, 128, KC * 256])
        cs = P.din("cs", [NT, 128, 2, WW])
        cw = P.din("cw", [128, 24, 4])
        oqkv = P.dout("oqkv", [12, 128, TCORE])
        ox0 = P.dout("ox0", [8, 128, TCORE])
        ovx = P.dout("ovx", [8, 128, TCORE])
        cw_sb = P.sb("cw_sb", [128, 24, 4])
        P.dma(cw_sb[:], cw[:], w=["cw"])
        cs_sb = P.sb("cs_sb", [128, 2, WW])
        hold = P.sb("hold", [128, WW])
        hold2 = P.sb("hold2", [128, TO])
    else:
        oh = P.dout("oh", [KC, 128, TCORE])
    for ti in range(NT):
        s, nt, c0 = tile_info(ti)
        Wt = nt + 2
        P.dma(xt[:, :, :], xw[ti].rearrange("(kc p) w -> p kc w", p=128), w=[("xt", kc) for kc in range(KC)])
        norm_mod(P, T, xt, "xt", ht, "ht", Wt,
                 lambda kc, s=s: gs[:, s, kc:kc + 1], lambda kc, s=s: vec_sb[:, 1 + 2 * s, kc:kc + 1], r32=even)
        if not even:
            P.store(oh[:, :, c0:c0 + nt].rearrange("kc p t -> p kc t"), ht[:, :, 1:nt + 1],
                    r=[("ht", kc) for kc in range(KC)])
            continue
        zero_halo(P, T, fl_sb, ti, nt)
        P.dma(cs_sb[:], cs[ti], w=["cs"])

        def evac(oc, pp, pkey, ti=ti, nt=nt, Wt=Wt, c0=c0):
            if oc < 20:
                if oc % 2 == 0:
                    P.v("tensor_tensor", out=hold[:, :Wt], in0=pp[:, :Wt], in1=cs_sb[:, 0, :Wt], op=ALU.mult,
                        r=[pkey, "cs"], w=["hold"])
                else:
                    osb, ok = P.rot("osb", 3, [128, WW])
                    P.v("tensor_tensor", out=osb[:, :Wt], in0=pp[:, :Wt], in1=cs_sb[:, 1, :Wt], op=ALU.mult,
                        r=[pkey, "cs"], w=[ok])
                    P.v("tensor_tensor", out=osb[:, :Wt], in0=osb[:, :Wt], in1=hold[:, :Wt], op=ALU.add,
                        r=[ok, "hold"], w=[ok], eng="gpsimd")
                    P.store(oqkv[oc // 2, :, c0:c0 + nt], osb[:, 1:nt + 1], r=[ok])
            elif oc < 22:
                osb, ok = P.rot("osb", 3, [128, WW])
                P.act(osb[:, :Wt], pp[:, :Wt], AF.Copy, r=[pkey], w=[ok])
                P.store(oqkv[10 + oc - 20, :, c0:c0 + nt], osb[:, 1:nt + 1], r=[ok])
            elif oc < 30:
                acc, ak = conv3(P, pp, pkey, Wt, nt, fl_sb, ti, cw_sb, oc - 22)
                P.store(ox0[oc - 22, :, c0:c0 + nt], acc[:, :nt], r=[ak])
            else:
                j = (oc - 30) // 2
                acc, ak = conv3(P, pp, pkey, Wt, nt, fl_sb, ti, cw_sb, oc - 22)
                if (oc - 30) % 2 == 0:
                    P.v("tensor_copy", out=hold2[:, :nt], in_=acc[:, :nt], r=[ak], w=["hold2"], eng="gpsimd")
                else:
                    P.v("tensor_tensor", out=acc[:, :nt], in0=acc[:, :nt], in1=hold2[:, :nt], op=ALU.mult,
                        r=[ak, "hold2"], w=[ak], eng="gpsimd")
                    P.store(ovx[j, :, c0:c0 + nt], acc[:, :nt], r=[ak])

        T.gemm(w, KC, NA_EVEN, 256, lambda kc, Wt=Wt: T.hr[:, kc, :Wt].bitcast(F32R), lambda kc: [("hr", kc)], Wt,
               evac)
    return P


def build_D(even):
    P = Prog()
    yt = P.sb("yt", [128, KC, WW])
    T = TS(P, 43 * 128, sq=yt)
    YK = [("yt", kc) for kc in range(KC)]
    xw = P.din("xw", [NT, D, WW])
    yw = P.din("yw", [NT, D, WW])
    if not even:
        yw2 = P.din("yw2", [NT, D, WW])
    vec = P.din("vec", [128, 13, KC])
    fl = P.din("fl", [128, NT, 2])
    fcw = P.din("fcw", [128, 86, 4])
    wa = P.din("wa", [8 if even else 16, 128, KC * 256])
    wup = P.din("wup", [43, 128, KC * 256])
    wdn = P.din("wdn", [16, 128, 43 * 128])
    ox = P.dout("ox", [KC, 128, TCORE])
    vec_sb = P.sb("vec_sb", [128, 13, KC])
    fl_sb = P.sb("fl_sb", [128, NT, 2])
    fcw_sb = P.sb("fcw_sb", [128, 86, 4])
    P.dma(vec_sb[:], vec[:], w=["vec"])
    P.dma(fl_sb[:], fl[:], w=["fl"])
    P.dma(fcw_sb[:], fcw[:], w=["cw"])
    mg1 = P.sb("mg1", [128, 2, KC])
    gs2 = P.sb("gs2", [128, 2, KC])
    mg3 = P.sb("mg3", [128, 2, KC])
    for s in range(2):
        P.v("tensor_tensor", out=mg1[:, s, :], in0=vec_sb[:, 3 + 4 * s, :], in1=vec_sb[:, 0, :], op=ALU.mult,
            r=["vec"], w=["gs"])
        P.v("scalar_tensor_tensor", out=gs2[:, s, :], in0=vec_sb[:, 5 + 4 * s, :], scalar=1.0,
            in1=vec_sb[:, 1, :], op0=ALU.add, op1=ALU.mult, r=["vec"], w=["gs"])
        P.v("tensor_tensor", out=mg3[:, s, :], in0=vec_sb[:, 6 + 4 * s, :], in1=vec_sb[:, 2, :], op=ALU.mult,
            r=["vec"], w=["gs"])
    xt = P.sb("xt", [128, KC, WW])
    tt = P.sb("tt", [128, KC, WW])
    a_sb = P.sb("a_sb", [128, 43, TO])
    hold = P.sb("hold", [128, WW])
    allk = lambda nm: [(nm, kc) for kc in range(KC)]
    for ti in range(NT):
        s, nt, c0 = tile_info(ti)
        Wt = nt + 2
        P.dma(xt[:, :, :], xw[ti].rearrange("(kc p) w -> p kc w", p=128), w=allk("xt"))
        P.dma(yt[:, :, :], yw[ti].rearrange("(kc p) w -> p kc w", p=128), w=allk("yt"))
        if even:
            for kc in range(KC):
                P.v("tensor_copy", out=T.hr[:, kc, :Wt].bitcast(F32R), in_=yt[:, kc, :Wt], r=[("yt", kc)],
                    w=[("hr", kc)], eng="gpsimd")
        if not even:
            P.dma(tt[:, :, :], yw2[ti].rearrange("(kc p) w -> p kc w", p=128), w=allk("tt"))
            P.v("tensor_tensor", out=yt[:, :, :Wt], in0=yt[:, :, :Wt], in1=tt[:, :, :Wt], op=ALU.add,
                r=allk("yt") + allk("tt"), w=allk("yt"), eng="gpsimd")
            gc = 2.0 * math.sqrt(2.0 / math.pi)
            for kc in range(KC):
                g1_, gk1 = P.rot("sg", 1, [128, WW])
                P.act(g1_[:, :Wt], yt[:, kc, :Wt], AF.Square, r=[("yt", kc)], w=[gk1])
                P.v("tensor_scalar", out=g1_[:, :Wt], in0=g1_[:, :Wt], scalar1=0.044715, scalar2=1.0,
                    op0=ALU.mult, op1=ALU.add, r=[gk1], w=[gk1])
                P.v("tensor_tensor", out=g1_[:, :Wt], in0=g1_[:, :Wt], in1=yt[:, kc, :Wt], op=ALU.mult,
                    r=[gk1, ("yt", kc)], w=[gk1])
                P.act(g1_[:, :Wt], g1_[:, :Wt], AF.Sigmoid, scale=gc, r=[gk1], w=[gk1])
                P.v("tensor_tensor", out=T.hr[:, kc, :Wt].bitcast(F32R), in0=yt[:, kc, :Wt], in1=g1_[:, :Wt],
                    op=ALU.mult, r=[gk1, ("yt", kc)], w=[("hr", kc)])

        def evac_a(oc, pp, pkey, Wt=Wt):
            if even:
                P.act(tt[:, oc, :Wt], pp[:, :Wt], AF.Copy, r=[pkey], w=[("tt", oc)])
            else:
                j = oc // 2
                if oc % 2 == 0:
                    P.act(hold[:, :Wt], pp[:, :Wt], AF.Identity, bias=vec_sb[:, 11, j:j + 1], scale=1.0,
                          r=[pkey, "vec"], w=["hold"])
                else:
                    sg, sk = P.rot("sg", 1, [128, WW])
                    P.act(sg[:, :Wt], pp[:, :Wt], AF.Sigmoid, bias=vec_sb[:, 12, j:j + 1], scale=1.0,
                          r=[pkey, "vec"], w=[sk])
                    P.v("tensor_tensor", out=tt[:, j, :Wt], in0=hold[:, :Wt], in1=sg[:, :Wt], op=ALU.mult,
                        r=["hold", sk], w=[("tt", j)])

        T.gemm(wa, KC, 16 if even else 32, 256, lambda kc, Wt=Wt: T.hr[:, kc, :Wt].bitcast(F32R),
               lambda kc: [("hr", kc)], Wt, evac_a)
        T.rstd_of(tt, allk("tt"), 0, Wt, YK)
        for kc in range(KC):
            P.v("scalar_tensor_tensor", out=tt[:, kc, :Wt], in0=tt[:, kc, :Wt], scalar=mg1[:, s, kc:kc + 1],
                in1=T.rstd[:, :Wt], op0=ALU.mult, op1=ALU.mult, r=[("tt", kc), "rstd", "gs"], w=[("tt", kc)])
            P.v("tensor_tensor", out=xt[:, kc, :Wt], in0=xt[:, kc, :Wt], in1=tt[:, kc, :Wt], op=ALU.add,
                r=[("xt", kc), ("tt", kc)], w=[("xt", kc)], eng="gpsimd")
        norm_mod(P, T, xt, "xt", None, None, Wt,
                 lambda kc, s=s: gs2[:, s, kc:kc + 1], lambda kc, s=s: vec_sb[:, 4 + 4 * s, kc:kc + 1], r32=True,
                 sqkeys=YK)
        zero_halo(P, T, fl_sb, ti, nt)

        def evac_up(oc, pp, pkey, ti=ti, nt=nt, Wt=Wt):
            j = oc // 2
            acc, ak = conv3(P, pp, pkey, Wt, nt, fl_sb, ti, fcw_sb, oc)
            if oc % 2 == 0:
                P.act(hold[:, :nt], acc[:, :nt], AF.Silu, r=[ak], w=["hold"])
            else:
                P.v("tensor_tensor", out=a_sb[:, j, :nt].bitcast(F32R), in0=acc[:, :nt], in1=hold[:, :nt],
                    op=ALU.mult, r=[ak, "hold"], w=[("a", j)], eng="gpsimd")

        T.gemm(wup, KC, 86, 256, lambda kc, Wt=Wt: T.hr[:, kc, :Wt].bitcast(F32R), lambda kc: [("hr", kc)], Wt,
               evac_up)

        def evac_dn(oc, pp, pkey, nt=nt):
            P.act(tt[:, oc, :nt], pp[:, :nt], AF.Copy, r=[pkey], w=[("tt", oc)])

        T.gemm(wdn, 43, KC, 128, lambda kc, nt=nt: a_sb[:, kc, :nt].bitcast(F32R), lambda kc: [("a", kc)], nt,
               evac_dn)
        T.rstd_of(tt, allk("tt"), 0, nt, YK)
        for kc in range(KC):
            P.v("scalar_tensor_tensor", out=tt[:, kc, :nt], in0=tt[:, kc, :nt], scalar=mg3[:, s, kc:kc + 1],
                in1=T.rstd[:, :nt], op0=ALU.mult, op1=ALU.mult, r=[("tt", kc), "rstd", "gs"], w=[("tt", kc)])
            P.v("tensor_tensor", out=tt[:, kc, :nt], in0=tt[:, kc, :nt], in1=xt[:, kc, 1:nt + 1], op=ALU.add,
                r=[("xt", kc), ("tt", kc)], w=[("tt", kc)], eng="gpsimd")
        P.store(ox[:, :, c0:c0 + nt].rearrange("kc p t -> p kc t"), tt[:, :, :nt], r=allk("tt"))
    return P


def col16(vv):
    return np.ascontiguousarray(np.asarray(vv, np.float32).reshape(-1, 128).T)


def windows(lat, ctx, i):
    b, q = divmod(i, 4)
    F = lat.shape[-1]
    out = np.zeros((NT, F, WW), np.float32)
    lp = np.pad(lat[b], ((1, 1), (0, 0)))
    for t in range(NTL):
        s0 = 2048 * q + TO * t
        out[t] = lp[s0:s0 + WW].T
    cp = np.pad(ctx[b], ((1, 1), (0, 0)))
    s0 = 64 * q
    out[NTL, :, :66] = cp[s0:s0 + 66].T
    return out


def halo_flags(i):
    b, q = divmod(i, 4)
    f = np.zeros((NT, 2), np.float32)
    for t in range(NTL):
        s0 = 2048 * q + TO * t
        f[t, 0] = 1.0 if s0 > 0 else 0.0
        f[t, 1] = 1.0 if s0 + TO < SEQ else 0.0
    f[NTL, 0] = 1.0 if q > 0 else 0.0
    f[NTL, 1] = 1.0 if q < 3 else 0.0
    return np.ascontiguousarray(np.broadcast_to(f[None], (128, NT, 2)))


def unshard(outs):
    nch = outs[0].shape[0]
    F = nch * 128
    lat = np.empty((NB, SEQ, F), np.float32)
    ctx = np.empty((NB, NCTX, F), np.float32)
    for i, o in enumerate(outs):
        b, q = divmod(i, 4)
        m = o.reshape(F, TCORE)
        lat[b, 2048 * q:2048 * (q + 1)] = m[:, :2048].T
        ctx[b, 64 * q:64 * (q + 1)] = m[:, 2048:].T
    return lat, ctx


def host_L0(c, c_ctx, w_mod, b_mod):
    s = np.stack([c[0], c[1], c_ctx]).astype(np.float32)
    sT = np.ascontiguousarray(s.T.reshape(KC, 128, 3).transpose(1, 0, 2))
    P = build_L0()
    maps = []
    for i in range(NCORE):
        cols = slice(1536 * i, 1536 * (i + 1))
        maps.append({"sT": sT, "w": np.ascontiguousarray(w_mod[:, :, cols]),
                     "bias": np.ascontiguousarray(np.broadcast_to(b_mod[None, :, cols], (3, 4, 1536)))})
    res = run_prog(P, maps)
    return np.concatenate([r["o"] for r in res], axis=2)


def mod_cols(m_all, l, b):
    return [col16(m_all[b, l, k * D:(k + 1) * D]) for k in range(6)]


def rope_tables():
    pos = np.arange(SEQ)
    row = (pos // GRID_W).astype(np.float32)
    col = (pos % GRID_W).astype(np.float32)
    inv = (np.float32(10000.0) ** (-np.arange(0, 64, 2, dtype=np.float32) / np.float32(64))).astype(np.float32)
    ar = row[:, None] * inv[None, :]
    ac = col[:, None] * inv[None, :]
    cr, sr, cc, sc = np.cos(ar), np.sin(ar), np.cos(ac), np.sin(ac)
    COS = np.concatenate([cr, cr, cc, cc], axis=1).T.astype(np.float32)
    SIN = np.concatenate([-sr, sr, -sc, sc], axis=1).T.astype(np.float32)
    return COS, SIN


def a_even_perm():
    idx = []
    sw = np.arange(128) ^ 32
    for h in range(8):
        idx.append(128 * h + np.arange(128))
        idx.append(128 * h + sw)
    for g in range(2):
        idx.append(1024 + 128 * g + np.arange(128))
        idx.append(1024 + 128 * g + sw)
    for g in range(2):
        idx.append(1280 + 128 * g + np.arange(128))
    zc = []
    for j in range(8):
        idx.append(1536 + 128 * j + np.arange(128))
        zc.append(128 * j)
    for j in range(8):
        idx.append(1536 + 1024 + 128 * j + np.arange(128))
        zc.append(1024 + 128 * j)
        idx.append(1536 + 2048 + 128 * j + np.arange(128))
        zc.append(2048 + 128 * j)
    return np.concatenate(idx), zc


def host_A(l, x_lat, x_ctx, m_all, norm_g, ev=None):
    even = ev is not None
    P = build_A(even)
    g0 = col16(norm_g[l, 0])
    mc = mod_cols(m_all, l, 2)
    if even:
        w_in, conv_w, conv_b = ev
        idx, zc = a_even_perm()
        wperm = tile_w(w_in[:, idx], KC, 256)
        cw = np.zeros((128, 24, 4), np.float32)
        for ci, z0 in enumerate(zc):
            cw[:, ci, 0:3] = conv_w[:, z0:z0 + 128].T
            cw[:, ci, 3] = conv_b[z0:z0 + 128]
        COS, SIN = rope_tables()
    maps = []
    for i in range(NCORE):
        b, q = divmod(i, 4)
        ml = mod_cols(m_all, l, b)
        vec = np.ascontiguousarray(np.stack([g0, ml[0], ml[1], mc[0], mc[1]], axis=1))
        m = {"xw": windows(x_lat, x_ctx, i), "vec": vec, "fl": halo_flags(i)}
        if even:
            cs = np.zeros((NT, 128, 2, WW), np.float32)
            cs[NTL, :, 0, :] = 1.0
            cp = np.pad(COS, ((0, 0), (1, 1)))
            sp = np.pad(SIN, ((0, 0), (1, 1)))
            for t in range(NTL):
                s0 = 2048 * q + TO * t
                cs[t, :, 0, :] = cp[:, s0:s0 + WW]
                cs[t, :, 1, :] = sp[:, s0:s0 + WW]
            m.update({"w": wperm, "cs": cs, "cw": cw})
        maps.append(m)
    res = run_prog(P, maps)
    if not even:
        return unshard([r["oh"] for r in res])
    return (unshard([r["oqkv"] for r in res]), unshard([r["ox0"] for r in res]), unshard([r["ovx"] for r in res]))


def build_B(nqb=16, nh=8, ctxu=True):
    P = Prog()
    qT = P.din("qT", [8, 128, 2048])
    qcT = P.din("qcT", [8, 128, 64])
    kT = P.din("kT", [2, 128, 2304])
    vB = P.din("vB", [128, 2, 18, 128])
    kcT = P.din("kcT", [2, 128, 256])
    vC = P.din("vC", [128, 2, 2, 128])
    sink = P.din("sink", [128, 8])
    mask = P.din("mask", [128, 16, 384])
    ident = P.din("ident", [128, 128])
    oA = P.dout("oA", [TCORE, 1024])
    q_sb = P.sb("q_sb", [128, 8, 2048])
    qc_sb = P.sb("qc_sb", [128, 8, 64])
    k_sb = P.sb("k_sb", [128, 2, 2304])
    v_sb = P.sb("v_sb", [128, 2, 18, 128])
    kc_sb = P.sb("kc_sb", [128, 2, 256])
    vc_sb = P.sb("vc_sb", [128, 2, 2, 128])
    sink_sb = P.sb("sink_sb", [128, 8])
    mask_sb = P.sb("mask_sb", [128, 16, 384])
    id_sb = P.sb("id_sb", [128, 128])
    for h in range(8):
        P.dma(q_sb[:, h, :], qT[h], w=[("q", h)])
    P.dma(qc_sb[:], qcT.rearrange("h d t -> d h t"), w=["qc"])
    P.dma(k_sb[:], kT.rearrange("g d t -> d g t"), w=["k"])
    P.dma(v_sb[:], vB[:], w=["v"])
    P.dma(kc_sb[:], kcT.rearrange("g d t -> d g t"), w=["kc"])
    P.dma(vc_sb[:], vC[:], w=["vc"])
    P.dma(sink_sb[:], sink[:], w=["sink"])
    P.dma(mask_sb[:], mask[:], w=["mask"])
    P.dma(id_sb[:], ident[:], w=["id"])
    scale = 128.0 ** -0.5
    cp = [0]

    def unit(qp, ncols_band, lhsT_q, qkeys, g, h, qb, row0):
        nk = ncols_band + 256
        nkb = nk // 128
        sm, sk = P.rot("sm", 2, [128, 640])
        ska, skb = (sk, "a"), (sk, "b")
        psB, kB = P.rot("psB", 2, [128, 512], psum=True)
        if ncols_band:
            psA, kA = P.rot("psA", 2, [128, 512], psum=True)
            P.mm(psA[:qp, :384], lhsT_q, k_sb[:, g, qb * 128:qb * 128 + 384], True, True,
                 r=qkeys + ["k"], w=[kA])
            P.v("scalar_tensor_tensor", out=sm[:qp, :384], in0=psA[:qp, :384], scalar=scale,
                in1=mask_sb[:qp, qb, :], op0=ALU.mult, op1=ALU.add, r=[kA, "mask"], w=[ska])
        P.mm(psB[:qp, :256], lhsT_q, kc_sb[:, g, :], True, True, r=qkeys + ["kc"], w=[kB])
        P.act(sm[:qp, ncols_band:nk], psB[:qp, :256], AF.Copy, scale=scale, r=[kB], w=[skb])
        mx, mk = P.rot("mx", 4, [128, 4])
        P.v("tensor_reduce", out=mx[:qp, 0:1], in_=sm[:qp, :nk], axis=AX.X, op=ALU.max, r=[ska, skb], w=[mk])
        P.v("tensor_tensor", out=mx[:qp, 0:1], in0=mx[:qp, 0:1], in1=sink_sb[:qp, h:h + 1], op=ALU.max,
            r=[mk, "sink"], w=[mk])
        P.v("tensor_scalar", out=mx[:qp, 1:2], in0=mx[:qp, 0:1], scalar1=-1.0, scalar2=None, op0=ALU.mult,
            r=[mk], w=[mk])
        P.act(sm[:qp, :nk], sm[:qp, :nk], AF.Exp, bias=mx[:qp, 1:2], scale=1.0, accum_out=mx[:qp, 2:3],
              r=[ska, skb, mk], w=[ska, skb, mk])
        P.act(mx[:qp, 3:4], sink_sb[:qp, h:h + 1], AF.Exp, bias=mx[:qp, 1:2], scale=1.0, r=["sink", mk], w=[mk])
        P.v("tensor_tensor", out=mx[:qp, 2:3], in0=mx[:qp, 2:3], in1=mx[:qp, 3:4], op=ALU.add, r=[mk], w=[mk])
        P.v("reciprocal", out=mx[:qp, 3:4], in_=mx[:qp, 2:3], r=[mk], w=[mk])
        eT, ek = P.rot("eT", 2, [128, 5, 128])
        for kb in range(nkb):
            pT, pk = P.rot("psT", 2, [128, 512], psum=True)
            P.tr(pT[:, :qp], sm[:qp, kb * 128:(kb + 1) * 128], id_sb[:qp, :qp], r=[ska, skb, "id"], w=[pk])
            cp[0] += 1
            if cp[0] % 2 == 0:
                P.act(eT[:, kb, :qp], pT[:, :qp], AF.Copy, r=[pk], w=[(ek, kb)])
            else:
                P.v("tensor_copy", out=eT[:, kb, :qp], in_=pT[:, :qp], r=[pk], w=[(ek, kb)])
        pO, ok = P.rot("psO", 2, [128, 512], psum=True)
        for kb in range(nkb):
            if ncols_band and kb < 3:
                vv = v_sb[:, g, qb + kb, :]
            else:
                vv = vc_sb[:, g, kb - (3 if ncols_band else 0), :]
            P.mm(pO[:qp, :128], eT[:, kb, :qp], vv, kb == 0, kb == nkb - 1, r=[(ek, kb), "v", "vc"], w=[ok])
        osb, osk = P.rot("osb", 3, [128, 128])
        P.v("tensor_scalar", out=osb[:qp, :], in0=pO[:qp, :128], scalar1=mx[:qp, 3:4], scalar2=None,
            op0=ALU.mult, r=[ok, mk], w=[osk])
        P.store(oA[row0:row0 + qp, h * 128:(h + 1) * 128], osb[:qp, :], r=[osk])

    for qb in range(nqb):
        for h in range(nh):
            unit(128, 384, q_sb[:, h, qb * 128:(qb + 1) * 128], [("q", h)], h // 4, h, qb, qb * 128)
    if ctxu:
        for h in range(nh):
            unit(64, 0, qc_sb[:, h, :], ["qc"], h // 4, h, 0, 2048)
    return P


def host_B(qkv_l, qkv_c, sinkv):
    P = build_B()
    maps = []
    ident = np.eye(128, dtype=np.float32)
    qi = np.arange(128)[:, None]
    kj = np.arange(384)[None, :]
    band = np.abs(kj - 128 - qi) <= 128
    for i in range(NCORE):
        b, q = divmod(i, 4)
        r0 = 2048 * q
        ql = qkv_l[b, r0:r0 + 2048, :1024]
        qT = np.ascontiguousarray(ql.reshape(2048, 8, 128).transpose(1, 2, 0))
        qc = qkv_c[b, 64 * q:64 * q + 64, :1024]
        qcT = np.ascontiguousarray(qc.reshape(64, 8, 128).transpose(1, 2, 0))
        kp = np.pad(qkv_l[b, :, 1024:1280], ((128, 128), (0, 0)))[r0:r0 + 2304]
        kT = np.ascontiguousarray(kp.reshape(2304, 2, 128).transpose(1, 2, 0))
        vp = np.pad(qkv_l[b, :, 1280:1536], ((128, 128), (0, 0)))[r0:r0 + 2304]
        vB = np.ascontiguousarray(vp.reshape(18, 128, 2, 128).transpose(1, 2, 0, 3))
        kcT = np.ascontiguousarray(qkv_c[b, :, 1024:1280].reshape(256, 2, 128).transpose(1, 2, 0))
        vC = np.ascontiguousarray(qkv_c[b, :, 1280:1536].reshape(2, 128, 2, 128).transpose(1, 2, 0, 3))
        mask = np.zeros((128, 16, 384), np.float32)
        for qb in range(16):
            kpos = r0 + 128 * (qb - 1) + kj
            valid = band & (kpos >= 0) & (kpos < SEQ)
            mask[:, qb, :] = np.where(valid, np.float32(0.0), np.float32(-1e30))
        maps.append({"qT": qT, "qcT": qcT, "kT": kT, "vB": vB, "kcT": kcT, "vC": vC,
                     "sink": np.ascontiguousarray(np.broadcast_to(sinkv[None, :], (128, 8))).astype(np.float32),
                     "mask": mask, "ident": ident})
    res = run_prog(P, maps)
    ya_l = np.empty((NB, SEQ, 1024), np.float32)
    ya_c = np.empty((NB, NCTX, 1024), np.float32)
    for i, r in enumerate(res):
        b, q = divmod(i, 4)
        ya_l[b, 2048 * q:2048 * q + 2048] = r["oA"][:2048]
        ya_c[b, 64 * q:64 * q + 64] = r["oA"][2048:]
    return ya_l, ya_c


def build_C():
    P = Prog()
    nc = P.nc
    vx_l = P.din("vx_l", [128, 128, 2, 64])
    x0_l = P.din("x0_l", [128, 128, 2, 64])
    vx_c = P.din("vx_c", [128, 128, 2, 2])
    x0_c = P.din("x0_c", [128, 128, 2, 2])
    zin = {(8192, 0): P.din("zf8", [33, 8192]), (8192, 1): P.din("zr8", [33, 8192]),
           (256, 0): P.din("zf2", [33, 256]), (256, 1): P.din("zr2", [33, 256])}
    din = {(8192, 0): P.din("df8", [128, 8192]), (8192, 1): P.din("dr8", [128, 8192]),
           (256, 0): P.din("df2", [128, 256]), (256, 1): P.din("dr2", [128, 256])}
    w1 = P.din("w1", [33, 64])
    w2 = P.din("w2", [64, 64])
    w3f = P.din("w3f", [64, 128])
    w3b = P.din("w3b", [64, 128])
    fb = P.din("fb", [64, 3])
    hbias = P.din("hbias", [128, 1])
    yb_l = P.dout("yb_l", [128, 128, 2, 64])
    yb_c = P.dout("yb_c", [128, 128, 2, 2])
    kl = {8192: P.dscr("kline8", [128, 16384]), 256: P.dscr("kline2", [128, 512])}
    w1_sb = P.sb("w1_sb", [33, 64])
    w2_sb = P.sb("w2_sb", [64, 64])
    w3_sb = [P.sb("w3f_sb", [64, 128]), P.sb("w3b_sb", [64, 128])]
    fb_sb = P.sb("fb_sb", [64, 3])
    hb_sb = P.sb("hbias_sb", [128, 1])
    sp = P.sb("sp", [64, 3])
    P.dma(w1_sb[:], w1[:], w=["w"])
    P.dma(w2_sb[:], w2[:], w=["w"])
    P.dma(w3_sb[0][:], w3f[:], w=["w"])
    P.dma(w3_sb[1][:], w3b[:], w=["w"])
    P.dma(fb_sb[:], fb[:], w=["fb"])
    P.dma(hb_sb[:], hbias[:], w=["hbias"])
    P.v("tensor_scalar", out=sp[:, 0:1], in0=fb_sb[:, 0:1], scalar1=1.0 / TWO_PI, scalar2=None, op0=ALU.mult,
        r=["fb"], w=["sp"])
    for k in (1, 2):
        P.v("tensor_scalar", out=sp[:, k:k + 1], in0=fb_sb[:, k:k + 1], scalar1=sp[:, 0:1], scalar2=64.0,
            op0=ALU.mult, op1=ALU.add, r=["fb", "sp"], w=["sp"])
    filt = [P.sb("filt_f", [128, 8192]), P.sb("filt_r", [128, 8192])]

    def sinpipe(ps, pkey, L, s2col):
        y, yk = P.rot("sy", 2, [64, 512])
        yi, ik = P.rot("syi", 2, [64, 512], dt=I32)
        f, fk = P.rot("sf", 2, [64, 512])
        h, hk = P.rot("sh", 3, [64, 512])
        P.v("tensor_scalar", out=y[:, :L], in0=ps[:64, :L], scalar1=sp[:, 0:1], scalar2=sp[:, s2col:s2col + 1],
            op0=ALU.mult, op1=ALU.add, r=[pkey, "sp"], w=[yk])
        P.v("tensor_copy", out=yi[:, :L], in_=y[:, :L], r=[yk], w=[ik])
        P.v("scalar_tensor_tensor", out=f[:, :L], in0=yi[:, :L], scalar=-1.0, in1=y[:, :L], op0=ALU.mult,
            op1=ALU.add, r=[ik, yk], w=[fk])
        P.v("scalar_tensor_tensor", out=y[:, :L], in0=f[:, :L], scalar=0.5, in1=f[:, :L], op0=ALU.is_gt,
            op1=ALU.subtract, r=[fk], w=[yk])
        P.act(h[:, :L], y[:, :L], AF.Sin, scale=-TWO_PI, r=[yk], w=[hk])
        return h, hk

    for n in (8192, 256):
        for d in (0, 1):
            for t0 in range(0, n, 512):
                L = min(512, n - t0)
                zs, zk = P.rot("zs", 2, [33, 512])
                P.dma(zs[:, :L], zin[(n, d)][:, t0:t0 + L], w=[zk])
                ps1, k1 = P.rot("fps", 3, [128, 512], psum=True)
                P.mm(ps1[:64, :L], w1_sb[:], zs[:, :L], True, True, r=["w", zk], w=[k1])
                h1, hk1 = sinpipe(ps1, k1, L, 1)
                ps2, k2 = P.rot("fps", 3, [128, 512], psum=True)
                P.mm(ps2[:64, :L], w2_sb[:], h1[:, :L], True, True, r=["w", hk1], w=[k2])
                h2, hk2 = sinpipe(ps2, k2, L, 2)
                ps3, k3 = P.rot("fps", 3, [128, 512], psum=True)
                P.mm(ps3[:, :L], w3_sb[d][:], h2[:, :L], True, True, r=["w", hk2], w=[k3])
                dt_, dk = P.rot("dect", 2, [128, 512])
                P.dma(dt_[:, :L], din[(n, d)][:, t0:t0 + L], w=[dk])
                P.v("tensor_tensor", out=filt[d][:, t0:t0 + L], in0=ps3[:, :L], in1=dt_[:, :L], op=ALU.mult,
                    r=[k3, dk], w=[("filt", d)])
        nrm = P.sb("nrm%d" % n, [128, 4])
        P.v("tensor_reduce", out=nrm[:, 0:1], in_=filt[0][:, :n], axis=AX.X, op=ALU.add, apply_absolute_value=True,
            r=[("filt", 0)], w=["nrm"])
        P.v("tensor_reduce", out=nrm[:, 1:2], in_=filt[1][:, :n - 1], axis=AX.X, op=ALU.add,
            apply_absolute_value=True, r=[("filt", 1)], w=["nrm"])
        P.v("tensor_tensor", out=nrm[:, 2:3], in0=nrm[:, 0:1], in1=nrm[:, 1:2], op=ALU.add, r=["nrm"], w=["nrm"])
        P.v("reciprocal", out=nrm[:, 3:4], in_=nrm[:, 2:3], r=["nrm"], w=["nrm"])
        for d in (0, 1):
            P.v("tensor_scalar", out=filt[d][:, :n], in0=filt[d][:, :n], scalar1=nrm[:, 3:4], scalar2=None,
                op0=ALU.mult, r=[("filt", d), "nrm"], w=[("filt", d)])
        P.v("tensor_tensor", out=filt[0][:, 0:1], in0=filt[0][:, 0:1], in1=hb_sb[:, 0:1], op=ALU.add,
            r=[("filt", 0), "hbias"], w=[("filt", 0)])
        kla = kl[n].ap()
        P.dma(kla[:, 0:n - 1], filt[1][:, 0:n - 1], r=[("filt", 1)], w=[("kl", n, 1)])
        P.dma(kla[:, n - 1:2 * n - 1], filt[0][:, 0:n], r=[("filt", 0)], w=[("kl", n, 0)])

    qn = [0]
    for n, nblk, vx, x0, yb in ((8192, 64, vx_l, x0_l, yb_l), (256, 2, vx_c, x0_c, yb_c)):
        pad = nblk - 1
        vps = [P.sb("vp%d_%d" % (n, i), [128, 2, nblk + 2 * pad]) for i in range(2)]
        zer = P.sb("zer%d" % n, [128, 2, nblk + 2 * pad])
        P.v("memset", ap=zer[:], constant=0.0, w=[("zer", n)], eng="gpsimd")
        for i in range(2):
            P.v("tensor_copy", out=vps[i][:].bitcast(F32R), in_=zer[:], r=[("zer", n)], w=[("vp", n, i)], eng="gpsimd")
        lags = list(range(-pad, pad + 1))
        for c in range(128):
            vp, vk = vps[c % 2], ("vp", n, c % 2)
            vst, vsk = P.rot("vst%d" % n, 2, [128, 2, nblk])
            P.dma(vst[:], vx[:, c], w=[vsk])
            P.v("tensor_copy", out=vp[:, :, pad:pad + nblk].bitcast(F32R), in_=vst[:], r=[vsk], w=[vk], eng="gpsimd")
            x0t, xk = P.rot("x0t%d" % n, 2, [128, 2, nblk])
            P.dma(x0t[:], x0[:, c], w=[xk])
            acc, ak = P.rot("hacc", 2, [128, 512], psum=True)
            accv = acc[:, 0:2 * nblk].rearrange("p (b k) -> p b k", b=2)
            for g0 in range(0, len(lags), 4):
                grp = lags[g0:g0 + 4]
                wd = 128 * len(grp)
                hst, hsk = P.rot("hst", 6, [128, 512])
                hb, hk = P.rot("hb", 4, [128, 512])
                src = bass.AP(kl[n], c * (2 * n) + (n - 1) + 128 * grp[0] - 127, [[1, 128], [1, wd]])
                P.dma(hst[:, :wd], src, r=[("kl", n, 0), ("kl", n, 1)], w=[hsk])
                qn[0] += 1
                if qn[0] % 2:
                    P.act(hb[:, :wd].bitcast(F32R), hst[:, :wd], AF.Copy, r=[hsk], w=[hk])
                else:
                    P.v("tensor_copy", out=hb[:, :wd].bitcast(F32R), in_=hst[:, :wd], r=[hsk], w=[hk])
                for r_, dl in enumerate(grp):
                    P.mm(accv, hb[:, 128 * r_:128 * r_ + 128].bitcast(F32R),
                         vp[:, :, pad - dl:pad - dl + nblk].bitcast(F32R), dl == -pad, dl == pad,
                         r=[hk, vk], w=[ak])
            ysb, yk = P.rot("ysb%d" % n, 2, [128, 2, nblk])
            P.v("tensor_tensor", out=ysb[:], in0=accv, in1=x0t[:], op=ALU.mult, r=[ak, xk], w=[yk])
            P.store(yb[:, c], ysb[:], r=[yk])
    return P


def hyena_tables(n):
    f32 = np.float32
    t = np.linspace(0.0, 1.0, n, dtype=f32)[:, None]
    w = (f32(2.0 * math.pi / n) * np.arange(n, dtype=f32)).astype(f32)
    bands = np.linspace(1e-4, 15, 16, dtype=f32)
    ang = (w[:, None] * bands[None, :]).astype(f32)
    z = np.concatenate([t, np.cos(ang), -np.sin(ang)], axis=-1).astype(f32)
    max_decay = math.log(1e-2) / 0.3
    min_decay = math.log(1e-2) / 1.5
    deltas = np.linspace(min_decay, max_decay, 1024, dtype=f32)
    decay = np.exp(-t * np.abs(deltas)[None, :]).astype(f32)
    return z, decay


def host_C(vx_l, x0_l, vx_c, x0_c, hp):
    f_w1, f_b1, f_w2, f_b2, f_w3, f_freq, hy_bias = hp
    P = build_C()
    z8, d8 = hyena_tables(SEQ)
    z2, d2 = hyena_tables(NCTX)
    fb = np.ascontiguousarray(np.stack([f_freq, f_b1, f_b2], axis=1)).astype(np.float32)
    maps = []

    def lay(a, nblk, rev):
        a = a.reshape(NB, nblk, 128, 128)
        if rev:
            a = a[:, :, ::-1, :]
        return np.ascontiguousarray(a.transpose(2, 3, 0, 1))

    for i in range(NCORE):
        cs = slice(128 * i, 128 * (i + 1))
        maps.append({
            "vx_l": lay(vx_l[:, :, cs], 64, True), "x0_l": lay(x0_l[:, :, cs], 64, False),
            "vx_c": lay(vx_c[:, :, cs], 2, True), "x0_c": lay(x0_c[:, :, cs], 2, False),
            "zf8": np.ascontiguousarray(z8.T), "zr8": np.ascontiguousarray(z8[::-1].T),
            "zf2": np.ascontiguousarray(z2.T), "zr2": np.ascontiguousarray(z2[::-1].T),
            "df8": np.ascontiguousarray(d8[:, cs].T), "dr8": np.ascontiguousarray(d8[::-1, cs].T),
            "df2": np.ascontiguousarray(d2[:, cs].T), "dr2": np.ascontiguousarray(d2[::-1, cs].T),
            "w1": np.ascontiguousarray(f_w1), "w2": np.ascontiguousarray(f_w2),
            "w3f": np.ascontiguousarray(f_w3[:, cs]), "w3b": np.ascontiguousarray(f_w3[:, 1024 + 128 * i:1024 + 128 * (i + 1)]),
            "fb": fb, "hbias": np.ascontiguousarray(hy_bias[cs].reshape(128, 1)),
        })
    res = run_prog(P, maps)
    yb_l = np.empty((NB, SEQ, 1024), np.float32)
    yb_c = np.empty((NB, NCTX, 1024), np.float32)
    for i, r in enumerate(res):
        cs = slice(128 * i, 128 * (i + 1))
        yb_l[:, :, cs] = r["yb_l"].transpose(2, 3, 0, 1).reshape(NB, SEQ, 128)
        yb_c[:, :, cs] = r["yb_c"].transpose(2, 3, 0, 1).reshape(NB, NCTX, 128)
    return yb_l, yb_c


NV = SEQ + NCTX


def build_S5():
    P = Prog()
    u = P.din("u", [2, 32, 8, 2, NV])
    pv = P.din("pv", [128, 16, 3])
    bex = P.din("bex", [128, 16, 2, 32])
    cex = P.din("cex", [128, 16, 2, 32])
    dd = P.din("dd", [32, 8, 32])
    ident = P.din("ident", [128, 128])
    iota = P.din("iota", [128, 512])
    y = P.dout("y", [2, 32, 8, 2, NV])
    pv_sb = P.sb("pv_sb", [128, 16, 3])
    bex_sb = P.sb("bex_sb", [128, 16, 2, 32])
    cex_sb = P.sb("cex_sb", [128, 16, 2, 32])
    dd_sb = P.sb("dd_sb", [32, 8, 32])
    id_sb = P.sb("id_sb", [128, 128])
    io_sb = P.sb("io_sb", [128, 512])
    P.dma(pv_sb[:], pv[:], w=["pv"])
    P.dma(bex_sb[:], bex[:], w=["bex"])
    P.dma(cex_sb[:], cex[:], w=["cex"])
    P.dma(dd_sb[:], dd[:], w=["dd"])
    P.dma(id_sb[:], ident[:], w=["id"])
    P.dma(io_sb[:], iota[:], w=["iota"])
    halfpi = P.sb("halfpi", [128, 1])
    P.v("memset", ap=halfpi[:], constant=math.pi / 2, w=["halfpi"])
    ones = P.sb("ones512", [128, 512])
    P.v("memset", ap=ones[:], constant=1.0, w=["ones"])
    cn = [0]

    def col(name):
        cn[0] += 1
        return P.sb("c_%s_%d" % (name, cn[0]), [128, 16]), ("col", cn[0])

    def tt(out, ok, a, ak, b, bk, op, eng="vector"):
        P.v("tensor_tensor", out=out, in0=a, in1=b, op=op, r=[ak, bk], w=[ok], eng=eng)

    def wrap_turns(dst, dk, src, sk, shape, name):
        ti_, tik = P.rot("wi_" + name, 2, shape, dt=I32)
        f, fk = P.rot("wf_" + name, 2, shape)
        P.v("tensor_copy", out=ti_[:], in_=src, r=[sk], w=[tik])
        P.v("scalar_tensor_tensor", out=f[:], in0=ti_[:], scalar=-1.0, in1=src, op0=ALU.mult, op1=ALU.add,
            r=[tik, sk], w=[fk])
        P.v("scalar_tensor_tensor", out=dst, in0=f[:], scalar=0.5, in1=f[:], op0=ALU.is_gt, op1=ALU.subtract,
            r=[fk], w=[dk])
        P.v("scalar_tensor_tensor", out=dst, in0=dst, scalar=0.5, in1=dst, op0=ALU.is_gt, op1=ALU.subtract,
            r=[dk], w=[dk])

    def sincos(sin_t, sk_, cos_t, ck_, c, ck, shape, name):
        ab, abk = P.rot("ab_" + name, 2, shape)
        P.act(sin_t, c, AF.Sin, scale=TWO_PI, r=[ck], w=[sk_])
        P.v("scalar_tensor_tensor", out=ab[:], in0=c, scalar=-1.0, in1=c, op0=ALU.mult, op1=ALU.max,
            r=[ck], w=[abk])
        P.act(cos_t, ab[:], AF.Sin, scale=-TWO_PI, bias=halfpi[:, 0:1], r=[abk, "halfpi"], w=[ck_])

    lre = pv_sb[:, :, 0]
    lim = pv_sb[:, :, 1]
    dtc, dtk = col("dt")
    P.act(dtc[:], pv_sb[:, :, 2], AF.Exp, r=["pv"], w=[dtk])
    a_, a_k = col("a")
    tt(a_[:], a_k, lre, "pv", dtc[:], dtk, ALU.mult)
    th, thk = col("th")
    tt(th[:], thk, lim, "pv", dtc[:], dtk, ALU.mult)
    rr, rk = col("r")
    P.act(rr[:], a_[:], AF.Exp, r=[a_k], w=[rk])
    ph, phk = col("ph")
    P.v("tensor_scalar", out=ph[:], in0=th[:], scalar1=1.0 / TWO_PI, scalar2=None, op0=ALU.mult, r=[thk], w=[phk])
    phi, phik = col("phi")
    wrap_turns(phi[:], phik, ph[:], phk, [128, 16], "p")
    sn, snk = col("sn")
    cs_, csk = col("cs")
    sincos(sn[:], snk, cs_[:], csk, phi[:], phik, [128, 16], "p")
    nr, nrk = col("nr")
    tt(nr[:], nrk, rr[:], rk, cs_[:], csk, ALU.mult)
    P.v("tensor_scalar", out=nr[:], in0=nr[:], scalar1=-1.0, scalar2=None, op0=ALU.add, r=[nrk], w=[nrk])
    ni, nik = col("ni")
    tt(ni[:], nik, rr[:], rk, sn[:], snk, ALU.mult)
    den, denk = col("den")
    t0_, t0k = col("t0")
    tt(den[:], denk, lre, "pv", lre, "pv", ALU.mult)
    tt(t0_[:], t0k, lim, "pv", lim, "pv", ALU.mult)
    tt(den[:], denk, den[:], denk, t0_[:], t0k, ALU.add)
    inv, invk = col("inv")
    P.v("reciprocal", out=inv[:], in_=den[:], r=[denk], w=[invk])
    wre, wrek = col("wre")
    wim, wimk = col("wim")
    t1_, t1k = col("t1")
    tt(wre[:], wrek, nr[:], nrk, lre, "pv", ALU.mult)
    tt(t1_[:], t1k, ni[:], nik, lim, "pv", ALU.mult)
    tt(wre[:], wrek, wre[:], wrek, t1_[:], t1k, ALU.add)
    tt(wre[:], wrek, wre[:], wrek, inv[:], invk, ALU.mult)
    tt(wim[:], wimk, ni[:], nik, lre, "pv", ALU.mult)
    tt(t1_[:], t1k, nr[:], nrk, lim, "pv", ALU.mult)
    tt(wim[:], wimk, wim[:], wimk, t1_[:], t1k, ALU.subtract)
    tt(wim[:], wimk, wim[:], wimk, inv[:], invk, ALU.mult)
    nwim, nwimk = col("nwim")
    P.v("tensor_scalar", out=nwim[:], in0=wim[:], scalar1=-1.0, scalar2=None, op0=ALU.mult, r=[wimk], w=[nwimk])
    bbar = P.sb("bbar", [128, 16, 2, 32])
    BT = P.sb("BT", [32, 16, 2, 128])
    ncim = P.sb("ncim", [128, 16, 32])
    P.v("tensor_scalar", out=ncim[:], in0=cex_sb[:, :, 1, :], scalar1=-1.0, scalar2=None, op0=ALU.mult,
        r=["cex"], w=["ncim"])
    rT = P.sb("rT", [128, 16, 512])
    psT = P.ps("psT", [128, 512])
    for s in range(16):
        P.v("tensor_scalar", out=bbar[:, s, 0, :], in0=bex_sb[:, s, 0, :], scalar1=wre[:, s:s + 1], scalar2=None,
            op0=ALU.mult, r=["bex", wrek], w=[("bbar", s, 0)])
        P.v("scalar_tensor_tensor", out=bbar[:, s, 0, :], in0=bex_sb[:, s, 1, :], scalar=nwim[:, s:s + 1],
            in1=bbar[:, s, 0, :], op0=ALU.mult, op1=ALU.add, r=["bex", nwimk, ("bbar", s, 0)], w=[("bbar", s, 0)])
        P.v("tensor_scalar", out=bbar[:, s, 1, :], in0=bex_sb[:, s, 1, :], scalar1=wre[:, s:s + 1], scalar2=None,
            op0=ALU.mult, r=["bex", wrek], w=[("bbar", s, 1)])
        P.v("scalar_tensor_tensor", out=bbar[:, s, 1, :], in0=bex_sb[:, s, 0, :], scalar=wim[:, s:s + 1],
            in1=bbar[:, s, 1, :], op0=ALU.mult, op1=ALU.add, r=["bex", wimk, ("bbar", s, 1)], w=[("bbar", s, 1)])
        for ri in range(2):
            P.tr(psT[:32, :128], bbar[:, s, ri, :], id_sb[:], r=[("bbar", s, ri), "id"], w=["psT"])
            P.act(BT[:, s, ri, :], psT[:32, :128], AF.Copy, r=["psT"], w=[("BT", s)])
        P.v("tensor_scalar", out=rT[:, s, :], in0=ones[:], scalar1=rr[:, s:s + 1], scalar2=None, op0=ALU.mult,
            r=["ones", rk], w=[("rT", s)], eng="gpsimd")

    chunks = [(t0, min(512, NV - t0)) for t0 in range(0, NV, 512)]
    sh = [128, 512]
    for dr in range(2):
        for tau in range(8):
            s = dr * 8 + tau
            prev = {0: None, 1: None}
            for (t0, L) in chunks:
                ang, angk = P.rot("ang", 2, sh)
                P.v("tensor_scalar", out=ang[:, :L], in0=io_sb[:, :L], scalar1=float(t0), scalar2=phi[:, s:s + 1],
                    op0=ALU.add, op1=ALU.mult, r=["iota", phik], w=[angk])
                cc, cck = P.rot("cc", 2, sh)
                ti_, tik = P.rot("wi_m", 2, sh, dt=I32)
                f, fk = P.rot("wf_m", 2, sh)
                P.v("tensor_copy", out=ti_[:, :L], in_=ang[:, :L], r=[angk], w=[tik])
                P.v("scalar_tensor_tensor", out=f[:, :L], in0=ti_[:, :L], scalar=-1.0, in1=ang[:, :L], op0=ALU.mult,
                    op1=ALU.add, r=[tik, angk], w=[fk])
                P.v("scalar_tensor_tensor", out=cc[:, :L], in0=f[:, :L], scalar=0.5, in1=f[:, :L], op0=ALU.is_gt,
                    op1=ALU.subtract, r=[fk], w=[cck])
                P.v("scalar_tensor_tensor", out=cc[:, :L], in0=cc[:, :L], scalar=0.5, in1=cc[:, :L], op0=ALU.is_gt,
                    op1=ALU.subtract, r=[cck], w=[cck])
                sinT, sink_ = P.rot("sinT", 2, sh)
                cosT, cosk = P.rot("cosT", 2, sh)
                ab, abk = P.rot("ab_m", 2, sh)
                P.act(sinT[:, :L], cc[:, :L], AF.Sin, scale=TWO_PI, r=[cck], w=[sink_])
                P.v("scalar_tensor_tensor", out=ab[:, :L], in0=cc[:, :L], scalar=-1.0, in1=cc[:, :L], op0=ALU.mult,
                    op1=ALU.max, r=[cck], w=[abk])
                P.act(cosT[:, :L], ab[:, :L], AF.Sin, scale=-TWO_PI, bias=halfpi[:, 0:1], r=[abk, "halfpi"], w=[cosk])
                for b in range(2):
                    ut, uk = P.rot("ut", 3, [32, 512])
                    P.dma(ut[:, :L], u[dr, :, tau, b, t0:t0 + L], w=[uk])
                    pre_, prk = P.rot("pbre", 2, [128, 512], psum=True)
                    pim, pik = P.rot("pbim", 2, [128, 512], psum=True)
                    P.mm(pre_[:, :L], BT[:, s, 0, :], ut[:, :L], True, True, r=[("BT", s), uk], w=[prk])
                    P.mm(pim[:, :L], BT[:, s, 1, :], ut[:, :L], True, True, r=[("BT", s), uk], w=[pik])
                    t1, k1 = P.rot("m1", 2, sh)
                    t2, k2 = P.rot("m2", 2, sh)
                    t3, k3 = P.rot("m3", 2, sh)
                    t4, k4 = P.rot("m4", 2, sh)
                    tt(t1[:, :L], k1, pre_[:, :L], prk, cosT[:, :L], cosk, ALU.mult)
                    tt(t2[:, :L], k2, pim[:, :L], pik, sinT[:, :L], sink_, ALU.mult)
                    tt(t3[:, :L], k3, pim[:, :L], pik, cosT[:, :L], cosk, ALU.mult)
                    tt(t4[:, :L], k4, pre_[:, :L], prk, sinT[:, :L], sink_, ALU.mult)
                    tt(t1[:, :L], k1, t1[:, :L], k1, t2[:, :L], k2, ALU.add, eng="gpsimd")
                    tt(t3[:, :L], k3, t3[:, :L], k3, t4[:, :L], k4, ALU.subtract, eng="gpsimd")
                    sre, srk = P.rot("sre%d" % b, 2, sh)
                    sim, sik = P.rot("sim%d" % b, 2, sh)
                    if prev[b] is None:
                        ire, iim, ikeys = 0.0, 0.0, []
                    else:
                        (pre_t, pre_k, pim_t, pim_k, pL) = prev[b]
                        ire, iim, ikeys = pre_t[:, pL - 1:pL], pim_t[:, pL - 1:pL], [pre_k, pim_k]
                    P.v("tensor_tensor_scan", out=sre[:, :L], data0=rT[:, s, :L], data1=t1[:, :L], initial=ire,
                        op0=ALU.mult, op1=ALU.add, r=[("rT", s), k1] + ikeys, w=[srk])
                    P.v("tensor_tensor_scan", out=sim[:, :L], data0=rT[:, s, :L], data1=t3[:, :L], initial=iim,
                        op0=ALU.mult, op1=ALU.add, r=[("rT", s), k3] + ikeys, w=[sik])
                    prev[b] = (sre, srk, sim, sik, L)
                    d1, dk1 = P.rot("d1", 2, sh)
                    d2, dk2 = P.rot("d2", 2, sh)
                    d3, dk3 = P.rot("d3", 2, sh)
                    d4, dk4 = P.rot("d4", 2, sh)
                    tt(d1[:, :L], dk1, sre[:, :L], srk, cosT[:, :L], cosk, ALU.mult)
                    tt(d2[:, :L], dk2, sim[:, :L], sik, sinT[:, :L], sink_, ALU.mult, eng="gpsimd")
                    tt(d3[:, :L], dk3, sim[:, :L], sik, cosT[:, :L], cosk, ALU.mult)
                    tt(d4[:, :L], dk4, sre[:, :L], srk, sinT[:, :L], sink_, ALU.mult, eng="gpsimd")
                    tt(d1[:, :L], dk1, d1[:, :L], dk1, d2[:, :L], dk2, ALU.subtract, eng="gpsimd")
                    tt(d3[:, :L], dk3, d3[:, :L], dk3, d4[:, :L], dk4, ALU.add, eng="gpsimd")
                    py, pyk = P.rot("psy", 2, [128, 512], psum=True)
                    P.mm(py[:32, :L], cex_sb[:, s, 0, :], d1[:, :L], True, False, r=["cex", dk1], w=[pyk])
                    P.mm(py[:32, :L], ncim[:, s, :], d3[:, :L], False, dr == 1, r=["ncim", dk3], w=[pyk])
                    if dr == 0:
                        P.mm(py[:32, :L], dd_sb[:, tau, :], ut[:, :L], False, True, r=["dd", uk], w=[pyk])
                    ysb, yk = P.rot("ysb", 3, [32, 512])
                    P.act(ysb[:, :L], py[:32, :L], AF.Copy, r=[pyk], w=[yk])
                    P.store(y[dr, :, tau, b, t0:t0 + L], ysb[:, :L], r=[yk])
    return P


def host_S5(hl, hc, sp):
    lam_re, lam_im, log_step, b_re, b_im, c_re, c_im, dvec = sp
    P = build_S5()
    uv = [np.concatenate([hc, hl], axis=1), np.concatenate([hc[:, ::-1], hl[:, ::-1]], axis=1)]
    ident = np.eye(128, dtype=np.float32)
    iota = np.ascontiguousarray(np.broadcast_to(np.arange(512, dtype=np.float32)[None], (128, 512)))
    maps = []
    for i in range(NCORE):
        f0 = 256 * i
        u = np.empty((2, 32, 8, 2, NV), np.float32)
        for dr in range(2):
            u[dr] = uv[dr][:, :, f0:f0 + 256].reshape(NB, NV, 8, 32).transpose(3, 2, 0, 1)
        pv = np.zeros((128, 16, 3), np.float32)
        bex = np.zeros((128, 16, 2, 32), np.float32)
        cex = np.zeros((128, 16, 2, 32), np.float32)
        dd = np.zeros((32, 8, 32), np.float32)
        for dr in range(2):
            for tau in range(8):
                s = dr * 8 + tau
                for g2 in range(2):
                    g = 16 * i + 2 * tau + g2
                    rows = slice(64 * g2, 64 * g2 + 64)
                    cols = slice(16 * g2, 16 * g2 + 16)
                    pv[rows, s, 0] = lam_re[dr, g]
                    pv[rows, s, 1] = lam_im[dr, g]
                    pv[rows, s, 2] = log_step[dr, g]
                    bex[rows, s, 0, cols] = b_re[dr, g]
                    bex[rows, s, 1, cols] = b_im[dr, g]
                    cex[rows, s, 0, cols] = c_re[dr, g].T
                    cex[rows, s, 1, cols] = c_im[dr, g].T
        for tau in range(8):
            dd[np.arange(32), tau, np.arange(32)] = dvec[f0 + 32 * tau:f0 + 32 * tau + 32]
        maps.append({"u": u, "pv": pv, "bex": bex, "cex": cex, "dd": dd, "ident": ident, "iota": iota})
    res = run_prog(P, maps)
    yv = [np.empty((NB, NV, D), np.float32) for _ in range(2)]
    for i, r in enumerate(res):
        f0 = 256 * i
        for dr in range(2):
            yv[dr][:, :, f0:f0 + 256] = r["y"][dr].transpose(2, 3, 1, 0).reshape(NB, NV, 256)
    yf_c, yf_l = yv[0][:, :NCTX], yv[0][:, NCTX:]
    yr_c, yr_l = yv[1][:, :NCTX][:, ::-1], yv[1][:, NCTX:][:, ::-1]
    return (np.ascontiguousarray(yf_l), np.ascontiguousarray(yf_c), np.ascontiguousarray(yr_l), np.ascontiguousarray(yr_c))


def host_D(l, x_lat, x_ctx, y_lat, y_ctx, m_all, norm_g, wa, wup, wdn, fconv_w, fconv_b, y2=None, b_glu=None):
    even = y2 is None
    P = build_D(even)
    g1, g2, g3 = col16(norm_g[l, 1]), col16(norm_g[l, 2]), col16(norm_g[l, 3])
    mc = mod_cols(m_all, l, 2)
    up_idx = []
    fcw = np.zeros((128, 86, 4), np.float32)
    for j in range(43):
        for k, c0 in enumerate((128 * j, DFF + 128 * j)):
            up_idx.append(c0 + np.arange(128))
            fcw[:, 2 * j + k, 0:3] = fconv_w[:, c0:c0 + 128].T
            fcw[:, 2 * j + k, 3] = fconv_b[c0:c0 + 128]
    wup_p = tile_w(wup[:, np.concatenate(up_idx)], KC, 256)
    if even:
        wa_p = tile_w(wa, KC, 256)
        ba = bg = np.zeros((128, KC), np.float32)
    else:
        a_idx = []
        for j in range(16):
            a_idx.append(128 * j + np.arange(128))
            a_idx.append(2048 + 128 * j + np.arange(128))
        wa_p = tile_w(wa[:, np.concatenate(a_idx)], KC, 256)
        ba, bg = col16(b_glu[:2048]), col16(b_glu[2048:])
    wdn_p = tile_w(wdn, 43, 128)
    maps = []
    for i in range(NCORE):
        b, q = divmod(i, 4)
        ml = mod_cols(m_all, l, b)
        vec = np.ascontiguousarray(np.stack([g1, g2, g3, ml[2], ml[3], ml[4], ml[5], mc[2], mc[3], mc[4], mc[5], ba, bg],
                                            axis=1))
        m = {"xw": windows(x_lat, x_ctx, i), "yw": windows(y_lat, y_ctx, i), "vec": vec, "fl": halo_flags(i),
             "fcw": fcw, "wa": wa_p, "wup": wup_p, "wdn": wdn_p}
        if not even:
            m["yw2"] = windows(y2[0], y2[1], i)
        maps.append(m)
    res = run_prog(P, maps)
    return unshard([r["ox"] for r in res])


def kernel(**inputs):
    inp = {k: np.asarray(v, dtype=np.float32) for k, v in inputs.items()}
    ng = inp["norm_g"]
    m_all = host_L0(inp["c"], inp["c_ctx"], inp["w_mod"], inp["b_mod"])
    xl, xc = inp["x"], inp["ctx"]
    for l in range(4):
        j = l // 2
        if l % 2 == 0:
            (qkv_l, qkv_c), (x0_l, x0_c), (vx_l, vx_c) = host_A(
                l, xl, xc, m_all, ng, (inp["ab_w_in"][j], inp["hy_conv_w"][j], inp["hy_conv_b"][j]))
            ya_l, ya_c = host_B(qkv_l, qkv_c, inp["attn_sink"][j])
            yb_l, yb_c = host_C(vx_l, x0_l, vx_c, x0_c,
                                (inp["hy_f_w1"][j], inp["hy_f_b1"][j], inp["hy_f_w2"][j], inp["hy_f_b2"][j],
                                 inp["hy_f_w3"][j], inp["hy_f_freq"][j], inp["hy_bias"][j]))
            yl = np.concatenate([ya_l, yb_l], axis=2)
            yc = np.concatenate([ya_c, yb_c], axis=2)
            xl, xc = host_D(l, xl, xc, yl, yc, m_all, ng, inp["ab_w_out"][j], inp["ffn_w_up"][l],
                            inp["ffn_w_down"][l], inp["ffn_conv_w"][l], inp["ffn_conv_b"][l])
        else:
            hl, hc = host_A(l, xl, xc, m_all, ng)
            yf_l, yf_c, yr_l, yr_c = host_S5(
                hl, hc, (inp["s5_lam_re"][j], inp["s5_lam_im"][j], inp["s5_log_step"][j], inp["s5_b_re"][j],
                         inp["s5_b_im"][j], inp["s5_c_re"][j], inp["s5_c_im"][j], inp["s5_d"][j]))
            xl, xc = host_D(l, xl, xc, yf_l, yf_c, m_all, ng, inp["s5_w_glu"][j], inp["ffn_w_up"][l],
                            inp["ffn_w_down"][l], inp["ffn_conv_w"][l], inp["ffn_conv_b"][l],
                            y2=(yr_l, yr_c), b_glu=inp["s5_b_glu"][j])
    return np.ascontiguousarray(xl.astype(np.float32))
```

```python
import math
import numpy as np
from contextlib import ExitStack
import concourse.bass as bass
import concourse.mybir as mybir
from concourse.bass_utils import run_bass_kernel_spmd

F32 = mybir.dt.float32
F32R = mybir.dt.float32r
I32 = mybir.dt.int32
AF = mybir.ActivationFunctionType
ALU = mybir.AluOpType
AX = mybir.AxisListType

DMA_R = 14
ENGS = ("sync", "scalar", "vector", "gpsimd", "tensor")

D = 2048
KC = 16
SEQ = 8192
NCTX = 256
NB = 2
NCORE = 8
TO = 256
WW = TO + 2
NTL = 8
NT = 9
TCORE = 2112
DFF = 5504
GRID_W = 64
EPS = 1e-6
TWO_PI = 2.0 * math.pi


class Prog:
    def __init__(self):
        self.nc = bass.Bass("TRN2", target_bir_lowering=False)
        self.es = ExitStack()
        self.ops = []
        self.outkeys = []
        self.n = 0
        self.rots = {}

    def din(self, name, shape, dt=F32):
        return self.nc.dram_tensor(name, list(shape), dt, kind="ExternalInput").ap()

    def dout(self, name, shape, dt=F32):
        return self.nc.dram_tensor(name, list(shape), dt, kind="ExternalOutput").ap()

    def dscr(self, name, shape, dt=F32):
        return self.nc.dram_tensor(name, list(shape), dt, kind="Internal")

    def sb(self, name, shape, dt=F32):
        return self.es.enter_context(self.nc.sbuf_tensor(name, list(shape), dt))

    def ps(self, name, shape, dt=F32):
        return self.es.enter_context(self.nc.psum_tensor(name, list(shape), dt))

    def rot(self, name, n, shape, dt=F32, psum=False):
        if name not in self.rots:
            mk = self.ps if psum else self.sb
            self.rots[name] = [[mk("%s_%d" % (name, i), shape, dt) for i in range(n)], 0]
        lst = self.rots[name]
        i = lst[1] % len(lst[0])
        lst[1] += 1
        return lst[0][i], (name, i)

    def op(self, eng, fn, r=(), w=(), dma=False):
        self.ops.append((eng, fn, tuple(r), tuple(w), dma))

    def dma(self, out, in_, r=(), w=(), q="sync", **kw):
        self.op(q, lambda e: e.dma_start(out=out, in_=in_, **kw), r, w, dma=True)

    def store(self, out, in_, r=(), q="sync", **kw):
        self.n += 1
        k = ("__out", self.n)
        self.outkeys.append(k)
        self.dma(out, in_, r=r, w=(k,), q=q, **kw)

    def mm(self, out, lhsT, rhs, start, stop, r=(), w=()):
        self.op("tensor", lambda e: e.matmul(out, lhsT, rhs, start=start, stop=stop), r, w)

    def tr(self, out, in_, ident, r=(), w=()):
        self.op("tensor", lambda e: e.transpose(out, in_, ident), r, w)

    def act(self, out, in_, func, r=(), w=(), **kw):
        self.op("scalar", lambda e: e.activation(out=out, in_=in_, func=func, **kw), r, w)

    def v(self, name, r=(), w=(), eng="vector", **kw):
        self.op(eng, lambda e: getattr(e, name)(**kw), r, w)

    def build(self):
        nc = self.nc
        ops = self.ops
        ops.append(("sync", None, tuple(self.outkeys), (), False))
        N = len(ops)
        last_w = {}
        rd_c = {}
        rd_d = {}
        deps = [None] * N
        for i, (eng, fn, r, w, dma) in enumerate(ops):
            d = set()
            for k in r:
                j = last_w.get(k)
                if j is not None:
                    d.add(j)
            for k in w:
                j = last_w.get(k)
                if j is not None:
                    d.add(j)
                for j in rd_c.get(k, {}).values():
                    d.add(j)
                for j in rd_d.get(k, ()):
                    d.add(j)
            for k in r:
                if dma:
                    rd_d.setdefault(k, []).append(i)
                else:
                    rd_c.setdefault(k, {})[eng] = i
            for k in w:
                last_w[k] = i
                rd_c[k] = {}
                rd_d[k] = []
            d.discard(i)
            deps[i] = d
        waited_c = {e: {} for e in ENGS}
        waited_d = {e: set() for e in ENGS}
        final = [None] * N
        signaled = set()
        for i, (eng, fn, r, w, dma) in enumerate(ops):
            best = {}
            dl = []
            for j in deps[i]:
                je, _, _, _, jd = ops[j]
                if jd:
                    if j not in waited_d[eng]:
                        dl.append(j)
                        waited_d[eng].add(j)
                else:
                    if je == eng and eng == "tensor" and not dma:
                        continue
                    if j > best.get(je, -1):
                        best[je] = j
            cl = []
            for je, j in best.items():
                if waited_c[eng].get(je, -1) >= j:
                    continue
                waited_c[eng][je] = j
                cl.append(j)
                signaled.add(j)
            final[i] = (cl, dl)
        sem = {e: self.es.enter_context(nc.semaphore("s_" + e)) for e in ENGS}
        dq = {}
        for q in ("sync", "scalar", "gpsimd"):
            dq[q] = [self.es.enter_context(nc.semaphore("d_%s_%d" % (q, t))) for t in range(DMA_R)]
        cnt = {e: 0 for e in ENGS}
        dcnt = {q: 0 for q in dq}
        sig = [None] * N
        pre = [None] * N
        for i, (eng, fn, r, w, dma) in enumerate(ops):
            if dma:
                n = dcnt[eng]
                dcnt[eng] += 1
                s = dq[eng][n % DMA_R]
                sig[i] = (s, 16 * (n // DMA_R + 1))
                if n >= DMA_R:
                    pre[i] = (s, 16 * (n // DMA_R))
            elif i in signaled:
                cnt[eng] += 1
                sig[i] = (sem[eng], cnt[eng])
        per = {e: [] for e in ENGS}
        for i, o in enumerate(ops):
            per[o[0]].append(i)
        self.stats = {e: len(per[e]) for e in per}

        def emit(e, name):
            for i in per[name]:
                eng, fn, r, w, dma = ops[i]
                cl, dl = final[i]
                if pre[i] is not None:
                    e.wait_ge(pre[i][0], pre[i][1])
                for j in cl + dl:
                    e.wait_ge(sig[j][0], sig[j][1])
                if fn is None:
                    continue
                ins = fn(e)
                if dma:
                    ins.then_inc(sig[i][0], 16)
                elif sig[i] is not None:
                    ins.then_inc(sig[i][0], 1)

        with nc.Block() as block:
            @block.sync
            def _(e):
                emit(e, "sync")

            @block.scalar
            def _(e):
                emit(e, "scalar")

            @block.vector
            def _(e):
                emit(e, "vector")

            @block.gpsimd
            def _(e):
                emit(e, "gpsimd")

            @block.tensor
            def _(e):
                emit(e, "tensor")
        self.es.close()
        return nc


_N_LAUNCH = [0]


def run_prog(P, in_maps):
    nc = P.build()
    res = run_bass_kernel_spmd(nc, in_maps, core_ids=list(range(NCORE)))
    _N_LAUNCH[0] += 1
    return res.results


class TS:
    def __init__(self, P, wb_elems, sq=None):
        self.P = P
        self.ones = P.sb("ones", [128, 128])
        P.v("memset", ap=self.ones[:], constant=1.0, w=["ones"])
        self.eps = P.sb("eps", [128, 1])
        P.v("memset", ap=self.eps[:], constant=EPS, w=["eps"])
        self.sq = sq if sq is not None else P.sb("sq", [128, KC, WW])
        self.ps_stat = P.ps("ps_stat", [128, 512])
        self.rt = P.sb("rt", [128, WW])
        self.rstd = P.sb("rstd", [128, WW])
        self.wbuf = [P.sb("wbuf%d" % i, [128, wb_elems]) for i in range(2)]
        self.wbr = [P.sb("wbr%d" % i, [128, wb_elems]) for i in range(2)]
        self.hr = P.sb("hr", [128, KC, WW])
        self.pp = [P.ps("pp%d" % i, [128, 512]) for i in range(4)]
        self.wcnt = 0
        self.pcnt = 0

    def rstd_of(self, src, skeys, c0, ncol, sqkeys=("sq",)):
        P = self.P
        sqk = list(sqkeys)
        P.act(self.sq[:, :, :ncol], src[:, :, c0:c0 + ncol], AF.Square, r=skeys, w=sqk)
        for kc in range(KC):
            P.mm(self.ps_stat[:, :ncol], self.ones[:], self.sq[:, kc, :ncol], kc == 0, kc == KC - 1,
                 r=["ones"] + sqk, w=["ps_stat"])
        P.act(self.rt[:, :ncol], self.ps_stat[:, :ncol], AF.Sqrt, bias=self.eps[:, 0:1], scale=1.0 / D,
              r=["ps_stat", "eps"], w=["rt"])
        P.v("reciprocal", out=self.rstd[:, :ncol], in_=self.rt[:, :ncol], r=["rt"], w=["rstd"])

    def gemm(self, w_t, kch, nchunks, sw, rhs, rkeys, ncol, evac):
        P = self.P
        per = sw // 128
        for si, s0 in enumerate(range(0, nchunks, per)):
            b = self.wcnt % 2
            self.wcnt += 1
            wbf = self.wbuf[b][:, 0:kch * sw]
            wrf = self.wbr[b][:, 0:kch * sw]
            P.dma(wbf, w_t[si], w=[("wb", b)])
            if self.wcnt % 2 == 0:
                P.act(wrf.bitcast(F32R), wbf, AF.Copy, r=[("wb", b)], w=[("wr", b)])
            else:
                P.v("tensor_copy", out=wrf.bitcast(F32R), in_=wbf, r=[("wb", b)], w=[("wr", b)], eng="gpsimd")
            wb = wrf.rearrange("p (kc n) -> p kc n", kc=kch)
            for o in range(per):
                oc = s0 + o
                pi = self.pcnt % 4
                self.pcnt += 1
                pp = self.pp[pi]
                for kc in range(kch):
                    P.mm(pp[:, :ncol], wb[:, kc, o * 128:(o + 1) * 128].bitcast(F32R), rhs(kc), kc == 0,
                         kc == kch - 1, r=[("wr", b)] + rkeys(kc), w=[("pp", pi)])
                evac(oc, pp, ("pp", pi))


def tile_w(w, kch, sw):
    ns = w.shape[1] // sw
    return np.ascontiguousarray(w.reshape(kch, 128, ns, sw).transpose(2, 1, 0, 3).reshape(ns, 128, kch * sw))


def tile_info(ti):
    if ti < NTL:
        return 0, TO, TO * ti
    return 1, 64, 2048


def build_L0():
    P = Prog()
    sT = P.din("sT", [128, KC, 3])
    w = P.din("w", [4, D, 1536])
    bias = P.din("bias", [3, 4, 1536])
    o = P.dout("o", [3, 4, 1536])
    s_sb = P.sb("s_sb", [128, KC, 3])
    b_sb = P.sb("b_sb", [3, 4, 1536])
    o_sb = P.sb("o_sb", [3, 4, 1536])
    wb = [P.sb("wb%d" % i, [128, KC, 512]) for i in range(2)]
    pp = [P.ps("pp%d" % i, [128, 512]) for i in range(2)]
    P.dma(s_sb[:], sT[:], w=["s"])
    P.dma(b_sb[:], bias[:], w=["b"])
    P.act(s_sb[:], s_sb[:], AF.Silu, r=["s"], w=["s"])
    it = 0
    for l in range(4):
        for n0 in range(0, 1536, 512):
            b = it % 2
            it += 1
            P.dma(wb[b][:], w[l, :, n0:n0 + 512].rearrange("(kc p) n -> p kc n", p=128), w=[("wb", b)])
            for kc in range(KC):
                P.mm(pp[b][:3, :], s_sb[:, kc, :], wb[b][:, kc, :], kc == 0, kc == KC - 1,
                     r=["s", ("wb", b)], w=[("pp", b)])
            P.v("tensor_tensor", out=o_sb[:, l, n0:n0 + 512], in0=pp[b][:3, :], in1=b_sb[:, l, n0:n0 + 512],
                op=ALU.add, r=[("pp", b), "b"], w=["o"])
    P.store(o[:], o_sb[:], r=["o"])
    return P


NA_EVEN = 46


def norm_mod(P, T, src, skey, dst, dkey, Wt, gs_col, sh_col, r32=False, sqkeys=("sq",)):
    T.rstd_of(src, [(skey, kc) for kc in range(KC)], 0, Wt, sqkeys)
    for kc in range(KC):
        if r32:
            tmp, tk = P.rot("nmtmp", 2, [128, WW])
            P.v("scalar_tensor_tensor", out=tmp[:, :Wt], in0=src[:, kc, :Wt], scalar=gs_col(kc),
                in1=T.rstd[:, :Wt], op0=ALU.mult, op1=ALU.mult, r=[(skey, kc), "rstd", "gs"], w=[tk])
            P.act(T.hr[:, kc, :Wt].bitcast(F32R), tmp[:, :Wt], AF.Identity, bias=sh_col(kc), scale=1.0,
                  r=[tk, "vec"], w=[("hr", kc)])
        else:
            P.v("scalar_tensor_tensor", out=dst[:, kc, :Wt], in0=src[:, kc, :Wt], scalar=gs_col(kc),
                in1=T.rstd[:, :Wt], op0=ALU.mult, op1=ALU.mult,
                r=[(skey, kc), "rstd", "gs"], w=[(dkey, kc)])
            P.act(dst[:, kc, :Wt], dst[:, kc, :Wt], AF.Identity, bias=sh_col(kc), scale=1.0,
                  r=[(dkey, kc), "vec"], w=[(dkey, kc)])


def conv3(P, pp, pkey, Wt, nt, fl_sb, ti, cw_sb, ci):
    acc, ak = P.rot("acc", 2, [128, TO])
    P.act(acc[:, :nt], pp[:, 1:nt + 1], AF.Identity, scale=cw_sb[:, ci, 1:2], bias=cw_sb[:, ci, 3:4],
          r=[pkey, "cw"], w=[ak])
    P.v("scalar_tensor_tensor", out=acc[:, :nt], in0=pp[:, 0:nt], scalar=cw_sb[:, ci, 0:1],
        in1=acc[:, :nt], op0=ALU.mult, op1=ALU.add, r=[pkey, "cw", ak], w=[ak])
    P.v("scalar_tensor_tensor", out=acc[:, :nt], in0=pp[:, 2:nt + 2], scalar=cw_sb[:, ci, 2:3],
        in1=acc[:, :nt], op0=ALU.mult, op1=ALU.add, r=[pkey, "cw", ak], w=[ak])
    return acc, ak


def zero_halo(P, T, fl_sb, ti, nt):
    hk = [("hr", kc) for kc in range(KC)]
    for col, f in ((0, 0), (nt + 1, 1)):
        P.v("tensor_scalar", out=T.hr[:, :, col:col + 1].bitcast(F32R), in0=T.hr[:, :, col:col + 1],
            scalar1=fl_sb[:, ti, f:f + 1], scalar2=None, op0=ALU.mult, r=hk + ["fl"], w=hk)


def build_A(even):
    P = Prog()
    T = TS(P, KC * 256)
    xw = P.din("xw", [NT, D, WW])
    vec = P.din("vec", [128, 5, KC])
    fl = P.din("fl", [128, NT, 2])
    vec_sb = P.sb("vec_sb", [128, 5, KC])
    fl_sb = P.sb("fl_sb", [128, NT, 2])
    gs = P.sb("gs", [128, 2, KC])
    P.dma(vec_sb[:], vec[:], w=["vec"])
    P.dma(fl_sb[:], fl[:], w=["fl"])
    for s in range(2):
        P.v("scalar_tensor_tensor", out=gs[:, s, :], in0=vec_sb[:, 2 + 2 * s, :], scalar=1.0,
            in1=vec_sb[:, 0, :], op0=ALU.add, op1=ALU.mult, r=["vec"], w=["gs"])
    xt = P.sb("xt", [128, KC, WW])
    ht = None if even else P.sb("ht", [128, KC, WW])
    if even:
        w = P.din("w", [NA_EVEN // 2, 128, KC * 256])
        cs = P.din("cs", [NT, 128, 2, WW])
        cw = P.din("cw", [128, 24, 4])
        oqkv = P.dout("oqkv", [12, 128, TCORE])
        ox0 = P.dout("ox0", [8, 128, TCORE])
        ovx = P.dout("ovx", [8, 128, TCORE])
        cw_sb = P.sb("cw_sb", [128, 24, 4])
        P.dma(cw_sb[:], cw[:], w=["cw"])
        cs_sb = P.sb("cs_sb", [128, 2, WW])
        hold = P.sb("hold", [128, WW])
        hold2 = P.sb("hold2", [128, TO])
    else:
        oh = P.dout("oh", [KC, 128, TCORE])
    for ti in range(NT):
        s, nt, c0 = tile_info(ti)
        Wt = nt + 2
        P.dma(xt[:, :, :], xw[ti].rearrange("(kc p) w -> p kc w", p=128), w=[("xt", kc) for kc in range(KC)])
        norm_mod(P, T, xt, "xt", ht, "ht", Wt,
                 lambda kc, s=s: gs[:, s, kc:kc + 1], lambda kc, s=s: vec_sb[:, 1 + 2 * s, kc:kc + 1], r32=even)
        if not even:
            P.store(oh[:, :, c0:c0 + nt].rearrange("kc p t -> p kc t"), ht[:, :, 1:nt + 1],
                    r=[("ht", kc) for kc in range(KC)])
            continue
        zero_halo(P, T, fl_sb, ti, nt)
        P.dma(cs_sb[:], cs[ti], w=["cs"])

        def evac(oc, pp, pkey, ti=ti, nt=nt, Wt=Wt, c0=c0):
            if oc < 20:
                if oc % 2 == 0:
                    P.v("tensor_tensor", out=hold[:, :Wt], in0=pp[:, :Wt], in1=cs_sb[:, 0, :Wt], op=ALU.mult,
                        r=[pkey, "cs"], w=["hold"])
                else:
                    osb, ok = P.rot("osb", 3, [128, WW])
                    P.v("tensor_tensor", out=osb[:, :Wt], in0=pp[:, :Wt], in1=cs_sb[:, 1, :Wt], op=ALU.mult,
                        r=[pkey, "cs"], w=[ok])
                    P.v("tensor_tensor", out=osb[:, :Wt], in0=osb[:, :Wt], in1=hold[:, :Wt], op=ALU.add,
                        r=[ok, "hold"], w=[ok], eng="gpsimd")
                    P.store(oqkv[oc // 2, :, c0:c0 + nt], osb[:, 1:nt + 1], r=[ok])
            elif oc < 22:
                osb, ok = P.rot("osb", 3, [128, WW])
                P.act(osb[:, :Wt], pp[:, :Wt], AF.Copy, r=[pkey], w=[ok])
                P.store(oqkv[10 + oc - 20, :, c0:c0 + nt], osb[:, 1:nt + 1], r=[ok])
            elif oc < 30:
                acc, ak = conv3(P, pp, pkey, Wt, nt, fl_sb, ti, cw_sb, oc - 22)
                P.store(ox0[oc - 22, :, c0:c0 + nt], acc[:, :nt], r=[ak])
            else:
                j = (oc - 30) // 2
                acc, ak = conv3(P, pp, pkey, Wt, nt, fl_sb, ti, cw_sb, oc - 22)
                if (oc - 30) % 2 == 0:
                    P.v("tensor_copy", out=hold2[:, :nt], in_=acc[:, :nt], r=[ak], w=["hold2"], eng="gpsimd")
                else:
                    P.v("tensor_tensor", out=acc[:, :nt], in0=acc[:, :nt], in1=hold2[:, :nt], op=ALU.mult,
                        r=[ak, "hold2"], w=[ak], eng="gpsimd")
                    P.store(ovx[j, :, c0:c0 + nt], acc[:, :nt], r=[ak])

        T.gemm(w, KC, NA_EVEN, 256, lambda kc, Wt=Wt: T.hr[:, kc, :Wt].bitcast(F32R), lambda kc: [("hr", kc)], Wt,
               evac)
    return P


def build_D(even):
    P = Prog()
    yt = P.sb("yt", [128, KC, WW])
    T = TS(P, 43 * 128, sq=yt)
    YK = [("yt", kc) for kc in range(KC)]
    xw = P.din("xw", [NT, D, WW])
    yw = P.din("yw", [NT, D, WW])
    if not even:
        yw2 = P.din("yw2", [NT, D, WW])
    vec = P.din("vec", [128, 13, KC])
    fl = P.din("fl", [128, NT, 2])
    fcw = P.din("fcw", [128, 86, 4])
    wa = P.din("wa", [8 if even else 16, 128, KC * 256])
    wup = P.din("wup", [43, 128, KC * 256])
    wdn = P.din("wdn", [16, 128, 43 * 128])
    ox = P.dout("ox", [KC, 128, TCORE])
    vec_sb = P.sb("vec_sb", [128, 13, KC])
    fl_sb = P.sb("fl_sb", [128, NT, 2])
    fcw_sb = P.sb("fcw_sb", [128, 86, 4])
    P.dma(vec_sb[:], vec[:], w=["vec"])
    P.dma(fl_sb[:], fl[:], w=["fl"])
    P.dma(fcw_sb[:], fcw[:], w=["cw"])
    mg1 = P.sb("mg1", [128, 2, KC])
    gs2 = P.sb("gs2", [128, 2, KC])
    mg3 = P.sb("mg3", [128, 2, KC])
    for s in range(2):
        P.v("tensor_tensor", out=mg1[:, s, :], in0=vec_sb[:, 3 + 4 * s, :], in1=vec_sb[:, 0, :], op=ALU.mult,
            r=["vec"], w=["gs"])
        P.v("scalar_tensor_tensor", out=gs2[:, s, :], in0=vec_sb[:, 5 + 4 * s, :], scalar=1.0,
            in1=vec_sb[:, 1, :], op0=ALU.add, op1=ALU.mult, r=["vec"], w=["gs"])
        P.v("tensor_tensor", out=mg3[:, s, :], in0=vec_sb[:, 6 + 4 * s, :], in1=vec_sb[:, 2, :], op=ALU.mult,
            r=["vec"], w=["gs"])
    xt = P.sb("xt", [128, KC, WW])
    tt = P.sb("tt", [128, KC, WW])
    a_sb = P.sb("a_sb", [128, 43, TO])
    hold = P.sb("hold", [128, WW])
    allk = lambda nm: [(nm, kc) for kc in range(KC)]
    for ti in range(NT):
        s, nt, c0 = tile_info(ti)
        Wt = nt + 2
        P.dma(xt[:, :, :], xw[ti].rearrange("(kc p) w -> p kc w", p=128), w=allk("xt"))
        P.dma(yt[:, :, :], yw[ti].rearrange("(kc p) w -> p kc w", p=128), w=allk("yt"))
        if even:
            for kc in range(KC):
                P.v("tensor_copy", out=T.hr[:, kc, :Wt].bitcast(F32R), in_=yt[:, kc, :Wt], r=[("yt", kc)],
                    w=[("hr", kc)], eng="gpsimd")
        if not even:
            P.dma(tt[:, :, :], yw2[ti].rearrange("(kc p) w -> p kc w", p=128), w=allk("tt"))
            P.v("tensor_tensor", out=yt[:, :, :Wt], in0=yt[:, :, :Wt], in1=tt[:, :, :Wt], op=ALU.add,
                r=allk("yt") + allk("tt"), w=allk("yt"), eng="gpsimd")
            gc = 2.0 * math.sqrt(2.0 / math.pi)
            for kc in range(KC):
                g1_, gk1 = P.rot("sg", 1, [128, WW])
                P.act(g1_[:, :Wt], yt[:, kc, :Wt], AF.Square, r=[("yt", kc)], w=[gk1])
                P.v("tensor_scalar", out=g1_[:, :Wt], in0=g1_[:, :Wt], scalar1=0.044715, scalar2=1.0,
                    op0=ALU.mult, op1=ALU.add, r=[gk1], w=[gk1])
                P.v("tensor_tensor", out=g1_[:, :Wt], in0=g1_[:, :Wt], in1=yt[:, kc, :Wt], op=ALU.mult,
                    r=[gk1, ("yt", kc)], w=[gk1])
                P.act(g1_[:, :Wt], g1_[:, :Wt], AF.Sigmoid, scale=gc, r=[gk1], w=[gk1])
                P.v("tensor_tensor", out=T.hr[:, kc, :Wt].bitcast(F32R), in0=yt[:, kc, :Wt], in1=g1_[:, :Wt],
                    op=ALU.mult, r=[gk1, ("yt", kc)], w=[("hr", kc)])

        def evac_a(oc, pp, pkey, Wt=Wt):
            if even:
                P.act(tt[:, oc, :Wt], pp[:, :Wt], AF.Copy, r=[pkey], w=[("tt", oc)])
            else:
                j = oc // 2
                if oc % 2 == 0:
                    P.act(hold[:, :Wt], pp[:, :Wt], AF.Identity, bias=vec_sb[:, 11, j:j + 1], scale=1.0,
                          r=[pkey, "vec"], w=["hold"])
                else:
                    sg, sk = P.rot("sg", 1, [128, WW])
                    P.act(sg[:, :Wt], pp[:, :Wt], AF.Sigmoid, bias=vec_sb[:, 12, j:j + 1], scale=1.0,
                          r=[pkey, "vec"], w=[sk])
                    P.v("tensor_tensor", out=tt[:, j, :Wt], in0=hold[:, :Wt], in1=sg[:, :Wt], op=ALU.mult,
                        r=["hold", sk], w=[("tt", j)])

        T.gemm(wa, KC, 16 if even else 32, 256, lambda kc, Wt=Wt: T.hr[:, kc, :Wt].bitcast(F32R),
               lambda kc: [("hr", kc)], Wt, evac_a)
        T.rstd_of(tt, allk("tt"), 0, Wt, YK)
        for kc in range(KC):
            P.v("scalar_tensor_tensor", out=tt[:, kc, :Wt], in0=tt[:, kc, :Wt], scalar=mg1[:, s, kc:kc + 1],
                in1=T.rstd[:, :Wt], op0=ALU.mult, op1=ALU.mult, r=[("tt", kc), "rstd", "gs"], w=[("tt", kc)])
            P.v("tensor_tensor", out=xt[:, kc, :Wt], in0=xt[:, kc, :Wt], in1=tt[:, kc, :Wt], op=ALU.add,
                r=[("xt", kc), ("tt", kc)], w=[("xt", kc)], eng="gpsimd")
        norm_mod(P, T, xt, "xt", None, None, Wt,
                 lambda kc, s=s: gs2[:, s, kc:kc + 1], lambda kc, s=s: vec_sb[:, 4 + 4 * s, kc:kc + 1], r32=True,
                 sqkeys=YK)
        zero_halo(P, T, fl_sb, ti, nt)

        def evac_up(oc, pp, pkey, ti=ti, nt=nt, Wt=Wt):
            j = oc // 2
            acc, ak = conv3(P, pp, pkey, Wt, nt, fl_sb, ti, fcw_sb, oc)
            if oc % 2 == 0:
                P.act(hold[:, :nt], acc[:, :nt], AF.Silu, r=[ak], w=["hold"])
            else:
                P.v("tensor_tensor", out=a_sb[:, j, :nt].bitcast(F32R), in0=acc[:, :nt], in1=hold[:, :nt],
                    op=ALU.mult, r=[ak, "hold"], w=[("a", j)], eng="gpsimd")

        T.gemm(wup, KC, 86, 256, lambda kc, Wt=Wt: T.hr[:, kc, :Wt].bitcast(F32R), lambda kc: [("hr", kc)], Wt,
               evac_up)

        def evac_dn(oc, pp, pkey, nt=nt):
            P.act(tt[:, oc, :nt], pp[:, :nt], AF.Copy, r=[pkey], w=[("tt", oc)])

        T.gemm(wdn, 43, KC, 128, lambda kc, nt=nt: a_sb[:, kc, :nt].bitcast(F32R), lambda kc: [("a", kc)], nt,
               evac_dn)
        T.rstd_of(tt, allk("tt"), 0, nt, YK)
        for kc in range(KC):
            P.v("scalar_tensor_tensor", out=tt[:, kc, :nt], in0=tt[:, kc, :nt], scalar=mg3[:, s, kc:kc + 1],
                in1=T.rstd[:, :nt], op0=ALU.mult, op1=ALU.mult, r=[("tt", kc), "rstd", "gs"], w=[("tt", kc)])
            P.v("tensor_tensor", out=tt[:, kc, :nt], in0=tt[:, kc, :nt], in1=xt[:, kc, 1:nt + 1], op=ALU.add,
                r=[("xt", kc), ("tt", kc)], w=[("tt", kc)], eng="gpsimd")
        P.store(ox[:, :, c0:c0 + nt].rearrange("kc p t -> p kc t"), tt[:, :, :nt], r=allk("tt"))
    return P


def col16(vv):
    return np.ascontiguousarray(np.asarray(vv, np.float32).reshape(-1, 128).T)


def windows(lat, ctx, i):
    b, q = divmod(i, 4)
    F = lat.shape[-1]
    out = np.zeros((NT, F, WW), np.float32)
    lp = np.pad(lat[b], ((1, 1), (0, 0)))
    for t in range(NTL):
        s0 = 2048 * q + TO * t
        out[t] = lp[s0:s0 + WW].T
    cp = np.pad(ctx[b], ((1, 1), (0, 0)))
    s0 = 64 * q
    out[NTL, :, :66] = cp[s0:s0 + 66].T
    return out


def halo_flags(i):
    b, q = divmod(i, 4)
    f = np.zeros((NT, 2), np.float32)
    for t in range(NTL):
        s0 = 2048 * q + TO * t
        f[t, 0] = 1.0 if s0 > 0 else 0.0
        f[t, 1] = 1.0 if s0 + TO < SEQ else 0.0
    f[NTL, 0] = 1.0 if q > 0 else 0.0
    f[NTL, 1] = 1.0 if q < 3 else 0.0
    return np.ascontiguousarray(np.broadcast_to(f[None], (128, NT, 2)))


def unshard(outs):
    nch = outs[0].shape[0]
    F = nch * 128
    lat = np.empty((NB, SEQ, F), np.float32)
    ctx = np.empty((NB, NCTX, F), np.float32)
    for i, o in enumerate(outs):
        b, q = divmod(i, 4)
        m = o.reshape(F, TCORE)
        lat[b, 2048 * q:2048 * (q + 1)] = m[:, :2048].T
        ctx[b, 64 * q:64 * (q + 1)] = m[:, 2048:].T
    return lat, ctx


def host_L0(c, c_ctx, w_mod, b_mod):
    s = np.stack([c[0], c[1], c_ctx]).astype(np.float32)
    sT = np.ascontiguousarray(s.T.reshape(KC, 128, 3).transpose(1, 0, 2))
    P = build_L0()
    maps = []
    for i in range(NCORE):
        cols = slice(1536 * i, 1536 * (i + 1))
        maps.append({"sT": sT, "w": np.ascontiguousarray(w_mod[:, :, cols]),
                     "bias": np.ascontiguousarray(np.broadcast_to(b_mod[None, :, cols], (3, 4, 1536)))})
    res = run_prog(P, maps)
    return np.concatenate([r["o"] for r in res], axis=2)


def mod_cols(m_all, l, b):
    return [col16(m_all[b, l, k * D:(k + 1) * D]) for k in range(6)]


def rope_tables():
    pos = np.arange(SEQ)
    row = (pos // GRID_W).astype(np.float32)
    col = (pos % GRID_W).astype(np.float32)
    inv = (np.float32(10000.0) ** (-np.arange(0, 64, 2, dtype=np.float32) / np.float32(64))).astype(np.float32)
    ar = row[:, None] * inv[None, :]
    ac = col[:, None] * inv[None, :]
    cr, sr, cc, sc = np.cos(ar), np.sin(ar), np.cos(ac), np.sin(ac)
    COS = np.concatenate([cr, cr, cc, cc], axis=1).T.astype(np.float32)
    SIN = np.concatenate([-sr, sr, -sc, sc], axis=1).T.astype(np.float32)
    return COS, SIN


def a_even_perm():
    idx = []
    sw = np.arange(128) ^ 32
    for h in range(8):
        idx.append(128 * h + np.arange(128))
        idx.append(128 * h + sw)
    for g in range(2):
        idx.append(1024 + 128 * g + np.arange(128))
        idx.append(1024 + 128 * g + sw)
    for g in range(2):
        idx.append(1280 + 128 * g + np.arange(128))
    zc = []
    for j in range(8):
        idx.append(1536 + 128 * j + np.arange(128))
        zc.append(128 * j)
    for j in range(8):
        idx.append(1536 + 1024 + 128 * j + np.arange(128))
        zc.append(1024 + 128 * j)
        idx.append(1536 + 2048 + 128 * j + np.arange(128))
        zc.append(2048 + 128 * j)
    return np.concatenate(idx), zc


def host_A(l, x_lat, x_ctx, m_all, norm_g, ev=None):
    even = ev is not None
    P = build_A(even)
    g0 = col16(norm_g[l, 0])
    mc = mod_cols(m_all, l, 2)
    if even:
        w_in, conv_w, conv_b = ev
        idx, zc = a_even_perm()
        wperm = tile_w(w_in[:, idx], KC, 256)
        cw = np.zeros((128, 24, 4), np.float32)
        for ci, z0 in enumerate(zc):
            cw[:, ci, 0:3] = conv_w[:, z0:z0 + 128].T
            cw[:, ci, 3] = conv_b[z0:z0 + 128]
        COS, SIN = rope_tables()
    maps = []
    for i in range(NCORE):
        b, q = divmod(i, 4)
        ml = mod_cols(m_all, l, b)
        vec = np.ascontiguousarray(np.stack([g0, ml[0], ml[1], mc[0], mc[1]], axis=1))
        m = {"xw": windows(x_lat, x_ctx, i), "vec": vec, "fl": halo_flags(i)}
        if even:
            cs = np.zeros((NT, 128, 2, WW), np.float32)
            cs[NTL, :, 0, :] = 1.0
            cp = np.pad(COS, ((0, 0), (1, 1)))
            sp = np.pad(SIN, ((0, 0), (1, 1)))
            for t in range(NTL):
                s0 = 2048 * q + TO * t
                cs[t, :, 0, :] = cp[:, s0:s0 + WW]
                cs[t, :, 1, :] = sp[:, s0:s0 + WW]
            m.update({"w": wperm, "cs": cs, "cw": cw})
        maps.append(m)
    res = run_prog(P, maps)
    if not even:
        return unshard([r["oh"] for r in res])
    return (unshard([r["oqkv"] for r in res]), unshard([r["ox0"] for r in res]), unshard([r["ovx"] for r in res]))


def build_B(nqb=16, nh=8, ctxu=True):
    P = Prog()
    qT = P.din("qT", [8, 128, 2048])
    qcT = P.din("qcT", [8, 128, 64])
    kT = P.din("kT", [2, 128, 2304])
    vB = P.din("vB", [128, 2, 18, 128])
    kcT = P.din("kcT", [2, 128, 256])
    vC = P.din("vC", [128, 2, 2, 128])
    sink = P.din("sink", [128, 8])
    mask = P.din("mask", [128, 16, 384])
    ident = P.din("ident", [128, 128])
    oA = P.dout("oA", [TCORE, 1024])
    q_sb = P.sb("q_sb", [128, 8, 2048])
    qc_sb = P.sb("qc_sb", [128, 8, 64])
    k_sb = P.sb("k_sb", [128, 2, 2304])
    v_sb = P.sb("v_sb", [128, 2, 18, 128])
    kc_sb = P.sb("kc_sb", [128, 2, 256])
    vc_sb = P.sb("vc_sb", [128, 2, 2, 128])
    sink_sb = P.sb("sink_sb", [128, 8])
    mask_sb = P.sb("mask_sb", [128, 16, 384])
    id_sb = P.sb("id_sb", [128, 128])
    for h in range(8):
        P.dma(q_sb[:, h, :], qT[h], w=[("q", h)])
    P.dma(qc_sb[:], qcT.rearrange("h d t -> d h t"), w=["qc"])
    P.dma(k_sb[:], kT.rearrange("g d t -> d g t"), w=["k"])
    P.dma(v_sb[:], vB[:], w=["v"])
    P.dma(kc_sb[:], kcT.rearrange("g d t -> d g t"), w=["kc"])
    P.dma(vc_sb[:], vC[:], w=["vc"])
    P.dma(sink_sb[:], sink[:], w=["sink"])
    P.dma(mask_sb[:], mask[:], w=["mask"])
    P.dma(id_sb[:], ident[:], w=["id"])
    scale = 128.0 ** -0.5
    cp = [0]

    def unit(qp, ncols_band, lhsT_q, qkeys, g, h, qb, row0):
        nk = ncols_band + 256
        nkb = nk // 128
        sm, sk = P.rot("sm", 2, [128, 640])
        ska, skb = (sk, "a"), (sk, "b")
        psB, kB = P.rot("psB", 2, [128, 512], psum=True)
        if ncols_band:
            psA, kA = P.rot("psA", 2, [128, 512], psum=True)
            P.mm(psA[:qp, :384], lhsT_q, k_sb[:, g, qb * 128:qb * 128 + 384], True, True,
                 r=qkeys + ["k"], w=[kA])
            P.v("scalar_tensor_tensor", out=sm[:qp, :384], in0=psA[:qp, :384], scalar=scale,
                in1=mask_sb[:qp, qb, :], op0=ALU.mult, op1=ALU.add, r=[kA, "mask"], w=[ska])
        P.mm(psB[:qp, :256], lhsT_q, kc_sb[:, g, :], True, True, r=qkeys + ["kc"], w=[kB])
        P.act(sm[:qp, ncols_band:nk], psB[:qp, :256], AF.Copy, scale=scale, r=[kB], w=[skb])
        mx, mk = P.rot("mx", 4, [128, 4])
        P.v("tensor_reduce", out=mx[:qp, 0:1], in_=sm[:qp, :nk], axis=AX.X, op=ALU.max, r=[ska, skb], w=[mk])
        P.v("tensor_tensor", out=mx[:qp, 0:1], in0=mx[:qp, 0:1], in1=sink_sb[:qp, h:h + 1], op=ALU.max,
            r=[mk, "sink"], w=[mk])
        P.v("tensor_scalar", out=mx[:qp, 1:2], in0=mx[:qp, 0:1], scalar1=-1.0, scalar2=None, op0=ALU.mult,
            r=[mk], w=[mk])
        P.act(sm[:qp, :nk], sm[:qp, :nk], AF.Exp, bias=mx[:qp, 1:2], scale=1.0, accum_out=mx[:qp, 2:3],
              r=[ska, skb, mk], w=[ska, skb, mk])
        P.act(mx[:qp, 3:4], sink_sb[:qp, h:h + 1], AF.Exp, bias=mx[:qp, 1:2], scale=1.0, r=["sink", mk], w=[mk])
        P.v("tensor_tensor", out=mx[:qp, 2:3], in0=mx[:qp, 2:3], in1=mx[:qp, 3:4], op=ALU.add, r=[mk], w=[mk])
        P.v("reciprocal", out=mx[:qp, 3:4], in_=mx[:qp, 2:3], r=[mk], w=[mk])
        eT, ek = P.rot("eT", 2, [128, 5, 128])
        for kb in range(nkb):
            pT, pk = P.rot("psT", 2, [128, 512], psum=True)
            P.tr(pT[:, :qp], sm[:qp, kb * 128:(kb + 1) * 128], id_sb[:qp, :qp], r=[ska, skb, "id"], w=[pk])
            cp[0] += 1
            if cp[0] % 2 == 0:
                P.act(eT[:, kb, :qp], pT[:, :qp], AF.Copy, r=[pk], w=[(ek, kb)])
            else:
                P.v("tensor_copy", out=eT[:, kb, :qp], in_=pT[:, :qp], r=[pk], w=[(ek, kb)])
        pO, ok = P.rot("psO", 2, [128, 512], psum=True)
        for kb in range(nkb):
            if ncols_band and kb < 3:
                vv = v_sb[:, g, qb + kb, :]
            else:
                vv = vc_sb[:, g, kb - (3 if ncols_band else 0), :]
            P.mm(pO[:qp, :128], eT[:, kb, :qp], vv, kb == 0, kb == nkb - 1, r=[(ek, kb), "v", "vc"], w=[ok])
        osb, osk = P.rot("osb", 3, [128, 128])
        P.v("tensor_scalar", out=osb[:qp, :], in0=pO[:qp, :128], scalar1=mx[:qp, 3:4], scalar2=None,
            op0=ALU.mult, r=[ok, mk], w=[osk])
        P.store(oA[row0:row0 + qp, h * 128:(h + 1) * 128], osb[:qp, :], r=[osk])

    for qb in range(nqb):
        for h in range(nh):
            unit(128, 384, q_sb[:, h, qb * 128:(qb + 1) * 128], [("q", h)], h // 4, h, qb, qb * 128)
    if ctxu:
        for h in range(nh):
            unit(64, 0, qc_sb[:, h, :], ["qc"], h // 4, h, 0, 2048)
    return P


def host_B(qkv_l, qkv_c, sinkv):
    P = build_B()
    maps = []
    ident = np.eye(128, dtype=np.float32)
    qi = np.arange(128)[:, None]
    kj = np.arange(384)[None, :]
    band = np.abs(kj - 128 - qi) <= 128
    for i in range(NCORE):
        b, q = divmod(i, 4)
        r0 = 2048 * q
        ql = qkv_l[b, r0:r0 + 2048, :1024]
        qT = np.ascontiguousarray(ql.reshape(2048, 8, 128).transpose(1, 2, 0))
        qc = qkv_c[b, 64 * q:64 * q + 64, :1024]
        qcT = np.ascontiguousarray(qc.reshape(64, 8, 128).transpose(1, 2, 0))
        kp = np.pad(qkv_l[b, :, 1024:1280], ((128, 128), (0, 0)))[r0:r0 + 2304]
        kT = np.ascontiguousarray(kp.reshape(2304, 2, 128).transpose(1, 2, 0))
        vp = np.pad(qkv_l[b, :, 1280:1536], ((128, 128), (0, 0)))[r0:r0 + 2304]
        vB = np.ascontiguousarray(vp.reshape(18, 128, 2, 128).transpose(1, 2, 0, 3))
        kcT = np.ascontiguousarray(qkv_c[b, :, 1024:1280].reshape(256, 2, 128).transpose(1, 2, 0))
        vC = np.ascontiguousarray(qkv_c[b, :, 1280:1536].reshape(2, 128, 2, 128).transpose(1, 2, 0, 3))
        mask = np.zeros((128, 16, 384), np.float32)
        for qb in range(16):
            kpos = r0 + 128 * (qb - 1) + kj
            valid = band & (kpos >= 0) & (kpos < SEQ)
            mask[:, qb, :] = np.where(valid, np.float32(0.0), np.float32(-1e30))
        maps.append({"qT": qT, "qcT": qcT, "kT": kT, "vB": vB, "kcT": kcT, "vC": vC,
                     "sink": np.ascontiguousarray(np.broadcast_to(sinkv[None, :], (128, 8))).astype(np.float32),
                     "mask": mask, "ident": ident})
    res = run_prog(P, maps)
    ya_l = np.empty((NB, SEQ, 1024), np.float32)
    ya_c = np.empty((NB, NCTX, 1024), np.float32)
    for i, r in enumerate(res):
        b, q = divmod(i, 4)
        ya_l[b, 2048 * q:2048 * q + 2048] = r["oA"][:2048]
        ya_c[b, 64 * q:64 * q + 64] = r["oA"][2048:]
    return ya_l, ya_c


def build_C():
    P = Prog()
    nc = P.nc
    vx_l = P.din("vx_l", [128, 128, 2, 64])
    x0_l = P.din("x0_l", [128, 128, 2, 64])
    vx_c = P.din("vx_c", [128, 128, 2, 2])
    x0_c = P.din("x0_c", [128, 128, 2, 2])
    zin = {(8192, 0): P.din("zf8", [33, 8192]), (8192, 1): P.din("zr8", [33, 8192]),
           (256, 0): P.din("zf2", [33, 256]), (256, 1): P.din("zr2", [33, 256])}
    din = {(8192, 0): P.din("df8", [128, 8192]), (8192, 1): P.din("dr8", [128, 8192]),
           (256, 0): P.din("df2", [128, 256]), (256, 1): P.din("dr2", [128, 256])}
    w1 = P.din("w1", [33, 64])
    w2 = P.din("w2", [64, 64])
    w3f = P.din("w3f", [64, 128])
    w3b = P.din("w3b", [64, 128])
    fb = P.din("fb", [64, 3])
    hbias = P.din("hbias", [128, 1])
    yb_l = P.dout("yb_l", [128, 128, 2, 64])
    yb_c = P.dout("yb_c", [128, 128, 2, 2])
    kl = {8192: P.dscr("kline8", [128, 16384]), 256: P.dscr("kline2", [128, 512])}
    w1_sb = P.sb("w1_sb", [33, 64])
    w2_sb = P.sb("w2_sb", [64, 64])
    w3_sb = [P.sb("w3f_sb", [64, 128]), P.sb("w3b_sb", [64, 128])]
    fb_sb = P.sb("fb_sb", [64, 3])
    hb_sb = P.sb("hbias_sb", [128, 1])
    sp = P.sb("sp", [64, 3])
    P.dma(w1_sb[:], w1[:], w=["w"])
    P.dma(w2_sb[:], w2[:], w=["w"])
    P.dma(w3_sb[0][:], w3f[:], w=["w"])
    P.dma(w3_sb[1][:], w3b[:], w=["w"])
    P.dma(fb_sb[:], fb[:], w=["fb"])
    P.dma(hb_sb[:], hbias[:], w=["hbias"])
    P.v("tensor_scalar", out=sp[:, 0:1], in0=fb_sb[:, 0:1], scalar1=1.0 / TWO_PI, scalar2=None, op0=ALU.mult,
        r=["fb"], w=["sp"])
    for k in (1, 2):
        P.v("tensor_scalar", out=sp[:, k:k + 1], in0=fb_sb[:, k:k + 1], scalar1=sp[:, 0:1], scalar2=64.0,
            op0=ALU.mult, op1=ALU.add, r=["fb", "sp"], w=["sp"])
    filt = [P.sb("filt_f", [128, 8192]), P.sb("filt_r", [128, 8192])]

    def sinpipe(ps, pkey, L, s2col):
        y, yk = P.rot("sy", 2, [64, 512])
        yi, ik = P.rot("syi", 2, [64, 512], dt=I32)
        f, fk = P.rot("sf", 2, [64, 512])
        h, hk = P.rot("sh", 3, [64, 512])
        P.v("tensor_scalar", out=y[:, :L], in0=ps[:64, :L], scalar1=sp[:, 0:1], scalar2=sp[:, s2col:s2col + 1],
            op0=ALU.mult, op1=ALU.add, r=[pkey, "sp"], w=[yk])
        P.v("tensor_copy", out=yi[:, :L], in_=y[:, :L], r=[yk], w=[ik])
        P.v("scalar_tensor_tensor", out=f[:, :L], in0=yi[:, :L], scalar=-1.0, in1=y[:, :L], op0=ALU.mult,
            op1=ALU.add, r=[ik, yk], w=[fk])
        P.v("scalar_tensor_tensor", out=y[:, :L], in0=f[:, :L], scalar=0.5, in1=f[:, :L], op0=ALU.is_gt,
            op1=ALU.subtract, r=[fk], w=[yk])
        P.act(h[:, :L], y[:, :L], AF.Sin, scale=-TWO_PI, r=[yk], w=[hk])
        return h, hk

    for n in (8192, 256):
        for d in (0, 1):
            for t0 in range(0, n, 512):
                L = min(512, n - t0)
                zs, zk = P.rot("zs", 2, [33, 512])
                P.dma(zs[:, :L], zin[(n, d)][:, t0:t0 + L], w=[zk])
                ps1, k1 = P.rot("fps", 3, [128, 512], psum=True)
                P.mm(ps1[:64, :L], w1_sb[:], zs[:, :L], True, True, r=["w", zk], w=[k1])
                h1, hk1 = sinpipe(ps1, k1, L, 1)
                ps2, k2 = P.rot("fps", 3, [128, 512], psum=True)
                P.mm(ps2[:64, :L], w2_sb[:], h1[:, :L], True, True, r=["w", hk1], w=[k2])
                h2, hk2 = sinpipe(ps2, k2, L, 2)
                ps3, k3 = P.rot("fps", 3, [128, 512], psum=True)
                P.mm(ps3[:, :L], w3_sb[d][:], h2[:, :L], True, True, r=["w", hk2], w=[k3])
                dt_, dk = P.rot("dect", 2, [128, 512])
                P.dma(dt_[:, :L], din[(n, d)][:, t0:t0 + L], w=[dk])
                P.v("tensor_tensor", out=filt[d][:, t0:t0 + L], in0=ps3[:, :L], in1=dt_[:, :L], op=ALU.mult,
                    r=[k3, dk], w=[("filt", d)])
        nrm = P.sb("nrm%d" % n, [128, 4])
        P.v("tensor_reduce", out=nrm[:, 0:1], in_=filt[0][:, :n], axis=AX.X, op=ALU.add, apply_absolute_value=True,
            r=[("filt", 0)], w=["nrm"])
        P.v("tensor_reduce", out=nrm[:, 1:2], in_=filt[1][:, :n - 1], axis=AX.X, op=ALU.add,
            apply_absolute_value=True, r=[("filt", 1)], w=["nrm"])
        P.v("tensor_tensor", out=nrm[:, 2:3], in0=nrm[:, 0:1], in1=nrm[:, 1:2], op=ALU.add, r=["nrm"], w=["nrm"])
        P.v("reciprocal", out=nrm[:, 3:4], in_=nrm[:, 2:3], r=["nrm"], w=["nrm"])
        for d in (0, 1):
            P.v("tensor_scalar", out=filt[d][:, :n], in0=filt[d][:, :n], scalar1=nrm[:, 3:4], scalar2=None,
                op0=ALU.mult, r=[("filt", d), "nrm"], w=[("filt", d)])
        P.v("tensor_tensor", out=filt[0][:, 0:1], in0=filt[0][:, 0:1], in1=hb_sb[:, 0:1], op=ALU.add,
            r=[("filt", 0), "hbias"], w=[("filt", 0)])
        kla = kl[n].ap()
        P.dma(kla[:, 0:n - 1], filt[1][:, 0:n - 1], r=[("filt", 1)], w=[("kl", n, 1)])
        P.dma(kla[:, n - 1:2 * n - 1], filt[0][:, 0:n], r=[("filt", 0)], w=[("kl", n, 0)])

    qn = [0]
    for n, nblk, vx, x0, yb in ((8192, 64, vx_l, x0_l, yb_l), (256, 2, vx_c, x0_c, yb_c)):
        pad = nblk - 1
        vps = [P.sb("vp%d_%d" % (n, i), [128, 2, nblk + 2 * pad]) for i in range(2)]
        zer = P.sb("zer%d" % n, [128, 2, nblk + 2 * pad])
        P.v("memset", ap=zer[:], constant=0.0, w=[("zer", n)], eng="gpsimd")
        for i in range(2):
            P.v("tensor_copy", out=vps[i][:].bitcast(F32R), in_=zer[:], r=[("zer", n)], w=[("vp", n, i)], eng="gpsimd")
        lags = list(range(-pad, pad + 1))
        for c in range(128):
            vp, vk = vps[c % 2], ("vp", n, c % 2)
            vst, vsk = P.rot("vst%d" % n, 2, [128, 2, nblk])
            P.dma(vst[:], vx[:, c], w=[vsk])
            P.v("tensor_copy", out=vp[:, :, pad:pad + nblk].bitcast(F32R), in_=vst[:], r=[vsk], w=[vk], eng="gpsimd")
            x0t, xk = P.rot("x0t%d" % n, 2, [128, 2, nblk])
            P.dma(x0t[:], x0[:, c], w=[xk])
            acc, ak = P.rot("hacc", 2, [128, 512], psum=True)
            accv = acc[:, 0:2 * nblk].rearrange("p (b k) -> p b k", b=2)
            for g0 in range(0, len(lags), 4):
                grp = lags[g0:g0 + 4]
                wd = 128 * len(grp)
                hst, hsk = P.rot("hst", 6, [128, 512])
                hb, hk = P.rot("hb", 4, [128, 512])
                src = bass.AP(kl[n], c * (2 * n) + (n - 1) + 128 * grp[0] - 127, [[1, 128], [1, wd]])
                P.dma(hst[:, :wd], src, r=[("kl", n, 0), ("kl", n, 1)], w=[hsk])
                qn[0] += 1
                if qn[0] % 2:
                    P.act(hb[:, :wd].bitcast(F32R), hst[:, :wd], AF.Copy, r=[hsk], w=[hk])
                else:
                    P.v("tensor_copy", out=hb[:, :wd].bitcast(F32R), in_=hst[:, :wd], r=[hsk], w=[hk])
                for r_, dl in enumerate(grp):
                    P.mm(accv, hb[:, 128 * r_:128 * r_ + 128].bitcast(F32R),
                         vp[:, :, pad - dl:pad - dl + nblk].bitcast(F32R), dl == -pad, dl == pad,
                         r=[hk, vk], w=[ak])
            ysb, yk = P.rot("ysb%d" % n, 2, [128, 2, nblk])
            P.v("tensor_tensor", out=ysb[:], in0=accv, in1=x0t[:], op=ALU.mult, r=[ak, xk], w=[yk])
            P.store(yb[:, c], ysb[:], r=[yk])
    return P


def hyena_tables(n):
    f32 = np.float32
    t = np.linspace(0.0, 1.0, n, dtype=f32)[:, None]
    w = (f32(2.0 * math.pi / n) * np.arange(n, dtype=f32)).astype(f32)
    bands = np.linspace(1e-4, 15, 16, dtype=f32)
    ang = (w[:, None] * bands[None, :]).astype(f32)
    z = np.concatenate([t, np.cos(ang), -np.sin(ang)], axis=-1).astype(f32)
    max_decay = math.log(1e-2) / 0.3
    min_decay = math.log(1e-2) / 1.5
    deltas = np.linspace(min_decay, max_decay, 1024, dtype=f32)
    decay = np.exp(-t * np.abs(deltas)[None, :]).astype(f32)
    return z, decay


def host_C(vx_l, x0_l, vx_c, x0_c, hp):
    f_w1, f_b1, f_w2, f_b2, f_w3, f_freq, hy_bias = hp
    P = build_C()
    z8, d8 = hyena_tables(SEQ)
    z2, d2 = hyena_tables(NCTX)
    fb = np.ascontiguousarray(np.stack([f_freq, f_b1, f_b2], axis=1)).astype(np.float32)
    maps = []

    def lay(a, nblk, rev):
        a = a.reshape(NB, nblk, 128, 128)
        if rev:
            a = a[:, :, ::-1, :]
        return np.ascontiguousarray(a.transpose(2, 3, 0, 1))

    for i in range(NCORE):
        cs = slice(128 * i, 128 * (i + 1))
        maps.append({
            "vx_l": lay(vx_l[:, :, cs], 64, True), "x0_l": lay(x0_l[:, :, cs], 64, False),
            "vx_c": lay(vx_c[:, :, cs], 2, True), "x0_c": lay(x0_c[:, :, cs], 2, False),
            "zf8": np.ascontiguousarray(z8.T), "zr8": np.ascontiguousarray(z8[::-1].T),
            "zf2": np.ascontiguousarray(z2.T), "zr2": np.ascontiguousarray(z2[::-1].T),
            "df8": np.ascontiguousarray(d8[:, cs].T), "dr8": np.ascontiguousarray(d8[::-1, cs].T),
            "df2": np.ascontiguousarray(d2[:, cs].T), "dr2": np.ascontiguousarray(d2[::-1, cs].T),
            "w1": np.ascontiguousarray(f_w1), "w2": np.ascontiguousarray(f_w2),
            "w3f": np.ascontiguousarray(f_w3[:, cs]), "w3b": np.ascontiguousarray(f_w3[:, 1024 + 128 * i:1024 + 128 * (i + 1)]),
            "fb": fb, "hbias": np.ascontiguousarray(hy_bias[cs].reshape(128, 1)),
        })
    res = run_prog(P, maps)
    yb_l = np.empty((NB, SEQ, 1024), np.float32)
    yb_c = np.empty((NB, NCTX, 1024), np.float32)
    for i, r in enumerate(res):
        cs = slice(128 * i, 128 * (i + 1))
        yb_l[:, :, cs] = r["yb_l"].transpose(2, 3, 0, 1).reshape(NB, SEQ, 128)
        yb_c[:, :, cs] = r["yb_c"].transpose(2, 3, 0, 1).reshape(NB, NCTX, 128)
    return yb_l, yb_c


NV = SEQ + NCTX
S5CH = 1024


def build_S5():
    P = Prog()
    u = P.din("u", [2, 32, 8, 2, NV])
    pv = P.din("pv", [128, 16, 3])
    bex = P.din("bex", [128, 16, 2, 32])
    cex = P.din("cex", [128, 16, 2, 32])
    dd = P.din("dd", [32, 8, 32])
    ident = P.din("ident", [128, 128])
    iota = P.din("iota", [128, S5CH])
    y = P.dout("y", [2, 32, 8, 2, NV])
    pv_sb = P.sb("pv_sb", [128, 16, 3])
    bex_sb = P.sb("bex_sb", [128, 16, 2, 32])
    cex_sb = P.sb("cex_sb", [128, 16, 2, 32])
    dd_sb = P.sb("dd_sb", [32, 8, 32])
    id_sb = P.sb("id_sb", [128, 128])
    io_sb = P.sb("io_sb", [128, S5CH])
    P.dma(pv_sb[:], pv[:], w=["pv"])
    P.dma(bex_sb[:], bex[:], w=["bex"])
    P.dma(cex_sb[:], cex[:], w=["cex"])
    P.dma(dd_sb[:], dd[:], w=["dd"])
    P.dma(id_sb[:], ident[:], w=["id"])
    P.dma(io_sb[:], iota[:], w=["iota"])
    halfpi = P.sb("halfpi", [128, 1])
    P.v("memset", ap=halfpi[:], constant=math.pi / 2, w=["halfpi"])
    ones = P.sb("ones512", [128, S5CH])
    P.v("memset", ap=ones[:], constant=1.0, w=["ones"])
    cn = [0]

    def col(name):
        cn[0] += 1
        return P.sb("c_%s_%d" % (name, cn[0]), [128, 16]), ("col", cn[0])

    def tt(out, ok, a, ak, b, bk, op, eng="vector"):
        P.v("tensor_tensor", out=out, in0=a, in1=b, op=op, r=[ak, bk], w=[ok], eng=eng)

    def wrap_turns(dst, dk, src, sk, shape, name):
        ti_, tik = P.rot("wi_" + name, 2, shape, dt=I32)
        f, fk = P.rot("wf_" + name, 2, shape)
        P.v("tensor_copy", out=ti_[:], in_=src, r=[sk], w=[tik])
        P.v("scalar_tensor_tensor", out=f[:], in0=ti_[:], scalar=-1.0, in1=src, op0=ALU.mult, op1=ALU.add,
            r=[tik, sk], w=[fk])
        P.v("scalar_tensor_tensor", out=dst, in0=f[:], scalar=0.5, in1=f[:], op0=ALU.is_gt, op1=ALU.subtract,
            r=[fk], w=[dk])
        P.v("scalar_tensor_tensor", out=dst, in0=dst, scalar=0.5, in1=dst, op0=ALU.is_gt, op1=ALU.subtract,
            r=[dk], w=[dk])

    def sincos(sin_t, sk_, cos_t, ck_, c, ck, shape, name):
        ab, abk = P.rot("ab_" + name, 2, shape)
        P.act(sin_t, c, AF.Sin, scale=TWO_PI, r=[ck], w=[sk_])
        P.v("scalar_tensor_tensor", out=ab[:], in0=c, scalar=-1.0, in1=c, op0=ALU.mult, op1=ALU.max,
            r=[ck], w=[abk])
        P.act(cos_t, ab[:], AF.Sin, scale=-TWO_PI, bias=halfpi[:, 0:1], r=[abk, "halfpi"], w=[ck_])

    lre = pv_sb[:, :, 0]
    lim = pv_sb[:, :, 1]
    dtc, dtk = col("dt")
    P.act(dtc[:], pv_sb[:, :, 2], AF.Exp, r=["pv"], w=[dtk])
    a_, a_k = col("a")
    tt(a_[:], a_k, lre, "pv", dtc[:], dtk, ALU.mult)
    th, thk = col("th")
    tt(th[:], thk, lim, "pv", dtc[:], dtk, ALU.mult)
    rr, rk = col("r")
    P.act(rr[:], a_[:], AF.Exp, r=[a_k], w=[rk])
    ph, phk = col("ph")
    P.v("tensor_scalar", out=ph[:], in0=th[:], scalar1=1.0 / TWO_PI, scalar2=None, op0=ALU.mult, r=[thk], w=[phk])
    phi, phik = col("phi")
    wrap_turns(phi[:], phik, ph[:], phk, [128, 16], "p")
    sn, snk = col("sn")
    cs_, csk = col("cs")
    sincos(sn[:], snk, cs_[:], csk, phi[:], phik, [128, 16], "p")
    nr, nrk = col("nr")
    tt(nr[:], nrk, rr[:], rk, cs_[:], csk, ALU.mult)
    P.v("tensor_scalar", out=nr[:], in0=nr[:], scalar1=-1.0, scalar2=None, op0=ALU.add, r=[nrk], w=[nrk])
    ni, nik = col("ni")
    tt(ni[:], nik, rr[:], rk, sn[:], snk, ALU.mult)
    den, denk = col("den")
    t0_, t0k = col("t0")
    tt(den[:], denk, lre, "pv", lre, "pv", ALU.mult)
    tt(t0_[:], t0k, lim, "pv", lim, "pv", ALU.mult)
    tt(den[:], denk, den[:], denk, t0_[:], t0k, ALU.add)
    inv, invk = col("inv")
    P.v("reciprocal", out=inv[:], in_=den[:], r=[denk], w=[invk])
    wre, wrek = col("wre")
    wim, wimk = col("wim")
    t1_, t1k = col("t1")
    tt(wre[:], wrek, nr[:], nrk, lre, "pv", ALU.mult)
    tt(t1_[:], t1k, ni[:], nik, lim, "pv", ALU.mult)
    tt(wre[:], wrek, wre[:], wrek, t1_[:], t1k, ALU.add)
    tt(wre[:], wrek, wre[:], wrek, inv[:], invk, ALU.mult)
    tt(wim[:], wimk, ni[:], nik, lre, "pv", ALU.mult)
    tt(t1_[:], t1k, nr[:], nrk, lim, "pv", ALU.mult)
    tt(wim[:], wimk, wim[:], wimk, t1_[:], t1k, ALU.subtract)
    tt(wim[:], wimk, wim[:], wimk, inv[:], invk, ALU.mult)
    nwim, nwimk = col("nwim")
    P.v("tensor_scalar", out=nwim[:], in0=wim[:], scalar1=-1.0, scalar2=None, op0=ALU.mult, r=[wimk], w=[nwimk])
    bbar = P.sb("bbar", [128, 16, 2, 32])
    BT = P.sb("BT", [32, 16, 2, 128])
    ncim = P.sb("ncim", [128, 16, 32])
    P.v("tensor_scalar", out=ncim[:], in0=cex_sb[:, :, 1, :], scalar1=-1.0, scalar2=None, op0=ALU.mult,
        r=["cex"], w=["ncim"])
    psT = P.ps("psT", [128, 512])
    for s in range(16):
        P.v("tensor_scalar", out=bbar[:, s, 0, :], in0=bex_sb[:, s, 0, :], scalar1=wre[:, s:s + 1], scalar2=None,
            op0=ALU.mult, r=["bex", wrek], w=[("bbar", s, 0)])
        P.v("scalar_tensor_tensor", out=bbar[:, s, 0, :], in0=bex_sb[:, s, 1, :], scalar=nwim[:, s:s + 1],
            in1=bbar[:, s, 0, :], op0=ALU.mult, op1=ALU.add, r=["bex", nwimk, ("bbar", s, 0)], w=[("bbar", s, 0)])
        P.v("tensor_scalar", out=bbar[:, s, 1, :], in0=bex_sb[:, s, 1, :], scalar1=wre[:, s:s + 1], scalar2=None,
            op0=ALU.mult, r=["bex", wrek], w=[("bbar", s, 1)])
        P.v("scalar_tensor_tensor", out=bbar[:, s, 1, :], in0=bex_sb[:, s, 0, :], scalar=wim[:, s:s + 1],
            in1=bbar[:, s, 1, :], op0=ALU.mult, op1=ALU.add, r=["bex", wimk, ("bbar", s, 1)], w=[("bbar", s, 1)])
        for ri in range(2):
            P.tr(psT[:32, :128], bbar[:, s, ri, :], id_sb[:], r=[("bbar", s, ri), "id"], w=["psT"])
            P.act(BT[:, s, ri, :], psT[:32, :128], AF.Copy, r=["psT"], w=[("BT", s)])

    chunks = [(t0, min(S5CH, NV - t0)) for t0 in range(0, NV, S5CH)]
    sh = [128, S5CH]
    for dr in range(2):
        for tau in range(8):
            s = dr * 8 + tau
            prev = {0: None, 1: None}
            rTc, rTk = P.rot("rTc", 2, sh)
            P.v("tensor_scalar", out=rTc[:], in0=ones[:], scalar1=rr[:, s:s + 1], scalar2=None, op0=ALU.mult,
                r=["ones", rk], w=[rTk], eng="gpsimd")
            for (t0, L) in chunks:
                ang, angk = P.rot("ang", 1, sh)
                P.v("tensor_scalar", out=ang[:, :L], in0=io_sb[:, :L], scalar1=float(t0), scalar2=phi[:, s:s + 1],
                    op0=ALU.add, op1=ALU.mult, r=["iota", phik], w=[angk])
                cc, cck = P.rot("cc", 2, sh)
                ti_, tik = P.rot("wi_m", 1, sh, dt=I32)
                f, fk = P.rot("wf_m", 1, sh)
                P.v("tensor_copy", out=ti_[:, :L], in_=ang[:, :L], r=[angk], w=[tik])
                P.v("scalar_tensor_tensor", out=f[:, :L], in0=ti_[:, :L], scalar=-1.0, in1=ang[:, :L], op0=ALU.mult,
                    op1=ALU.add, r=[tik, angk], w=[fk])
                P.v("scalar_tensor_tensor", out=cc[:, :L], in0=f[:, :L], scalar=0.5, in1=f[:, :L], op0=ALU.is_gt,
                    op1=ALU.subtract, r=[fk], w=[cck])
                P.v("scalar_tensor_tensor", out=cc[:, :L], in0=cc[:, :L], scalar=0.5, in1=cc[:, :L], op0=ALU.is_gt,
                    op1=ALU.subtract, r=[cck], w=[cck])
                sinT, sink_ = P.rot("sinT", 2, sh)
                cosT, cosk = P.rot("cosT", 2, sh)
                ab, abk = P.rot("ab_m", 1, sh)
                P.act(sinT[:, :L], cc[:, :L], AF.Sin, scale=TWO_PI, r=[cck], w=[sink_])
                P.v("scalar_tensor_tensor", out=ab[:, :L], in0=cc[:, :L], scalar=-1.0, in1=cc[:, :L], op0=ALU.mult,
                    op1=ALU.max, r=[cck], w=[abk])
                P.act(cosT[:, :L], ab[:, :L], AF.Sin, scale=-TWO_PI, bias=halfpi[:, 0:1], r=[abk, "halfpi"], w=[cosk])
                for b in range(2):
                    ut, uk = P.rot("ut", 2, [32, S5CH])
                    P.dma(ut[:, :L], u[dr, :, tau, b, t0:t0 + L], w=[uk])
                    pre_, prk = P.rot("pbre", 1, [128, S5CH], psum=True)
                    pim, pik = P.rot("pbim", 1, [128, S5CH], psum=True)
                    halves = [(h0, min(512, L - h0)) for h0 in range(0, L, 512)]
                    for (h0, hl_) in halves:
                        P.mm(pre_[:, h0:h0 + hl_], BT[:, s, 0, :], ut[:, h0:h0 + hl_], True, True,
                             r=[("BT", s), uk], w=[prk])
                        P.mm(pim[:, h0:h0 + hl_], BT[:, s, 1, :], ut[:, h0:h0 + hl_], True, True,
                             r=[("BT", s), uk], w=[pik])
                    t1, k1 = P.rot("m1", 2, sh)
                    t2, k2 = P.rot("m2", 2, sh)
                    t3, k3 = P.rot("m3", 2, sh)
                    t4, k4 = P.rot("m4", 2, sh)
                    tt(t1[:, :L], k1, pre_[:, :L], prk, cosT[:, :L], cosk, ALU.mult)
                    tt(t2[:, :L], k2, pim[:, :L], pik, sinT[:, :L], sink_, ALU.mult)
                    tt(t3[:, :L], k3, pim[:, :L], pik, cosT[:, :L], cosk, ALU.mult)
                    tt(t4[:, :L], k4, pre_[:, :L], prk, sinT[:, :L], sink_, ALU.mult)
                    tt(t1[:, :L], k1, t1[:, :L], k1, t2[:, :L], k2, ALU.add, eng="gpsimd")
                    tt(t3[:, :L], k3, t3[:, :L], k3, t4[:, :L], k4, ALU.subtract, eng="gpsimd")
                    sre, srk = P.rot("sre%d" % b, 2, sh)
                    sim, sik = P.rot("sim%d" % b, 2, sh)
                    if prev[b] is None:
                        ire, iim, ikeys = 0.0, 0.0, []
                    else:
                        (pre_t, pre_k, pim_t, pim_k, pL) = prev[b]
                        ire, iim, ikeys = pre_t[:, pL - 1:pL], pim_t[:, pL - 1:pL], [pre_k, pim_k]
                    P.v("tensor_tensor_scan", out=sre[:, :L], data0=rTc[:, :L], data1=t1[:, :L], initial=ire,
                        op0=ALU.mult, op1=ALU.add, r=[rTk, k1] + ikeys, w=[srk])
                    P.v("tensor_tensor_scan", out=sim[:, :L], data0=rTc[:, :L], data1=t3[:, :L], initial=iim,
                        op0=ALU.mult, op1=ALU.add, r=[rTk, k3] + ikeys, w=[sik])
                    prev[b] = (sre, srk, sim, sik, L)
                    d1, dk1 = P.rot("d1", 2, sh)
                    d2, dk2 = P.rot("d2", 2, sh)
                    d3, dk3 = P.rot("d3", 2, sh)
                    d4, dk4 = P.rot("d4", 2, sh)
                    tt(d1[:, :L], dk1, sre[:, :L], srk, cosT[:, :L], cosk, ALU.mult)
                    tt(d2[:, :L], dk2, sim[:, :L], sik, sinT[:, :L], sink_, ALU.mult, eng="gpsimd")
                    tt(d3[:, :L], dk3, sim[:, :L], sik, cosT[:, :L], cosk, ALU.mult)
                    tt(d4[:, :L], dk4, sre[:, :L], srk, sinT[:, :L], sink_, ALU.mult, eng="gpsimd")
                    tt(d1[:, :L], dk1, d1[:, :L], dk1, d2[:, :L], dk2, ALU.subtract, eng="gpsimd")
                    tt(d3[:, :L], dk3, d3[:, :L], dk3, d4[:, :L], dk4, ALU.add, eng="gpsimd")
                    py, pyk = P.rot("psy", 1, [128, S5CH], psum=True)
                    for (h0, hl_) in halves:
                        P.mm(py[:32, h0:h0 + hl_], cex_sb[:, s, 0, :], d1[:, h0:h0 + hl_], True, False,
                             r=["cex", dk1], w=[pyk])
                        P.mm(py[:32, h0:h0 + hl_], ncim[:, s, :], d3[:, h0:h0 + hl_], False, dr == 1,
                             r=["ncim", dk3], w=[pyk])
                        if dr == 0:
                            P.mm(py[:32, h0:h0 + hl_], dd_sb[:, tau, :], ut[:, h0:h0 + hl_], False, True,
                                 r=["dd", uk], w=[pyk])
                    ysb, yk = P.rot("ysb", 2, [32, S5CH])
                    P.act(ysb[:, :L], py[:32, :L], AF.Copy, r=[pyk], w=[yk])
                    P.store(y[dr, :, tau, b, t0:t0 + L], ysb[:, :L], r=[yk])
    return P


def host_S5(hl, hc, sp):
    lam_re, lam_im, log_step, b_re, b_im, c_re, c_im, dvec = sp
    P = build_S5()
    uv = [np.concatenate([hc, hl], axis=1), np.concatenate([hc[:, ::-1], hl[:, ::-1]], axis=1)]
    ident = np.eye(128, dtype=np.float32)
    iota = np.ascontiguousarray(np.broadcast_to(np.arange(S5CH, dtype=np.float32)[None], (128, S5CH)))
    maps = []
    for i in range(NCORE):
        f0 = 256 * i
        u = np.empty((2, 32, 8, 2, NV), np.float32)
        for dr in range(2):
            u[dr] = uv[dr][:, :, f0:f0 + 256].reshape(NB, NV, 8, 32).transpose(3, 2, 0, 1)
        pv = np.zeros((128, 16, 3), np.float32)
        bex = np.zeros((128, 16, 2, 32), np.float32)
        cex = np.zeros((128, 16, 2, 32), np.float32)
        dd = np.zeros((32, 8, 32), np.float32)
        for dr in range(2):
            for tau in range(8):
                s = dr * 8 + tau
                for g2 in range(2):
                    g = 16 * i + 2 * tau + g2
                    rows = slice(64 * g2, 64 * g2 + 64)
                    cols = slice(16 * g2, 16 * g2 + 16)
                    pv[rows, s, 0] = lam_re[dr, g]
                    pv[rows, s, 1] = lam_im[dr, g]
                    pv[rows, s, 2] = log_step[dr, g]
                    bex[rows, s, 0, cols] = b_re[dr, g]
                    bex[rows, s, 1, cols] = b_im[dr, g]
                    cex[rows, s, 0, cols] = c_re[dr, g].T
                    cex[rows, s, 1, cols] = c_im[dr, g].T
        for tau in range(8):
            dd[np.arange(32), tau, np.arange(32)] = dvec[f0 + 32 * tau:f0 + 32 * tau + 32]
        maps.append({"u": u, "pv": pv, "bex": bex, "cex": cex, "dd": dd, "ident": ident, "iota": iota})
    res = run_prog(P, maps)
    yv = [np.empty((NB, NV, D), np.float32) for _ in range(2)]
    for i, r in enumerate(res):
        f0 = 256 * i
        for dr in range(2):
            yv[dr][:, :, f0:f0 + 256] = r["y"][dr].transpose(2, 3, 1, 0).reshape(NB, NV, 256)
    yf_c, yf_l = yv[0][:, :NCTX], yv[0][:, NCTX:]
    yr_c, yr_l = yv[1][:, :NCTX][:, ::-1], yv[1][:, NCTX:][:, ::-1]
    return (np.ascontiguousarray(yf_l), np.ascontiguousarray(yf_c), np.ascontiguousarray(yr_l), np.ascontiguousarray(yr_c))


def host_D(l, x_lat, x_ctx, y_lat, y_ctx, m_all, norm_g, wa, wup, wdn, fconv_w, fconv_b, y2=None, b_glu=None):
    even = y2 is None
    P = build_D(even)
    g1, g2, g3 = col16(norm_g[l, 1]), col16(norm_g[l, 2]), col16(norm_g[l, 3])
    mc = mod_cols(m_all, l, 2)
    up_idx = []
    fcw = np.zeros((128, 86, 4), np.float32)
    for j in range(43):
        for k, c0 in enumerate((128 * j, DFF + 128 * j)):
            up_idx.append(c0 + np.arange(128))
            fcw[:, 2 * j + k, 0:3] = fconv_w[:, c0:c0 + 128].T
            fcw[:, 2 * j + k, 3] = fconv_b[c0:c0 + 128]
    wup_p = tile_w(wup[:, np.concatenate(up_idx)], KC, 256)
    if even:
        wa_p = tile_w(wa, KC, 256)
        ba = bg = np.zeros((128, KC), np.float32)
    else:
        a_idx = []
        for j in range(16):
            a_idx.append(128 * j + np.arange(128))
            a_idx.append(2048 + 128 * j + np.arange(128))
        wa_p = tile_w(wa[:, np.concatenate(a_idx)], KC, 256)
        ba, bg = col16(b_glu[:2048]), col16(b_glu[2048:])
    wdn_p = tile_w(wdn, 43, 128)
    maps = []
    for i in range(NCORE):
        b, q = divmod(i, 4)
        ml = mod_cols(m_all, l, b)
        vec = np.ascontiguousarray(np.stack([g1, g2, g3, ml[2], ml[3], ml[4], ml[5], mc[2], mc[3], mc[4], mc[5], ba, bg],
                                            axis=1))
        m = {"xw": windows(x_lat, x_ctx, i), "yw": windows(y_lat, y_ctx, i), "vec": vec, "fl": halo_flags(i),
             "fcw": fcw, "wa": wa_p, "wup": wup_p, "wdn": wdn_p}
        if not even:
            m["yw2"] = windows(y2[0], y2[1], i)
        maps.append(m)
    res = run_prog(P, maps)
    return unshard([r["ox"] for r in res])


def kernel(**inputs):
    inp = {k: np.asarray(v, dtype=np.float32) for k, v in inputs.items()}
    ng = inp["norm_g"]
    m_all = host_L0(inp["c"], inp["c_ctx"], inp["w_mod"], inp["b_mod"])
    xl, xc = inp["x"], inp["ctx"]
    for l in range(4):
        j = l // 2
        if l % 2 == 0:
            (qkv_l, qkv_c), (x0_l, x0_c), (vx_l, vx_c) = host_A(
                l, xl, xc, m_all, ng, (inp["ab_w_in"][j], inp["hy_conv_w"][j], inp["hy_conv_b"][j]))
            ya_l, ya_c = host_B(qkv_l, qkv_c, inp["attn_sink"][j])
            yb_l, yb_c = host_C(vx_l, x0_l, vx_c, x0_c,
                                (inp["hy_f_w1"][j], inp["hy_f_b1"][j], inp["hy_f_w2"][j], inp["hy_f_b2"][j],
                                 inp["hy_f_w3"][j], inp["hy_f_freq"][j], inp["hy_bias"][j]))
            yl = np.concatenate([ya_l, yb_l], axis=2)
            yc = np.concatenate([ya_c, yb_c], axis=2)
            xl, xc = host_D(l, xl, xc, yl, yc, m_all, ng, inp["ab_w_out"][j], inp["ffn_w_up"][l],
                            inp["ffn_w_down"][l], inp["ffn_conv_w"][l], inp["ffn_conv_b"][l])
        else:
            hl, hc = host_A(l, xl, xc, m_all, ng)
            yf_l, yf_c, yr_l, yr_c = host_S5(
                hl, hc, (inp["s5_lam_re"][j], inp["s5_lam_im"][j], inp["s5_log_step"][j], inp["s5_b_re"][j],
                         inp["s5_b_im"][j], inp["s5_c_re"][j], inp["s5_c_im"][j], inp["s5_d"][j]))
            xl, xc = host_D(l, xl, xc, yf_l, yf_c, m_all, ng, inp["s5_w_glu"][j], inp["ffn_w_up"][l],
                            inp["ffn_w_down"][l], inp["ffn_conv_w"][l], inp["ffn_conv_b"][l],
                            y2=(yr_l, yr_c), b_glu=inp["s5_b_glu"][j])
    return np.ascontiguousarray(xl.astype(np.float32))
```
